# Optimizing a Trainium2 kernel written in Bass

```python
import jax, jax.numpy as jnp
from jax import lax
import numpy as np

D_MODEL = 2048
BATCH = 8
SEQ = 4096
DEPTH = 2

N_META = 16
D_LRU = D_MODEL // 2
LRU_HEADS = 8
LRU_HEAD_DIM = D_LRU // LRU_HEADS
CONV_WIDTH = 4
LRU_C = 8.0
D_POOL = D_MODEL // 2
POOL_WINDOWS = (2, 4, 8, 16)
POOL_GROUPS = len(POOL_WINDOWS)
POOL_GROUP_DIM = D_POOL // POOL_GROUPS
D_MIX = D_LRU + D_POOL
D_IN = 2 * D_LRU + D_POOL
D_FF = ((8 * D_MODEL // 3 + 255) // 256) * 256
RMS_EPS = 1e-6

kernel_name = "hymba_rglru_pool_macaron"


def rms_norm(x, g):
    xf = x.astype(jnp.float32)
    y = xf * lax.rsqrt(jnp.mean(xf * xf, axis=-1, keepdims=True) + RMS_EPS)
    return (y * g.astype(jnp.float32)).astype(x.dtype)


def swiglu(h, w_in, w_out):
    g, u = jnp.split(h @ w_in, 2, axis=-1)
    return (jax.nn.silu(g) * u) @ w_out


def causal_dwconv(x, w, b):
    T = x.shape[1]
    xp = jnp.pad(x, ((0, 0), (CONV_WIDTH - 1, 0), (0, 0)))
    y = b
    for k in range(CONV_WIDTH):
        y = y + xp[:, k:k + T] * w[k]
    return y


def rg_lru(x, wa, ba, wx, bx, a_param):
    B, T, _ = x.shape
    xh = x.reshape(B, T, LRU_HEADS, LRU_HEAD_DIM)
    r = jax.nn.sigmoid((jnp.einsum('bthi,hij->bthj', xh, wa).reshape(B, T, D_LRU) + ba).astype(jnp.float32))
    i = jax.nn.sigmoid((jnp.einsum('bthi,hij->bthj', xh, wx).reshape(B, T, D_LRU) + bx).astype(jnp.float32))
    log_a = -LRU_C * r * jax.nn.softplus(-a_param.astype(jnp.float32))
    a = jnp.exp(log_a)
    mult = jnp.sqrt(-jnp.expm1(2.0 * log_a))
    bt = mult * i * x.astype(jnp.float32)

    def combine(left, right):
        a1, b1 = left
        a2, b2 = right
        return a1 * a2, a2 * b1 + b2

    _, h = lax.associative_scan(combine, (a, bt), axis=1)
    return h.astype(x.dtype)


def multiscale_pool(u, w, b, scale):
    B, T, _ = u.shape
    ug = u.reshape(B, T, POOL_GROUPS, POOL_GROUP_DIM).astype(jnp.float32)
    cs = jnp.cumsum(ug, axis=1)
    t1 = jnp.arange(1, T + 1, dtype=jnp.float32)
    outs = []
    for g, win in enumerate(POOL_WINDOWS):
        c = cs[:, :, g]
        lag = jnp.pad(c[:, :T - win], ((0, 0), (win, 0), (0, 0)))
        cnt = jnp.minimum(t1, float(win))[None, :, None]
        outs.append((c - lag) / cnt - ug[:, :, g])
    d = jnp.stack(outs, axis=2).astype(u.dtype)
    y = jnp.einsum('btgi,gij->btgj', d, w).reshape(B, T, D_POOL) + b
    return y * scale


def setup_inputs(seed: int = 0) -> dict:
    key = jax.random.key(seed)
    ks = iter(jax.random.split(key, 32))
    f32 = jnp.float32

    def nrm(shape, fan_in):
        return jax.random.normal(next(ks), shape, f32) * (fan_in ** -0.5)

    def gain(shape):
        return 1.0 + 0.02 * jax.random.normal(next(ks), shape, f32)

    def bias(shape):
        return 0.01 * jax.random.normal(next(ks), shape, f32)

    L = DEPTH
    x = jax.random.normal(next(ks), (BATCH, SEQ, D_MODEL), f32)
    meta_tokens = jax.random.normal(next(ks), (N_META, D_MODEL), f32)
    ffn1_norm = gain((L, D_MODEL))
    ffn1_w_in = nrm((L, D_MODEL, 2 * D_FF), D_MODEL)
    ffn1_w_out = nrm((L, D_FF, D_MODEL), D_FF)
    mix_norm = gain((L, D_MODEL))
    w_in = nrm((L, D_MODEL, D_IN), D_MODEL)
    conv_w = nrm((L, CONV_WIDTH, D_LRU), CONV_WIDTH)
    conv_b = bias((L, D_LRU))
    lru_wa = nrm((L, LRU_HEADS, LRU_HEAD_DIM, LRU_HEAD_DIM), LRU_HEAD_DIM)
    lru_ba = bias((L, D_LRU))
    lru_wx = nrm((L, LRU_HEADS, LRU_HEAD_DIM, LRU_HEAD_DIM), LRU_HEAD_DIM)
    lru_bx = bias((L, D_LRU))
    a8 = jax.random.uniform(next(ks), (L, D_LRU), f32, 0.9, 0.999)
    a0 = a8 ** (1.0 / LRU_C)
    lru_a_param = jnp.log(a0) - jnp.log1p(-a0)
    pool_w = nrm((L, POOL_GROUPS, POOL_GROUP_DIM, POOL_GROUP_DIM), POOL_GROUP_DIM)
    pool_b = bias((L, D_POOL))
    pool_scale = 1.0 + 0.1 * jax.random.normal(next(ks), (L, D_POOL), f32)
    w_out = nrm((L, D_MIX, D_MODEL), D_MIX)
    ffn2_norm = gain((L, D_MODEL))
    ffn2_w_in = nrm((L, D_MODEL, 2 * D_FF), D_MODEL)
    ffn2_w_out = nrm((L, D_FF, D_MODEL), D_FF)
    final_norm = gain((D_MODEL,))
    return {"x": x, "meta_tokens": meta_tokens,
            "ffn1_norm": ffn1_norm, "ffn1_w_in": ffn1_w_in, "ffn1_w_out": ffn1_w_out,
            "mix_norm": mix_norm, "w_in": w_in, "conv_w": conv_w, "conv_b": conv_b,
            "lru_wa": lru_wa, "lru_ba": lru_ba, "lru_wx": lru_wx, "lru_bx": lru_bx,
            "lru_a_param": lru_a_param, "pool_w": pool_w, "pool_b": pool_b,
            "pool_scale": pool_scale, "w_out": w_out,
            "ffn2_norm": ffn2_norm, "ffn2_w_in": ffn2_w_in, "ffn2_w_out": ffn2_w_out,
            "final_norm": final_norm}


def reference(x, meta_tokens, ffn1_norm, ffn1_w_in, ffn1_w_out, mix_norm, w_in, conv_w, conv_b,
              lru_wa, lru_ba, lru_wx, lru_bx, lru_a_param, pool_w, pool_b, pool_scale, w_out,
              ffn2_norm, ffn2_w_in, ffn2_w_out, final_norm):
    B = x.shape[0]
    meta = jnp.broadcast_to(meta_tokens.astype(x.dtype)[None], (B, N_META, D_MODEL))
    h = jnp.concatenate([meta, x], axis=1)
    for l in range(DEPTH):
        h = h + 0.5 * swiglu(rms_norm(h, ffn1_norm[l]), ffn1_w_in[l], ffn1_w_out[l])
        z = rms_norm(h, mix_norm[l]) @ w_in[l]
        zx, zg, zp = jnp.split(z, [D_LRU, 2 * D_LRU], axis=-1)
        ya = rg_lru(causal_dwconv(zx, conv_w[l], conv_b[l]),
                    lru_wa[l], lru_ba[l], lru_wx[l], lru_bx[l], lru_a_param[l]) * jax.nn.gelu(zg)
        yb = multiscale_pool(zp, pool_w[l], pool_b[l], pool_scale[l])
        h = h + jnp.concatenate([ya, yb], axis=-1) @ w_out[l]
        h = h + 0.5 * swiglu(rms_norm(h, ffn2_norm[l]), ffn2_w_in[l], ffn2_w_out[l])
    return rms_norm(h, final_norm)[:, N_META:]
```

```python
import numpy as np
from contextlib import ExitStack

import concourse.bass as bass
import concourse.mybir as mybir
from concourse.bass_utils import run_bass_kernel_spmd

F32 = mybir.dt.float32
BF16 = mybir.dt.bfloat16
AF = mybir.ActivationFunctionType
ALU = mybir.AluOpType

D = 2048
KC = 16
SEQ = 4096
NMETA = 16
TTOT = SEQ + NMETA
DFF = 5632
FC = DFF // 128
Q = 4
FQ = FC // Q
NST = 4
TT = TTOT // NST
SUBS = [(0, 343), (343, 343), (686, 342)]
HIST = 16
NSLOT = 6
EPS = 1e-6
NHEAD = 8
POOL_WIN = (2, 4, 8, 16)
NLAYER = 2
UW = 360

CV_L = 128
NCV = CV_L * NLAYER + KC
DV_L = 40


def phases_all():
    ph = []
    for l in range(NLAYER):
        ph += [(l, 'ffn1'), (l, 'mix'), (l, 'ffn2')]
    return ph


def phase_blocks(kind):
    bl = []
    if kind in ('ffn1', 'ffn2'):
        for q in range(Q):
            for fi in range(FQ):
                f = q * FQ + fi
                bl.append(('g', f, 0, KC * 128))
                bl.append(('u', f, 0, KC * 128))
            for d in range(KC):
                bl.append(('o', q, d, FQ * 128))
    else:
        for j in range(NHEAD):
            bl.append(('zx', j, 0, KC * 128))
            if j == 0:
                bl.append(('zg', 0, 0, KC * 128))
            if j + 1 < NHEAD:
                bl.append(('zg', j + 1, 0, KC * 128))
            bl.append(('gt', j, 0, 256))
        for g in range(4):
            bl.append(('zp', 2 * g, 0, KC * 128))
            bl.append(('zp', 2 * g + 1, 0, KC * 128))
            bl.append(('pw', g, 0, 512))
        for d in range(KC):
            bl.append(('wo', d, 0, KC * 128))
    return bl


def stream_layout(phases):
    out = []
    off = 0
    for (l, kind) in phases:
        for (tag, a, b, n) in phase_blocks(kind):
            out.append((l, kind, tag, a, b, off, n))
            off += n
    return out, off


class Prog:
    ENGS = ('pe', 'act', 'dve', 'pool', 'sp')

    def __init__(self):
        self.q = {e: [] for e in self.ENGS}
        self.cnt = {}
        self.waited = {e: {} for e in self.ENGS}
        self.lastw = {}
        self.readers = {}
        self.final_waits = {e: [] for e in self.ENGS}

    def op(self, eng, fn, reads=(), writes=(), sem=None, pre_inc=0):
        key = sem if sem is not None else eng
        inc = 16 if sem is not None else 1
        deps = {}

        def add(tok):
            if tok is None:
                return
            k, v = tok
            if deps.get(k, 0) < v:
                deps[k] = v

        for r in reads:
            add(self.lastw.get(r))
        for w in writes:
            add(self.lastw.get(w))
            rd = self.readers.get(w)
            if rd:
                for tok in rd.values():
                    add(tok)
        waits = []
        for k, v in deps.items():
            if k == 'pe' and eng == 'pe':
                continue
            if self.waited[eng].get(k, 0) >= v:
                continue
            self.waited[eng][k] = v
            waits.append((k, v))
        self.cnt[key] = self.cnt.get(key, 0) + pre_inc + inc
        tok = (key, self.cnt[key])
        self.q[eng].append((waits, fn, key, inc))
        for r in reads:
            self.readers.setdefault(r, {})[key] = tok
        for w in writes:
            self.lastw[w] = tok
            self.readers[w] = {}
        return tok

    def emit(self, block, sems):
        names = {'pe': 'tensor', 'act': 'scalar', 'dve': 'vector', 'pool': 'gpsimd', 'sp': 'sync'}
        for eng in self.ENGS:
            ops = self.q[eng]
            fw = self.final_waits[eng]

            def body(e, ops=ops, fw=fw):
                for waits, fn, key, inc in ops:
                    for k, v in waits:
                        e.wait_ge(sems[k], v)
                    inst = fn(e)
                    inst.then_inc(sems[key], inc)
                for k, v in fw:
                    e.wait_ge(sems[k], v)

            getattr(block, names[eng])(body)


def build_program(nst=NST, phases=None, nslot=NSLOT):
    if phases is None:
        phases = phases_all()
    layout, WTOT = stream_layout(phases)
    NBLK = len(layout)

    nc = bass.Bass("TRN2", target_bir_lowering=False)
    xin = nc.dram_tensor("xin", [128, KC, TTOT], F32, kind="ExternalInput").ap()
    wst = nc.dram_tensor("wst", [128, WTOT], F32, kind="ExternalInput").ap()
    cv = nc.dram_tensor("cv", [128, NCV], F32, kind="ExternalInput").ap()
    y = nc.dram_tensor("y", [128, KC, SEQ], F32, kind="ExternalOutput").ap()

    P = Prog()
    es = ExitStack()
    with es:
        def sb(name, shape, dt):
            return es.enter_context(nc.sbuf_tensor(name, shape, dt))

        hT = sb("hT", [128, KC, TT], F32)
        xn = sb("xn", [128, KC, HIST + TT], BF16)
        act = sb("act", [128, KC, TT], BF16)
        wr = [sb(f"wr{i}", [128, KC * 128], BF16) for i in range(nslot)]
        sq = [sb(f"sq{i}", [128, TT], BF16) for i in range(2)]
        rstd = sb("rstd", [128, TT], F32)
        NTB = 6
        Tb = [sb(f"Tb{i}", [128, 344], F32) for i in range(NTB)]
        NSET = 3
        U = sb("U", [128, NSET, 6, UW], F32)
        UB = sb("UB", [128, NSET, 2, UW], BF16)
        cvt = sb("cvt", [128, NCV], F32)
        dvt = sb("dvt", [128, NLAYER * DV_L], F32)
        ones = sb("ones", [128, 128], BF16)
        invc = sb("invc", [128, 4, 16], F32)
        xnh = sb("xnh", [128, NLAYER, KC, HIST], BF16)
        lst = sb("lst", [128, NLAYER, NHEAD], F32)
        tmp = sb("tmp", [128, 6, 8], F32)
        ps = [es.enter_context(nc.psum_tensor(f"ps{i}", [128, 512], F32)) for i in range(8)]

        bank_free = list(range(8))

        def next_bank():
            assert bank_free, "out of PSUM banks"
            return bank_free.pop(0)

        def free_bank(b):
            assert b not in bank_free
            bank_free.append(b)

        ring = {'emitted': 0, 'acquired': 0, 'free': [True] * nslot}
        total_blocks = NBLK * nst

        def try_emit_dma():
            while ring['emitted'] < total_blocks:
                n = ring['emitted']
                slot = n % nslot
                if not ring['free'][slot]:
                    return
                ring['free'][slot] = False
                (_, _, _, _, _, off, ncols) = layout[n % NBLK]

                def fn(e, slot=slot, off=off, ncols=ncols):
                    return e.dma_start(out=wr[slot][:, 0:ncols], in_=wst[:, off:off + ncols])

                P.op('pool', fn, reads=[('pro',)], writes=[('ws', slot)], sem=f'ws{slot}')
                ring['emitted'] += 1

        def acquire(tag, a=None):
            n = ring['acquired']
            assert n < ring['emitted'], "weight ring too small for emission order"
            ent = layout[n % NBLK]
            assert ent[2] == tag and (a is None or ent[3] == a), (ent, tag, a)
            ring['acquired'] += 1
            return n % nslot

        def release(slot):
            ring['free'][slot] = True
            try_emit_dma()

        P.op('sp', lambda e: e.dma_start(out=cvt[:, :], in_=cv[:, :]), writes=[('cv',)], sem='ldc')
        P.op('dve', lambda e: e.memset(ones[:, :], 1.0), writes=[('ones',)])
        P.op('dve', lambda e: e.memset(lst[:, :, :], 0.0), writes=[('lst',)])
        for g, win in enumerate(POOL_WIN):
            P.op('dve', lambda e, g=g, win=win: e.memset(invc[:, g, :], 1.0 / win), writes=[('invc',)])
            for t in range(win - 1):
                P.op('dve', lambda e, g=g, t=t: e.memset(invc[:, g, t:t + 1], 1.0 / (t + 1)), writes=[('invc',)])
        for l in range(NLAYER):
            c0 = l * CV_L
            d0 = l * DV_L
            apc = cvt[:, c0 + 104:c0 + 112]
            P.op('dve', lambda e, apc=apc: e.tensor_scalar(out=tmp[:, 0, :], in0=apc, scalar1=-1.0, scalar2=None, op0=ALU.mult),
                 reads=[('cv',)], writes=[('tmp', 0)])
            P.op('dve', lambda e, apc=apc: e.tensor_tensor(out=tmp[:, 1, :], in0=tmp[:, 0, :], in1=apc, op=ALU.max),
                 reads=[('tmp', 0), ('cv',)], writes=[('tmp', 1)])
            P.op('act', lambda e: e.activation(out=tmp[:, 2, :], in_=tmp[:, 1, :], func=AF.Exp, scale=-1.0),
                 reads=[('tmp', 1)], writes=[('tmp', 2)])
            P.op('act', lambda e: e.activation(out=tmp[:, 3, :], in_=tmp[:, 2, :], func=AF.Ln, bias=1.0),
                 reads=[('tmp', 2)], writes=[('tmp', 3)])
            P.op('dve', lambda e: e.tensor_scalar(out=tmp[:, 4, :], in0=tmp[:, 0, :], scalar1=0.0, scalar2=None, op0=ALU.max),
                 reads=[('tmp', 0)], writes=[('tmp', 4)])
            P.op('dve', lambda e: e.tensor_tensor(out=tmp[:, 5, :], in0=tmp[:, 4, :], in1=tmp[:, 3, :], op=ALU.add),
                 reads=[('tmp', 4), ('tmp', 3)], writes=[('tmp', 5)])
            P.op('dve', lambda e, d0=d0: e.tensor_scalar(out=dvt[:, d0:d0 + 8], in0=tmp[:, 5, :], scalar1=-8.0, scalar2=None, op0=ALU.mult),
                 reads=[('tmp', 5)], writes=[('dv',)])
            P.op('dve', lambda e, d0=d0: e.tensor_scalar(out=dvt[:, d0 + 8:d0 + 16], in0=tmp[:, 5, :], scalar1=-4.0, scalar2=None, op0=ALU.mult),
                 reads=[('tmp', 5)], writes=[('dv',)])
            P.op('dve', lambda e, d0=d0, c0=c0: e.tensor_tensor(out=dvt[:, d0 + 16:d0 + 24], in0=cvt[:, c0 + 112:c0 + 120],
                                                                 in1=cvt[:, c0 + 120:c0 + 128], op=ALU.mult),
                 reads=[('cv',)], writes=[('dv',)])
            P.op('dve', lambda e, d0=d0, c0=c0: e.tensor_scalar(out=dvt[:, d0 + 24:d0 + 40], in0=cvt[:, c0 + 88:c0 + 104], scalar1=0.5,
                                                                 scalar2=None, op0=ALU.mult),
                 reads=[('cv',)], writes=[('dv',)])
        P.op('dve', lambda e: e.memset(tmp[:, 0, :], 0.0), reads=[('dv',), ('ones',), ('invc',), ('lst',)], writes=[('pro',), ('tmp', 0)])

        try_emit_dma()

        sq_ctr = [0]
        tb_ctr = [0]

        def hreg(k):
            return [('h', k, s) for s in range(3)]

        def norm(gcol0, mode):
            banks = [next_bank() for _ in SUBS]
            for k in range(KC):
                i = k % 2
                if i == 0:
                    P.op('act', lambda e, k=k, i=i: e.activation(out=sq[i][:, :], in_=hT[:, k, :], func=AF.Square),
                         reads=hreg(k), writes=[('sq', i)])
                else:
                    P.op('dve', lambda e, k=k, i=i: e.tensor_tensor(out=sq[i][:, :], in0=hT[:, k, :], in1=hT[:, k, :], op=ALU.mult),
                         reads=hreg(k), writes=[('sq', i)])

                def mm(e, k=k, i=i):
                    last = None
                    for s, (o, w) in enumerate(SUBS):
                        last = e.matmul(ps[banks[s]][:, 0:w], ones[:, :], sq[i][:, o:o + w],
                                        start=(k == 0), stop=(k == KC - 1))
                    return last

                P.op('pe', mm, reads=[('sq', i), ('ones',)], writes=[('ps', b) for b in banks])
            for s, (o, w) in enumerate(SUBS):
                bk = banks[s]
                P.op('act', lambda e, bk=bk, w=w: e.activation(out=ps[bk][:, 0:w], in_=ps[bk][:, 0:w],
                                                               func=AF.Ln, scale=1.0 / D, bias=EPS),
                     reads=[('ps', bk)], writes=[('ps', bk)])
                P.op('act', lambda e, bk=bk, o=o, w=w: e.activation(out=rstd[:, o:o + w], in_=ps[bk][:, 0:w],
                                                                    func=AF.Exp, scale=-0.5),
                     reads=[('ps', bk)], writes=[('rstd', s)])
                free_bank(bk)
            for k in range(KC):
                if mode == 'xn':
                    P.op('dve', lambda e, k=k: e.scalar_tensor_tensor(
                        out=xn[:, k, HIST:HIST + TT], in0=hT[:, k, :], scalar=cvt[:, gcol0 + k:gcol0 + k + 1],
                        in1=rstd[:, :], op0=ALU.mult, op1=ALU.mult),
                        reads=hreg(k) + [('rstd', s) for s in range(3)] + [('cv',)],
                        writes=[('xn', k, s) for s in range(3)])
                else:
                    P.op('dve', lambda e, k=k: e.scalar_tensor_tensor(
                        out=hT[:, k, :], in0=hT[:, k, :], scalar=cvt[:, gcol0 + k:gcol0 + k + 1],
                        in1=rstd[:, :], op0=ALU.mult, op1=ALU.mult),
                        reads=hreg(k) + [('rstd', s) for s in range(3)] + [('cv',)],
                        writes=hreg(k))

        def xn_regs():
            return [('xn', k, s) for k in range(KC) for s in range(3)]

        def ffn(l, which):
            gcol0 = l * CV_L + (0 if which == 1 else 32)
            norm(gcol0, 'xn')
            for q in range(Q):
                for fi in range(FQ):
                    f = q * FQ + fi
                    bks = []
                    for tag in ('g', 'u'):
                        slot = acquire(tag, f)
                        banks = [next_bank() for _ in SUBS]

                        def mm(e, slot=slot, banks=banks):
                            last = None
                            for k in range(KC):
                                for s, (o, w) in enumerate(SUBS):
                                    last = e.matmul(ps[banks[s]][:, 0:w], wr[slot][:, k * 128:(k + 1) * 128],
                                                    xn[:, k, HIST + o:HIST + o + w], start=(k == 0), stop=(k == KC - 1))
                            return last

                        if f == 0 and tag == 'g':
                            for k in range(KC):
                                def mmk(e, slot=slot, banks=banks, k=k):
                                    last = None
                                    for s, (o, w) in enumerate(SUBS):
                                        last = e.matmul(ps[banks[s]][:, 0:w], wr[slot][:, k * 128:(k + 1) * 128],
                                                        xn[:, k, HIST + o:HIST + o + w], start=(k == 0), stop=(k == KC - 1))
                                    return last
                                P.op('pe', mmk, reads=[('ws', slot)] + [('xn', k, s) for s in range(3)],
                                     writes=[('ps', b) for b in banks])
                        else:
                            P.op('pe', mm, reads=[('ws', slot)] + xn_regs(), writes=[('ps', b) for b in banks])
                        release(slot)
                        bks.append(banks)
                    for s, (o, w) in enumerate(SUBS):
                        ti = tb_ctr[0] % NTB
                        tb_ctr[0] += 1
                        bg, bu = bks[0][s], bks[1][s]
                        P.op('act', lambda e, ti=ti, bg=bg, w=w: e.activation(out=Tb[ti][:, 0:w], in_=ps[bg][:, 0:w], func=AF.Silu),
                             reads=[('ps', bg)], writes=[('T', ti)])
                        P.op('dve', lambda e, ti=ti, bu=bu, fi=fi, o=o, w=w: e.tensor_tensor(
                            out=act[:, fi, o:o + w], in0=Tb[ti][:, 0:w], in1=ps[bu][:, 0:w], op=ALU.mult),
                            reads=[('T', ti), ('ps', bu)], writes=[('act', fi, s)])
                        free_bank(bg)
                        free_bank(bu)
                for d in range(KC):
                    slot = acquire('o', q)
                    banks = [next_bank() for _ in SUBS]

                    def mm(e, slot=slot, banks=banks):
                        last = None
                        for kk in range(FQ):
                            for s, (o, w) in enumerate(SUBS):
                                last = e.matmul(ps[banks[s]][:, 0:w], wr[slot][:, kk * 128:(kk + 1) * 128],
                                                act[:, kk, o:o + w], start=(kk == 0), stop=(kk == FQ - 1))
                        return last

                    P.op('pe', mm, reads=[('ws', slot)] + [('act', kk, s) for kk in range(FQ) for s in range(3)],
                         writes=[('ps', b) for b in banks])
                    release(slot)
                    for s, (o, w) in enumerate(SUBS):
                        b = banks[s]
                        P.op('dve', lambda e, b=b, d=d, o=o, w=w: e.scalar_tensor_tensor(
                            out=hT[:, d, o:o + w], in0=ps[b][:, 0:w], scalar=0.5, in1=hT[:, d, o:o + w],
                            op0=ALU.mult, op1=ALU.add),
                            reads=[('ps', b), ('h', d, s)], writes=[('h', d, s)])
                        free_bank(b)

        def mixer(l, st):
            c0 = l * CV_L
            d0 = l * DV_L
            if st == 0:
                P.op('dve', lambda e: e.memset(xn[:, :, 0:HIST], 0.0), writes=[('xnh',)])
            else:
                P.op('act', lambda e: e.activation(out=xn[:, :, 0:HIST], in_=xnh[:, l, :, :], func=AF.Identity),
                     reads=[('xnhist', l)], writes=[('xnh',)])
            norm(c0 + 16, 'xn')
            P.op('act', lambda e: e.activation(out=xnh[:, l, :, :], in_=xn[:, :, TT:TT + HIST], func=AF.Identity),
                 reads=[('xn', k, 2) for k in range(KC)], writes=[('xnhist', l)])

            def xin_regs(s, hist):
                r = [('xn', k, s) for k in range(KC)]
                if hist:
                    if s == 0:
                        r.append(('xnh',))
                    else:
                        r += [('xn', k, s - 1) for k in range(KC)]
                return r

            def proj(slot, bank, s, nh, split_k=False):
                o, w = SUBS[s]

                def mm(e, ks=range(KC)):
                    last = None
                    for k in ks:
                        last = e.matmul(ps[bank][:, 0:nh + w], wr[slot][:, k * 128:(k + 1) * 128],
                                        xn[:, k, HIST + o - nh:HIST + o + w], start=(k == 0), stop=(k == KC - 1))
                    return last

                if split_k:
                    for k in range(KC):
                        r = [('ws', slot), ('xn', k, s)]
                        if nh > 0:
                            r.append(('xnh',) if s == 0 else ('xn', k, s - 1))
                        P.op('pe', lambda e, k=k: mm(e, [k]), reads=r, writes=[('ps', bank)])
                else:
                    P.op('pe', mm, reads=[('ws', slot)] + xin_regs(s, nh > 0), writes=[('ps', bank)])

            units = [('lru', j, s) for j in range(NHEAD) for s in range(3)] + \
                    [('pool', g, s) for g in range(4) for s in range(3)]
            nu = len(units)
            blk = {}
            ctx = [dict() for _ in units]
            zgb = {}

            def R_(us, i):
                return ('U', us, i)

            def A_pe(u):
                kind, j, s = units[u]
                if kind == 'lru':
                    if s == 0:
                        blk['zx', j] = acquire('zx', j)
                    bzx = next_bank()
                    ctx[u]['bzx'] = bzx
                    proj(blk['zx', j], bzx, s, 3, split_k=(u == 0))
                    if u == 0:
                        zs = acquire('zg', 0)
                        zgb[0] = [next_bank() for _ in range(3)]
                        for s2 in range(3):
                            proj(zs, zgb[0][s2], s2, 0)
                        release(zs)
                    if j + 1 < NHEAD:
                        if s == 0:
                            blk['zg', j + 1] = acquire('zg', j + 1)
                            zgb[j + 1] = [None] * 3
                        zgb[j + 1][s] = next_bank()
                        proj(blk['zg', j + 1], zgb[j + 1][s], s, 0)
                        if s == 2:
                            release(blk['zg', j + 1])
                    if s == 2:
                        release(blk['zx', j])
                else:
                    g = j
                    if s == 0:
                        blk['zp', 2 * g] = acquire('zp', 2 * g)
                        blk['zp', 2 * g + 1] = acquire('zp', 2 * g + 1)
                    bz = [next_bank(), next_bank()]
                    ctx[u]['bz'] = bz
                    for cc in range(2):
                        proj(blk['zp', 2 * g + cc], bz[cc], s, 15)
                    if s == 2:
                        release(blk['zp', 2 * g])
                        release(blk['zp', 2 * g + 1])

            def A_evac(u):
                kind, j, s = units[u]
                us = u % NSET
                o, w = SUBS[s]
                if kind == 'lru':
                    bzx = ctx[u]['bzx']
                    ZX, XC = U[:, us, 0, :], U[:, us, 1, :]
                    cw3 = cvt[:, c0 + 48 + 3 * 8 + j:c0 + 48 + 3 * 8 + j + 1]
                    cb = cvt[:, c0 + 80 + j:c0 + 81 + j]
                    P.op('dve', lambda e: e.tensor_copy(out=ZX[:, 0:3 + w], in_=ps[bzx][:, 0:3 + w]),
                         reads=[('ps', bzx)], writes=[R_(us, 0)])
                    free_bank(bzx)
                    P.op('act', lambda e: e.activation(out=XC[:, 0:w], in_=ZX[:, 3:3 + w], func=AF.Identity, scale=cw3, bias=cb),
                         reads=[R_(us, 0), ('cv',)], writes=[R_(us, 1)])
                    gh = None
                    if u == 0:
                        gh = 0
                    elif s == 2 and j + 1 < NHEAD:
                        gh = j + 1
                    if gh is not None:
                        bzg = zgb[gh]
                        for s2 in range(3):
                            w2 = SUBS[s2][1]
                            gi = (3 * gh + s2) % NTB
                            P.op('act', lambda e, s2=s2, w2=w2, gi=gi, bzg=bzg: e.activation(
                                out=Tb[gi][:, 0:w2], in_=ps[bzg[s2]][:, 0:w2], func=AF.Gelu_apprx_tanh),
                                reads=[('ps', bzg[s2])], writes=[('T', gi)])
                            free_bank(bzg[s2])
                else:
                    bz = ctx[u]['bz']
                    for cc in range(2):
                        P.op('act', lambda e, cc=cc: e.activation(out=U[:, us, cc, 0:15 + w], in_=ps[bz[cc]][:, 0:15 + w], func=AF.Identity),
                             reads=[('ps', bz[cc])], writes=[R_(us, cc)])
                        free_bank(bz[cc])

            def B1_dve(u):
                kind, j, s = units[u]
                us = u % NSET
                o, w = SUBS[s]
                if kind == 'lru':
                    ZX, XC = U[:, us, 0, :], U[:, us, 1, :]
                    XCB = UB[:, us, 0, :]
                    cw = [cvt[:, c0 + 48 + kk * 8 + j:c0 + 48 + kk * 8 + j + 1] for kk in range(4)]
                    for kk in (2, 1, 0):
                        P.op('dve', lambda e, kk=kk: e.scalar_tensor_tensor(out=XC[:, 0:w], in0=ZX[:, kk:kk + w], scalar=cw[kk],
                                                                            in1=XC[:, 0:w], op0=ALU.mult, op1=ALU.add),
                             reads=[R_(us, 0), R_(us, 1), ('cv',)], writes=[R_(us, 1)])
                    P.op('dve', lambda e: e.tensor_tensor(out=XCB[:, 0:w], in0=XC[:, 0:w], in1=XC[:, 0:w], op=ALU.max),
                         reads=[R_(us, 1)], writes=[('UB', us, 0)])
                else:
                    g = j
                    win = POOL_WIN[g]
                    nlev = g + 1
                    n = 15 + w
                    for cc in range(2):
                        Uc = U[:, us, cc, :]
                        A = U[:, us, 2 + 2 * cc, :]
                        B = U[:, us, 3 + 2 * cc, :]
                        rA, rB = R_(us, 2 + 2 * cc), R_(us, 3 + 2 * cc)
                        src_, rsrc = Uc, R_(us, cc)
                        for lev in range(nlev):
                            sh = 1 << lev
                            lo = (1 << (lev + 1)) - 1
                            dst, rdst = (A, rA) if lev % 2 == 0 else (B, rB)
                            P.op('dve', lambda e, src_=src_, dst=dst, sh=sh, lo=lo: e.tensor_tensor(
                                out=dst[:, lo:n], in0=src_[:, lo:n], in1=src_[:, lo - sh:n - sh], op=ALU.add),
                                reads=[rsrc], writes=[rdst])
                            src_, rsrc = dst, rdst
                        Dd = UB[:, us, cc, :]
                        P.op('dve', lambda e, src_=src_, Uc=Uc, Dd=Dd: e.scalar_tensor_tensor(
                            out=Dd[:, 0:w], in0=src_[:, 15:15 + w], scalar=1.0 / win, in1=Uc[:, 15:15 + w],
                            op0=ALU.mult, op1=ALU.subtract),
                            reads=[rsrc, R_(us, cc)], writes=[('UB', us, cc)])
                        if st == 0 and s == 0:
                            m = win - 1
                            other, rother = (B, rB) if src_ is A else (A, rA)
                            P.op('dve', lambda e, src_=src_, other=other, m=m: e.tensor_tensor(
                                out=other[:, 0:m], in0=src_[:, 15:15 + m], in1=invc[:, g, 0:m], op=ALU.mult),
                                reads=[rsrc, ('invc',)], writes=[rother])
                            P.op('dve', lambda e, other=other, Uc=Uc, Dd=Dd, m=m: e.tensor_tensor(
                                out=Dd[:, 0:m], in0=other[:, 0:m], in1=Uc[:, 15:15 + m], op=ALU.subtract),
                                reads=[rother, R_(us, cc), ('UB', us, cc)], writes=[('UB', us, cc)])

            def B1_pe(u):
                kind, j, s = units[u]
                us = u % NSET
                o, w = SUBS[s]
                if kind == 'lru':
                    XCB = UB[:, us, 0, :]
                    if s == 0:
                        blk['gt', j] = acquire('gt', j)
                    gs = blk['gt', j]
                    br, bi = next_bank(), next_bank()
                    ctx[u]['br'], ctx[u]['bi'] = br, bi

                    def mm(e):
                        e.matmul(ps[br][:, 0:w], wr[gs][:, 0:128], XCB[:, 0:w], start=True, stop=True)
                        return e.matmul(ps[bi][:, 0:w], wr[gs][:, 128:256], XCB[:, 0:w], start=True, stop=True)

                    P.op('pe', mm, reads=[('ws', gs), ('UB', us, 0)], writes=[('ps', br), ('ps', bi)])
                    if s == 2:
                        release(gs)
                else:
                    g = j
                    if s == 0:
                        blk['pw', g] = acquire('pw', g)
                    pslot = blk['pw', g]
                    by = [next_bank(), next_bank()]
                    ctx[u]['by'] = by

                    def mm(e):
                        last = None
                        for jc in range(2):
                            for kc in range(2):
                                last = e.matmul(ps[by[jc]][:, 0:w], wr[pslot][:, kc * 256 + jc * 128:kc * 256 + jc * 128 + 128],
                                                UB[:, us, kc, 0:w], start=(kc == 0), stop=(kc == 1))
                        return last

                    P.op('pe', mm, reads=[('ws', pslot), ('UB', us, 0), ('UB', us, 1)], writes=[('ps', by[0]), ('ps', by[1])])
                    if s == 2:
                        release(pslot)

            def B1_act(u):
                kind, j, s = units[u]
                us = u % NSET
                o, w = SUBS[s]
                if kind == 'lru':
                    Rr, Ii, Mm = U[:, us, 2, :], U[:, us, 3, :], U[:, us, 4, :]
                    br, bi = ctx[u]['br'], ctx[u]['bi']
                    hba = dvt[:, d0 + 24 + j:d0 + 25 + j]
                    hbx = dvt[:, d0 + 32 + j:d0 + 33 + j]
                    cc1 = dvt[:, d0 + j:d0 + j + 1]
                    cch = dvt[:, d0 + 8 + j:d0 + 9 + j]
                    P.op('act', lambda e: e.activation(out=Rr[:, 0:w], in_=ps[br][:, 0:w], func=AF.Tanh, scale=0.5, bias=hba),
                         reads=[('ps', br), ('dv',)], writes=[R_(us, 2)])
                    P.op('act', lambda e: e.activation(out=Ii[:, 0:w], in_=ps[bi][:, 0:w], func=AF.Tanh, scale=0.5, bias=hbx),
                         reads=[('ps', bi), ('dv',)], writes=[R_(us, 3)])
                    free_bank(br)
                    free_bank(bi)
                    P.op('act', lambda e: e.activation(out=Mm[:, 0:w], in_=Rr[:, 0:w], func=AF.Exp, scale=cc1, bias=cc1),
                         reads=[R_(us, 2), ('dv',)], writes=[R_(us, 4)])
                    P.op('act', lambda e: e.activation(out=Rr[:, 0:w], in_=Rr[:, 0:w], func=AF.Exp, scale=cch, bias=cch),
                         reads=[R_(us, 2), ('dv',)], writes=[R_(us, 2)])
                    P.op('act', lambda e: e.activation(out=Mm[:, 0:w], in_=Mm[:, 0:w], func=AF.Sqrt, scale=-0.25, bias=0.25),
                         reads=[R_(us, 4)], writes=[R_(us, 4)])
                else:
                    g = j
                    by = ctx[u]['by']
                    for jc in range(2):
                        c = 2 * g + jc
                        sc = cvt[:, c0 + 120 + c:c0 + 121 + c]
                        bsc = dvt[:, d0 + 16 + c:d0 + 17 + c]
                        P.op('act', lambda e, jc=jc, c=c, sc=sc, bsc=bsc: e.activation(
                            out=act[:, 8 + c, o:o + w], in_=ps[by[jc]][:, 0:w], func=AF.Identity, scale=sc, bias=bsc),
                            reads=[('ps', by[jc]), ('cv',), ('dv',)], writes=[('act', 8 + c, s)])
                        free_bank(by[jc])

            def B2_dve(u):
                kind, j, s = units[u]
                if kind != 'lru':
                    return
                us = u % NSET
                o, w = SUBS[s]
                XC, Rr, Ii, Mm, HS = [U[:, us, i, :] for i in (1, 2, 3, 4, 5)]
                P.op('dve', lambda e: e.scalar_tensor_tensor(out=Mm[:, 0:w], in0=Ii[:, 0:w], scalar=1.0, in1=Mm[:, 0:w],
                                                             op0=ALU.add, op1=ALU.mult),
                     reads=[R_(us, 3), R_(us, 4)], writes=[R_(us, 4)])
                P.op('dve', lambda e: e.tensor_tensor(out=Mm[:, 0:w], in0=Mm[:, 0:w], in1=XC[:, 0:w], op=ALU.mult),
                     reads=[R_(us, 4), R_(us, 1)], writes=[R_(us, 4)])
                if s == 0:
                    init = lst[:, l, j:j + 1]
                    rinit = ('lst',)
                else:
                    pus = (u - 1) % NSET
                    pw_ = SUBS[s - 1][1]
                    init = U[:, pus, 5, pw_ - 1:pw_]
                    rinit = R_(pus, 5)
                P.op('dve', lambda e: e.tensor_tensor_scan(out=HS[:, 0:w], data0=Rr[:, 0:w], data1=Mm[:, 0:w], initial=init,
                                                           op0=ALU.mult, op1=ALU.add),
                     reads=[R_(us, 2), R_(us, 4), rinit], writes=[R_(us, 5)])
                if s == 2:
                    P.op('act', lambda e: e.activation(out=lst[:, l, j:j + 1], in_=HS[:, w - 1:w], func=AF.Identity),
                         reads=[R_(us, 5)], writes=[('lst',)])
                gi = u % NTB
                P.op('dve', lambda e: e.tensor_tensor(out=act[:, j, o:o + w], in0=HS[:, 0:w], in1=Tb[gi][:, 0:w], op=ALU.mult),
                     reads=[R_(us, 5), ('T', gi)], writes=[('act', j, s)])

            for t in range(nu + 2):
                if t < nu:
                    A_pe(t)
                if 0 <= t - 1 < nu:
                    B1_dve(t - 1)
                    B1_pe(t - 1)
                if t < nu:
                    A_evac(t)
                if 0 <= t - 1 < nu:
                    B1_act(t - 1)
                if 0 <= t - 2 < nu:
                    B2_dve(t - 2)

            for d in range(KC):
                slot = acquire('wo', d)
                banks = [next_bank() for _ in SUBS]

                def mm(e, slot=slot, banks=banks):
                    last = None
                    for k in range(KC):
                        for s, (o, w) in enumerate(SUBS):
                            last = e.matmul(ps[banks[s]][:, 0:w], wr[slot][:, k * 128:(k + 1) * 128],
                                            act[:, k, o:o + w], start=(k == 0), stop=(k == KC - 1))
                    return last

                P.op('pe', mm, reads=[('ws', slot)] + [('act', k, s) for k in range(KC) for s in range(3)],
                     writes=[('ps', b) for b in banks])
                release(slot)
                for s, (o, w) in enumerate(SUBS):
                    b = banks[s]
                    P.op('dve', lambda e, b=b, d=d, o=o, w=w: e.tensor_tensor(
                        out=hT[:, d, o:o + w], in0=ps[b][:, 0:w], in1=hT[:, d, o:o + w], op=ALU.add),
                        reads=[('ps', b), ('h', d, s)], writes=[('h', d, s)])
                    free_bank(b)

        allh = [('h', k, s) for k in range(KC) for s in range(3)]
        for st in range(nst):
            def ld(e, st=st):
                last = None
                for k in range(KC):
                    if last is not None:
                        last.then_inc(sems['ldh'], 16)
                    last = e.dma_start(out=hT[:, k, :], in_=xin[:, k, st * TT:(st + 1) * TT])
                return last

            P.op('sp', ld, writes=allh, sem='ldh', pre_inc=16 * (KC - 1))
            for (l, kind) in phases:
                if kind == 'mix':
                    mixer(l, st)
                else:
                    ffn(l, 1 if kind == 'ffn1' else 2)
            norm(NLAYER * CV_L, 'final')
            c_lo = NMETA if st == 0 else 0
            t_lo = st * TT + c_lo - NMETA
            ncol = TT - c_lo

            def stf(e, c_lo=c_lo, t_lo=t_lo, ncol=ncol):
                last = None
                for k in range(KC):
                    if last is not None:
                        last.then_inc(sems['sth'], 16)
                    last = e.dma_start(out=y[:, k, t_lo:t_lo + ncol], in_=hT[:, k, c_lo:c_lo + ncol])
                return last

            P.op('sp', stf, reads=allh, sem='sth', pre_inc=16 * (KC - 1))
        P.final_waits['sp'].append(('sth', P.cnt['sth']))
        assert ring['acquired'] == total_blocks and ring['emitted'] == total_blocks

        sems = {}
        for key in P.cnt:
            sems[key] = es.enter_context(nc.semaphore(f"s_{key}"))
        block = es.enter_context(nc.Block())
        P.emit(block, sems)
    return nc


def _kmaj(w):
    K = w.shape[0] // 128
    return np.ascontiguousarray(w.reshape(K, 128, w.shape[1]).transpose(1, 0, 2)).reshape(128, K * w.shape[1])


def build_wstream(inp, phases):
    layout, WTOT = stream_layout(phases)
    wst = np.empty((128, WTOT), dtype=np.float32)
    for (l, kind, tag, a, b, off, n) in layout:
        if kind in ('ffn1', 'ffn2'):
            w_in = inp['ffn1_w_in' if kind == 'ffn1' else 'ffn2_w_in'][l]
            w_out = inp['ffn1_w_out' if kind == 'ffn1' else 'ffn2_w_out'][l]
            if tag == 'g':
                blkv = _kmaj(w_in[:, a * 128:(a + 1) * 128])
            elif tag == 'u':
                blkv = _kmaj(w_in[:, DFF + a * 128:DFF + (a + 1) * 128])
            else:
                q, d = a, b
                blkv = _kmaj(w_out[q * FQ * 128:(q + 1) * FQ * 128, d * 128:(d + 1) * 128])
        else:
            w_in = inp['w_in'][l]
            if tag == 'zx':
                blkv = _kmaj(w_in[:, a * 128:(a + 1) * 128])
            elif tag == 'zg':
                blkv = _kmaj(w_in[:, 1024 + a * 128:1024 + (a + 1) * 128])
            elif tag == 'zp':
                blkv = _kmaj(w_in[:, 2048 + a * 128:2048 + (a + 1) * 128])
            elif tag == 'gt':
                blkv = np.concatenate([inp['lru_wa'][l, a], inp['lru_wx'][l, a]], axis=1)
            elif tag == 'pw':
                blkv = _kmaj(inp['pool_w'][l, a])
            else:
                blkv = _kmaj(inp['w_out'][l][:, a * 128:(a + 1) * 128])
        assert blkv.shape == (128, n), (blkv.shape, n, tag)
        wst[:, off:off + n] = blkv
    return wst


def build_cvec(inp):
    cvv = np.zeros((128, NCV), dtype=np.float32)

    def col(v):
        return np.asarray(v, dtype=np.float32).reshape(-1, 128).T

    for l in range(NLAYER):
        c0 = l * CV_L
        cvv[:, c0 + 0:c0 + 16] = col(inp['ffn1_norm'][l])
        cvv[:, c0 + 16:c0 + 32] = col(inp['mix_norm'][l])
        cvv[:, c0 + 32:c0 + 48] = col(inp['ffn2_norm'][l])
        for kk in range(4):
            cvv[:, c0 + 48 + kk * 8:c0 + 56 + kk * 8] = col(inp['conv_w'][l, kk])
        cvv[:, c0 + 80:c0 + 88] = col(inp['conv_b'][l])
        cvv[:, c0 + 88:c0 + 96] = col(inp['lru_ba'][l])
        cvv[:, c0 + 96:c0 + 104] = col(inp['lru_bx'][l])
        cvv[:, c0 + 104:c0 + 112] = col(inp['lru_a_param'][l])
        cvv[:, c0 + 112:c0 + 120] = col(inp['pool_b'][l])
        cvv[:, c0 + 120:c0 + 128] = col(inp['pool_scale'][l])
    cvv[:, NLAYER * CV_L:NLAYER * CV_L + KC] = col(inp['final_norm'])
    return cvv


def build_xin(x_b, meta):
    hfull = np.concatenate([meta, x_b], axis=0)
    return np.ascontiguousarray(hfull.T.reshape(KC, 128, TTOT).transpose(1, 0, 2))


_CACHE = {}


def kernel(**inputs):
    inp = {k: np.asarray(v) for k, v in inputs.items()}
    x = inp['x'].astype(np.float32, copy=False)
    B = x.shape[0]
    phases = phases_all()
    if 'nc' not in _CACHE:
        _CACHE['nc'] = build_program(NST, phases, NSLOT)
    nc = _CACHE['nc']
    wst = build_wstream(inp, phases)
    cvv = build_cvec(inp)
    meta = inp['meta_tokens'].astype(np.float32, copy=False)
    in_maps = [{"xin": build_xin(x[b], meta), "wst": wst, "cv": cvv} for b in range(B)]
    res = run_bass_kernel_spmd(nc, in_maps, core_ids=list(range(B)))
    out = np.empty((B, SEQ, D), dtype=np.float32)
    for b in range(B):
        yb = np.asarray(res.results[b]["y"])
        out[b] = yb.transpose(2, 1, 0).reshape(SEQ, D)
    return out
```

```python
import numpy as np
from contextlib import ExitStack

import concourse.bass as bass
import concourse.mybir as mybir
from concourse.bass_utils import run_bass_kernel_spmd

F32 = mybir.dt.float32
BF16 = mybir.dt.bfloat16
AF = mybir.ActivationFunctionType
ALU = mybir.AluOpType

D = 2048
KC = 16
SEQ = 4096
NMETA = 16
TTOT = SEQ + NMETA
DFF = 5632
FC = DFF // 128
Q = 4
FQ = FC // Q
NST = 4
TT = TTOT // NST
SUBS = [(0, 343), (343, 343), (686, 342)]
HIST = 16
NSLOT = 6
EPS = 1e-6
NHEAD = 8
POOL_WIN = (2, 4, 8, 16)
NLAYER = 2
UW = 360

CV_L = 128
NCV = CV_L * NLAYER + KC
DV_L = 40


def phases_all():
    ph = []
    for l in range(NLAYER):
        ph += [(l, 'ffn1'), (l, 'mix'), (l, 'ffn2')]
    return ph


def phase_blocks(kind):
    bl = []
    if kind in ('ffn1', 'ffn2'):
        for q in range(Q):
            for fi in range(FQ):
                f = q * FQ + fi
                bl.append(('g', f, 0, KC * 128))
                bl.append(('u', f, 0, KC * 128))
            for d in range(KC):
                bl.append(('o', q, d, FQ * 128))
    else:
        for j in range(NHEAD):
            bl.append(('zx', j, 0, KC * 128))
            if j == 0:
                bl.append(('zg', 0, 0, KC * 128))
            if j + 1 < NHEAD:
                bl.append(('zg', j + 1, 0, KC * 128))
            bl.append(('gt', j, 0, 256))
        for g in range(4):
            bl.append(('zp', 2 * g, 0, KC * 128))
            bl.append(('zp', 2 * g + 1, 0, KC * 128))
            bl.append(('pw', g, 0, 512))
        for d in range(KC):
            bl.append(('wo', d, 0, KC * 128))
    return bl


def stream_layout(phases):
    out = []
    off = 0
    for (l, kind) in phases:
        for (tag, a, b, n) in phase_blocks(kind):
            out.append((l, kind, tag, a, b, off, n))
            off += n
    return out, off


class Prog:
    ENGS = ('pe', 'act', 'dve', 'pool', 'sp')

    def __init__(self):
        self.q = {e: [] for e in self.ENGS}
        self.cnt = {}
        self.waited = {e: {} for e in self.ENGS}
        self.lastw = {}
        self.readers = {}
        self.final_waits = {e: [] for e in self.ENGS}

    def op(self, eng, fn, reads=(), writes=(), sem=None, pre_inc=0):
        key = sem if sem is not None else eng
        inc = 16 if sem is not None else 1
        deps = {}

        def add(tok):
            if tok is None:
                return
            k, v = tok
            if deps.get(k, 0) < v:
                deps[k] = v

        for r in reads:
            add(self.lastw.get(r))
        for w in writes:
            add(self.lastw.get(w))
            rd = self.readers.get(w)
            if rd:
                for tok in rd.values():
                    add(tok)
        waits = []
        for k, v in deps.items():
            if k == 'pe' and eng == 'pe':
                continue
            if self.waited[eng].get(k, 0) >= v:
                continue
            self.waited[eng][k] = v
            waits.append((k, v))
        self.cnt[key] = self.cnt.get(key, 0) + pre_inc + inc
        tok = (key, self.cnt[key])
        self.q[eng].append((waits, fn, key, inc))
        for r in reads:
            self.readers.setdefault(r, {})[key] = tok
        for w in writes:
            self.lastw[w] = tok
            self.readers[w] = {}
        return tok

    def emit(self, block, sems):
        names = {'pe': 'tensor', 'act': 'scalar', 'dve': 'vector', 'pool': 'gpsimd', 'sp': 'sync'}
        for eng in self.ENGS:
            ops = self.q[eng]
            fw = self.final_waits[eng]

            def body(e, ops=ops, fw=fw):
                for waits, fn, key, inc in ops:
                    for k, v in waits:
                        e.wait_ge(sems[k], v)
                    inst = fn(e)
                    inst.then_inc(sems[key], inc)
                for k, v in fw:
                    e.wait_ge(sems[k], v)

            getattr(block, names[eng])(body)


def build_program(nst=NST, phases=None, nslot=NSLOT):
    if phases is None:
        phases = phases_all()
    layout, WTOT = stream_layout(phases)
    NBLK = len(layout)

    nc = bass.Bass("TRN2", target_bir_lowering=False)
    xin = nc.dram_tensor("xin", [128, KC, TTOT], F32, kind="ExternalInput").ap()
    wst = nc.dram_tensor("wst", [128, WTOT], F32, kind="ExternalInput").ap()
    cv = nc.dram_tensor("cv", [128, NCV], F32, kind="ExternalInput").ap()
    y = nc.dram_tensor("y", [128, KC, SEQ], F32, kind="ExternalOutput").ap()

    P = Prog()
    es = ExitStack()
    with es:
        def sb(name, shape, dt):
            return es.enter_context(nc.sbuf_tensor(name, shape, dt))

        hT = sb("hT", [128, KC, TT], F32)
        xn = sb("xn", [128, KC, HIST + TT], BF16)
        act = sb("act", [128, KC, TT], BF16)
        wr = [sb(f"wr{i}", [128, KC * 128], BF16) for i in range(nslot)]
        sq = [sb(f"sq{i}", [128, TT], BF16) for i in range(2)]
        rstd = sb("rstd", [128, TT], F32)
        NTB = 6
        Tb = [sb(f"Tb{i}", [128, 344], F32) for i in range(NTB)]
        NSET = 3
        U = sb("U", [128, NSET, 6, UW], F32)
        UB = sb("UB", [128, NSET, 2, UW], BF16)
        cvt = sb("cvt", [128, NCV], F32)
        dvt = sb("dvt", [128, NLAYER * DV_L], F32)
        ones = sb("ones", [128, 128], BF16)
        invc = sb("invc", [128, 4, 16], F32)
        xnh = sb("xnh", [128, NLAYER, KC, HIST], BF16)
        lst = sb("lst", [128, NLAYER, NHEAD], F32)
        tmp = sb("tmp", [128, 6, 8], F32)
        ps = [es.enter_context(nc.psum_tensor(f"ps{i}", [128, 512], F32)) for i in range(8)]

        bank_free = list(range(8))

        def next_bank():
            assert bank_free, "out of PSUM banks"
            return bank_free.pop(0)

        def free_bank(b):
            assert b not in bank_free
            bank_free.append(b)

        ring = {'emitted': 0, 'acquired': 0, 'free': [True] * nslot}
        total_blocks = NBLK * nst

        def try_emit_dma():
            while ring['emitted'] < total_blocks:
                n = ring['emitted']
                slot = n % nslot
                if not ring['free'][slot]:
                    return
                ring['free'][slot] = False
                (_, _, _, _, _, off, ncols) = layout[n % NBLK]

                def fn(e, slot=slot, off=off, ncols=ncols):
                    return e.dma_start(out=wr[slot][:, 0:ncols], in_=wst[:, off:off + ncols])

                P.op('pool', fn, reads=[('pro',)], writes=[('ws', slot)], sem=f'ws{slot}')
                ring['emitted'] += 1

        def acquire(tag, a=None):
            n = ring['acquired']
            assert n < ring['emitted'], "weight ring too small for emission order"
            ent = layout[n % NBLK]
            assert ent[2] == tag and (a is None or ent[3] == a), (ent, tag, a)
            ring['acquired'] += 1
            return n % nslot

        def release(slot):
            ring['free'][slot] = True
            try_emit_dma()

        P.op('sp', lambda e: e.dma_start(out=cvt[:, :], in_=cv[:, :]), writes=[('cv',)], sem='ldc')
        P.op('dve', lambda e: e.memset(ones[:, :], 1.0), writes=[('ones',)])
        P.op('dve', lambda e: e.memset(lst[:, :, :], 0.0), writes=[('lst',)])
        for g, win in enumerate(POOL_WIN):
            P.op('dve', lambda e, g=g, win=win: e.memset(invc[:, g, :], 1.0 / win), writes=[('invc',)])
            for t in range(win - 1):
                P.op('dve', lambda e, g=g, t=t: e.memset(invc[:, g, t:t + 1], 1.0 / (t + 1)), writes=[('invc',)])
        for l in range(NLAYER):
            c0 = l * CV_L
            d0 = l * DV_L
            apc = cvt[:, c0 + 104:c0 + 112]
            P.op('dve', lambda e, apc=apc: e.tensor_scalar(out=tmp[:, 0, :], in0=apc, scalar1=-1.0, scalar2=None, op0=ALU.mult),
                 reads=[('cv',)], writes=[('tmp', 0)])
            P.op('dve', lambda e, apc=apc: e.tensor_tensor(out=tmp[:, 1, :], in0=tmp[:, 0, :], in1=apc, op=ALU.max),
                 reads=[('tmp', 0), ('cv',)], writes=[('tmp', 1)])
            P.op('act', lambda e: e.activation(out=tmp[:, 2, :], in_=tmp[:, 1, :], func=AF.Exp, scale=-1.0),
                 reads=[('tmp', 1)], writes=[('tmp', 2)])
            P.op('act', lambda e: e.activation(out=tmp[:, 3, :], in_=tmp[:, 2, :], func=AF.Ln, bias=1.0),
                 reads=[('tmp', 2)], writes=[('tmp', 3)])
            P.op('dve', lambda e: e.tensor_scalar(out=tmp[:, 4, :], in0=tmp[:, 0, :], scalar1=0.0, scalar2=None, op0=ALU.max),
                 reads=[('tmp', 0)], writes=[('tmp', 4)])
            P.op('dve', lambda e: e.tensor_tensor(out=tmp[:, 5, :], in0=tmp[:, 4, :], in1=tmp[:, 3, :], op=ALU.add),
                 reads=[('tmp', 4), ('tmp', 3)], writes=[('tmp', 5)])
            P.op('dve', lambda e, d0=d0: e.tensor_scalar(out=dvt[:, d0:d0 + 8], in0=tmp[:, 5, :], scalar1=-8.0, scalar2=None, op0=ALU.mult),
                 reads=[('tmp', 5)], writes=[('dv',)])
            P.op('dve', lambda e, d0=d0: e.tensor_scalar(out=dvt[:, d0 + 8:d0 + 16], in0=tmp[:, 5, :], scalar1=-4.0, scalar2=None, op0=ALU.mult),
                 reads=[('tmp', 5)], writes=[('dv',)])
            P.op('dve', lambda e, d0=d0, c0=c0: e.tensor_tensor(out=dvt[:, d0 + 16:d0 + 24], in0=cvt[:, c0 + 112:c0 + 120],
                                                                 in1=cvt[:, c0 + 120:c0 + 128], op=ALU.mult),
                 reads=[('cv',)], writes=[('dv',)])
            P.op('dve', lambda e, d0=d0, c0=c0: e.tensor_scalar(out=dvt[:, d0 + 24:d0 + 40], in0=cvt[:, c0 + 88:c0 + 104], scalar1=0.5,
                                                                 scalar2=None, op0=ALU.mult),
                 reads=[('cv',)], writes=[('dv',)])
        P.op('dve', lambda e: e.memset(tmp[:, 0, :], 0.0), reads=[('dv',), ('ones',), ('invc',), ('lst',)], writes=[('pro',), ('tmp', 0)])

        try_emit_dma()

        sq_ctr = [0]
        tb_ctr = [0]

        def hreg(k):
            return [('h', k, s) for s in range(3)]

        def norm(gcol0, mode):
            banks = [next_bank() for _ in SUBS]
            for k in range(KC):
                i = k % 2
                if i == 0:
                    P.op('act', lambda e, k=k, i=i: e.activation(out=sq[i][:, :], in_=hT[:, k, :], func=AF.Square),
                         reads=hreg(k), writes=[('sq', i)])
                else:
                    P.op('dve', lambda e, k=k, i=i: e.tensor_tensor(out=sq[i][:, :], in0=hT[:, k, :], in1=hT[:, k, :], op=ALU.mult),
                         reads=hreg(k), writes=[('sq', i)])

                def mm(e, k=k, i=i):
                    last = None
                    for s, (o, w) in enumerate(SUBS):
                        last = e.matmul(ps[banks[s]][:, 0:w], ones[:, :], sq[i][:, o:o + w],
                                        start=(k == 0), stop=(k == KC - 1))
                    return last

                P.op('pe', mm, reads=[('sq', i), ('ones',)], writes=[('ps', b) for b in banks])
            for s, (o, w) in enumerate(SUBS):
                bk = banks[s]
                P.op('act', lambda e, bk=bk, w=w: e.activation(out=ps[bk][:, 0:w], in_=ps[bk][:, 0:w],
                                                               func=AF.Ln, scale=1.0 / D, bias=EPS),
                     reads=[('ps', bk)], writes=[('ps', bk)])
                P.op('act', lambda e, bk=bk, o=o, w=w: e.activation(out=rstd[:, o:o + w], in_=ps[bk][:, 0:w],
                                                                    func=AF.Exp, scale=-0.5),
                     reads=[('ps', bk)], writes=[('rstd', s)])
                free_bank(bk)
            for k in range(KC):
                if mode == 'xn':
                    P.op('dve', lambda e, k=k: e.scalar_tensor_tensor(
                        out=xn[:, k, HIST:HIST + TT], in0=hT[:, k, :], scalar=cvt[:, gcol0 + k:gcol0 + k + 1],
                        in1=rstd[:, :], op0=ALU.mult, op1=ALU.mult),
                        reads=hreg(k) + [('rstd', s) for s in range(3)] + [('cv',)],
                        writes=[('xn', k, s) for s in range(3)])
                else:
                    P.op('dve', lambda e, k=k: e.scalar_tensor_tensor(
                        out=hT[:, k, :], in0=hT[:, k, :], scalar=cvt[:, gcol0 + k:gcol0 + k + 1],
                        in1=rstd[:, :], op0=ALU.mult, op1=ALU.mult),
                        reads=hreg(k) + [('rstd', s) for s in range(3)] + [('cv',)],
                        writes=hreg(k))

        def xn_regs():
            return [('xn', k, s) for k in range(KC) for s in range(3)]

        def ffn(l, which):
            gcol0 = l * CV_L + (0 if which == 1 else 32)
            norm(gcol0, 'xn')
            for q in range(Q):
                for fi in range(FQ):
                    f = q * FQ + fi
                    bks = []
                    if f == 0:
                        sg, su = acquire('g', f), acquire('u', f)
                        bkg = [next_bank() for _ in SUBS]
                        bku = [next_bank() for _ in SUBS]
                        for k in range(KC):
                            def mmk(e, sg=sg, su=su, bkg=bkg, bku=bku, k=k):
                                last = None
                                for slot, banks in ((sg, bkg), (su, bku)):
                                    for s, (o, w) in enumerate(SUBS):
                                        last = e.matmul(ps[banks[s]][:, 0:w], wr[slot][:, k * 128:(k + 1) * 128],
                                                        xn[:, k, HIST + o:HIST + o + w], start=(k == 0), stop=(k == KC - 1))
                                return last
                            P.op('pe', mmk, reads=[('ws', sg), ('ws', su)] + [('xn', k, s) for s in range(3)],
                                 writes=[('ps', b) for b in bkg + bku])
                        release(sg)
                        release(su)
                        bks = [bkg, bku]
                    else:
                        for tag in ('g', 'u'):
                            slot = acquire(tag, f)
                            banks = [next_bank() for _ in SUBS]

                            def mm(e, slot=slot, banks=banks):
                                last = None
                                for k in range(KC):
                                    for s, (o, w) in enumerate(SUBS):
                                        last = e.matmul(ps[banks[s]][:, 0:w], wr[slot][:, k * 128:(k + 1) * 128],
                                                        xn[:, k, HIST + o:HIST + o + w], start=(k == 0), stop=(k == KC - 1))
                                return last

                            P.op('pe', mm, reads=[('ws', slot)] + xn_regs(), writes=[('ps', b) for b in banks])
                            release(slot)
                            bks.append(banks)
                    for s, (o, w) in enumerate(SUBS):
                        ti = tb_ctr[0] % NTB
                        tb_ctr[0] += 1
                        bg, bu = bks[0][s], bks[1][s]
                        P.op('act', lambda e, ti=ti, bg=bg, w=w: e.activation(out=Tb[ti][:, 0:w], in_=ps[bg][:, 0:w], func=AF.Silu),
                             reads=[('ps', bg)], writes=[('T', ti)])
                        P.op('dve', lambda e, ti=ti, bu=bu, fi=fi, o=o, w=w: e.tensor_tensor(
                            out=act[:, fi, o:o + w], in0=Tb[ti][:, 0:w], in1=ps[bu][:, 0:w], op=ALU.mult),
                            reads=[('T', ti), ('ps', bu)], writes=[('act', fi, s)])
                        free_bank(bg)
                        free_bank(bu)
                for d in range(KC):
                    slot = acquire('o', q)
                    banks = [next_bank() for _ in SUBS]

                    def mm(e, slot=slot, banks=banks):
                        last = None
                        for kk in range(FQ):
                            for s, (o, w) in enumerate(SUBS):
                                last = e.matmul(ps[banks[s]][:, 0:w], wr[slot][:, kk * 128:(kk + 1) * 128],
                                                act[:, kk, o:o + w], start=(kk == 0), stop=(kk == FQ - 1))
                        return last

                    P.op('pe', mm, reads=[('ws', slot)] + [('act', kk, s) for kk in range(FQ) for s in range(3)],
                         writes=[('ps', b) for b in banks])
                    release(slot)
                    for s, (o, w) in enumerate(SUBS):
                        b = banks[s]
                        P.op('dve', lambda e, b=b, d=d, o=o, w=w: e.scalar_tensor_tensor(
                            out=hT[:, d, o:o + w], in0=ps[b][:, 0:w], scalar=0.5, in1=hT[:, d, o:o + w],
                            op0=ALU.mult, op1=ALU.add),
                            reads=[('ps', b), ('h', d, s)], writes=[('h', d, s)])
                        free_bank(b)

        def mixer(l, st):
            c0 = l * CV_L
            d0 = l * DV_L
            if st == 0:
                P.op('dve', lambda e: e.memset(xn[:, :, 0:HIST], 0.0), writes=[('xnh',)])
            else:
                P.op('act', lambda e: e.activation(out=xn[:, :, 0:HIST], in_=xnh[:, l, :, :], func=AF.Identity),
                     reads=[('xnhist', l)], writes=[('xnh',)])
            norm(c0 + 16, 'xn')
            P.op('act', lambda e: e.activation(out=xnh[:, l, :, :], in_=xn[:, :, TT:TT + HIST], func=AF.Identity),
                 reads=[('xn', k, 2) for k in range(KC)], writes=[('xnhist', l)])

            def xin_regs(s, hist):
                r = [('xn', k, s) for k in range(KC)]
                if hist:
                    if s == 0:
                        r.append(('xnh',))
                    else:
                        r += [('xn', k, s - 1) for k in range(KC)]
                return r

            def proj(slot, bank, s, nh, split_k=False):
                o, w = SUBS[s]

                def mm(e, ks=range(KC)):
                    last = None
                    for k in ks:
                        last = e.matmul(ps[bank][:, 0:nh + w], wr[slot][:, k * 128:(k + 1) * 128],
                                        xn[:, k, HIST + o - nh:HIST + o + w], start=(k == 0), stop=(k == KC - 1))
                    return last

                if split_k:
                    for k in range(KC):
                        r = [('ws', slot), ('xn', k, s)]
                        if nh > 0:
                            r.append(('xnh',) if s == 0 else ('xn', k, s - 1))
                        P.op('pe', lambda e, k=k: mm(e, [k]), reads=r, writes=[('ps', bank)])
                else:
                    P.op('pe', mm, reads=[('ws', slot)] + xin_regs(s, nh > 0), writes=[('ps', bank)])

            units = [('lru', j, s) for j in range(NHEAD) for s in range(3)] + \
                    [('pool', g, s) for g in range(4) for s in range(3)]
            nu = len(units)
            blk = {}
            ctx = [dict() for _ in units]
            zgb = {}

            def R_(us, i):
                return ('U', us, i)

            def A_pe(u):
                kind, j, s = units[u]
                if kind == 'lru':
                    if s == 0:
                        blk['zx', j] = acquire('zx', j)
                    bzx = next_bank()
                    ctx[u]['bzx'] = bzx
                    proj(blk['zx', j], bzx, s, 3, split_k=(u == 0))
                    if u == 0:
                        zs = acquire('zg', 0)
                        zgb[0] = [next_bank() for _ in range(3)]
                        for s2 in range(3):
                            proj(zs, zgb[0][s2], s2, 0)
                        release(zs)
                    if j + 1 < NHEAD:
                        if s == 0:
                            blk['zg', j + 1] = acquire('zg', j + 1)
                            zgb[j + 1] = [None] * 3
                        zgb[j + 1][s] = next_bank()
                        proj(blk['zg', j + 1], zgb[j + 1][s], s, 0)
                        if s == 2:
                            release(blk['zg', j + 1])
                    if s == 2:
                        release(blk['zx', j])
                else:
                    g = j
                    if s == 0:
                        blk['zp', 2 * g] = acquire('zp', 2 * g)
                        blk['zp', 2 * g + 1] = acquire('zp', 2 * g + 1)
                    bz = [next_bank(), next_bank()]
                    ctx[u]['bz'] = bz
                    for cc in range(2):
                        proj(blk['zp', 2 * g + cc], bz[cc], s, 15)
                    if s == 2:
                        release(blk['zp', 2 * g])
                        release(blk['zp', 2 * g + 1])

            def A_evac(u):
                kind, j, s = units[u]
                us = u % NSET
                o, w = SUBS[s]
                if kind == 'lru':
                    bzx = ctx[u]['bzx']
                    ZX, XC = U[:, us, 0, :], U[:, us, 1, :]
                    cw3 = cvt[:, c0 + 48 + 3 * 8 + j:c0 + 48 + 3 * 8 + j + 1]
                    cb = cvt[:, c0 + 80 + j:c0 + 81 + j]
                    P.op('dve', lambda e: e.tensor_copy(out=ZX[:, 0:3 + w], in_=ps[bzx][:, 0:3 + w]),
                         reads=[('ps', bzx)], writes=[R_(us, 0)])
                    free_bank(bzx)
                    P.op('act', lambda e: e.activation(out=XC[:, 0:w], in_=ZX[:, 3:3 + w], func=AF.Identity, scale=cw3, bias=cb),
                         reads=[R_(us, 0), ('cv',)], writes=[R_(us, 1)])
                    gh = None
                    if u == 0:
                        gh = 0
                    elif s == 2 and j + 1 < NHEAD:
                        gh = j + 1
                    if gh is not None:
                        bzg = zgb[gh]
                        for s2 in range(3):
                            w2 = SUBS[s2][1]
                            gi = (3 * gh + s2) % NTB
                            P.op('act', lambda e, s2=s2, w2=w2, gi=gi, bzg=bzg: e.activation(
                                out=Tb[gi][:, 0:w2], in_=ps[bzg[s2]][:, 0:w2], func=AF.Gelu_apprx_tanh),
                                reads=[('ps', bzg[s2])], writes=[('T', gi)])
                            free_bank(bzg[s2])
                else:
                    bz = ctx[u]['bz']
                    for cc in range(2):
                        P.op('act', lambda e, cc=cc: e.activation(out=U[:, us, cc, 0:15 + w], in_=ps[bz[cc]][:, 0:15 + w], func=AF.Identity),
                             reads=[('ps', bz[cc])], writes=[R_(us, cc)])
                        free_bank(bz[cc])

            def B1_dve(u):
                kind, j, s = units[u]
                us = u % NSET
                o, w = SUBS[s]
                if kind == 'lru':
                    ZX, XC = U[:, us, 0, :], U[:, us, 1, :]
                    XCB = UB[:, us, 0, :]
                    cw = [cvt[:, c0 + 48 + kk * 8 + j:c0 + 48 + kk * 8 + j + 1] for kk in range(4)]
                    for kk in (2, 1, 0):
                        P.op('dve', lambda e, kk=kk: e.scalar_tensor_tensor(out=XC[:, 0:w], in0=ZX[:, kk:kk + w], scalar=cw[kk],
                                                                            in1=XC[:, 0:w], op0=ALU.mult, op1=ALU.add),
                             reads=[R_(us, 0), R_(us, 1), ('cv',)], writes=[R_(us, 1)])
                    P.op('dve', lambda e: e.tensor_tensor(out=XCB[:, 0:w], in0=XC[:, 0:w], in1=XC[:, 0:w], op=ALU.max),
                         reads=[R_(us, 1)], writes=[('UB', us, 0)])
                else:
                    g = j
                    win = POOL_WIN[g]
                    nlev = g + 1
                    n = 15 + w
                    for cc in range(2):
                        Uc = U[:, us, cc, :]
                        A = U[:, us, 2 + 2 * cc, :]
                        B = U[:, us, 3 + 2 * cc, :]
                        rA, rB = R_(us, 2 + 2 * cc), R_(us, 3 + 2 * cc)
                        src_, rsrc = Uc, R_(us, cc)
                        for lev in range(nlev):
                            sh = 1 << lev
                            lo = (1 << (lev + 1)) - 1
                            dst, rdst = (A, rA) if lev % 2 == 0 else (B, rB)
                            P.op('dve', lambda e, src_=src_, dst=dst, sh=sh, lo=lo: e.tensor_tensor(
                                out=dst[:, lo:n], in0=src_[:, lo:n], in1=src_[:, lo - sh:n - sh], op=ALU.add),
                                reads=[rsrc], writes=[rdst])
                            src_, rsrc = dst, rdst
                        Dd = UB[:, us, cc, :]
                        P.op('dve', lambda e, src_=src_, Uc=Uc, Dd=Dd: e.scalar_tensor_tensor(
                            out=Dd[:, 0:w], in0=src_[:, 15:15 + w], scalar=1.0 / win, in1=Uc[:, 15:15 + w],
                            op0=ALU.mult, op1=ALU.subtract),
                            reads=[rsrc, R_(us, cc)], writes=[('UB', us, cc)])
                        if st == 0 and s == 0:
                            m = win - 1
                            other, rother = (B, rB) if src_ is A else (A, rA)
                            P.op('dve', lambda e, src_=src_, other=other, m=m: e.tensor_tensor(
                                out=other[:, 0:m], in0=src_[:, 15:15 + m], in1=invc[:, g, 0:m], op=ALU.mult),
                                reads=[rsrc, ('invc',)], writes=[rother])
                            P.op('dve', lambda e, other=other, Uc=Uc, Dd=Dd, m=m: e.tensor_tensor(
                                out=Dd[:, 0:m], in0=other[:, 0:m], in1=Uc[:, 15:15 + m], op=ALU.subtract),
                                reads=[rother, R_(us, cc), ('UB', us, cc)], writes=[('UB', us, cc)])

            def B1_pe(u):
                kind, j, s = units[u]
                us = u % NSET
                o, w = SUBS[s]
                if kind == 'lru':
                    XCB = UB[:, us, 0, :]
                    if s == 0:
                        blk['gt', j] = acquire('gt', j)
                    gs = blk['gt', j]
                    br, bi = next_bank(), next_bank()
                    ctx[u]['br'], ctx[u]['bi'] = br, bi

                    def mm(e):
                        e.matmul(ps[br][:, 0:w], wr[gs][:, 0:128], XCB[:, 0:w], start=True, stop=True)
                        return e.matmul(ps[bi][:, 0:w], wr[gs][:, 128:256], XCB[:, 0:w], start=True, stop=True)

                    P.op('pe', mm, reads=[('ws', gs), ('UB', us, 0)], writes=[('ps', br), ('ps', bi)])
                    if s == 2:
                        release(gs)
                else:
                    g = j
                    if s == 0:
                        blk['pw', g] = acquire('pw', g)
                    pslot = blk['pw', g]
                    by = [next_bank(), next_bank()]
                    ctx[u]['by'] = by

                    def mm(e):
                        last = None
                        for jc in range(2):
                            for kc in range(2):
                                last = e.matmul(ps[by[jc]][:, 0:w], wr[pslot][:, kc * 256 + jc * 128:kc * 256 + jc * 128 + 128],
                                                UB[:, us, kc, 0:w], start=(kc == 0), stop=(kc == 1))
                        return last

                    P.op('pe', mm, reads=[('ws', pslot), ('UB', us, 0), ('UB', us, 1)], writes=[('ps', by[0]), ('ps', by[1])])
                    if s == 2:
                        release(pslot)

            def B1_act(u):
                kind, j, s = units[u]
                us = u % NSET
                o, w = SUBS[s]
                if kind == 'lru':
                    Rr, Ii, Mm = U[:, us, 2, :], U[:, us, 3, :], U[:, us, 4, :]
                    br, bi = ctx[u]['br'], ctx[u]['bi']
                    hba = dvt[:, d0 + 24 + j:d0 + 25 + j]
                    hbx = dvt[:, d0 + 32 + j:d0 + 33 + j]
                    cc1 = dvt[:, d0 + j:d0 + j + 1]
                    cch = dvt[:, d0 + 8 + j:d0 + 9 + j]
                    P.op('act', lambda e: e.activation(out=Rr[:, 0:w], in_=ps[br][:, 0:w], func=AF.Tanh, scale=0.5, bias=hba),
                         reads=[('ps', br), ('dv',)], writes=[R_(us, 2)])
                    P.op('act', lambda e: e.activation(out=Ii[:, 0:w], in_=ps[bi][:, 0:w], func=AF.Tanh, scale=0.5, bias=hbx),
                         reads=[('ps', bi), ('dv',)], writes=[R_(us, 3)])
                    free_bank(br)
                    free_bank(bi)
                    P.op('act', lambda e: e.activation(out=Mm[:, 0:w], in_=Rr[:, 0:w], func=AF.Exp, scale=cc1, bias=cc1),
                         reads=[R_(us, 2), ('dv',)], writes=[R_(us, 4)])
                    P.op('act', lambda e: e.activation(out=Rr[:, 0:w], in_=Rr[:, 0:w], func=AF.Exp, scale=cch, bias=cch),
                         reads=[R_(us, 2), ('dv',)], writes=[R_(us, 2)])
                    P.op('act', lambda e: e.activation(out=Mm[:, 0:w], in_=Mm[:, 0:w], func=AF.Sqrt, scale=-0.25, bias=0.25),
                         reads=[R_(us, 4)], writes=[R_(us, 4)])
                else:
                    g = j
                    by = ctx[u]['by']
                    for jc in range(2):
                        c = 2 * g + jc
                        sc = cvt[:, c0 + 120 + c:c0 + 121 + c]
                        bsc = dvt[:, d0 + 16 + c:d0 + 17 + c]
                        P.op('act', lambda e, jc=jc, c=c, sc=sc, bsc=bsc: e.activation(
                            out=act[:, 8 + c, o:o + w], in_=ps[by[jc]][:, 0:w], func=AF.Identity, scale=sc, bias=bsc),
                            reads=[('ps', by[jc]), ('cv',), ('dv',)], writes=[('act', 8 + c, s)])
                        free_bank(by[jc])

            def B2_dve(u):
                kind, j, s = units[u]
                if kind != 'lru':
                    return
                us = u % NSET
                o, w = SUBS[s]
                XC, Rr, Ii, Mm, HS = [U[:, us, i, :] for i in (1, 2, 3, 4, 5)]
                P.op('dve', lambda e: e.scalar_tensor_tensor(out=Mm[:, 0:w], in0=Ii[:, 0:w], scalar=1.0, in1=Mm[:, 0:w],
                                                             op0=ALU.add, op1=ALU.mult),
                     reads=[R_(us, 3), R_(us, 4)], writes=[R_(us, 4)])
                P.op('dve', lambda e: e.tensor_tensor(out=Mm[:, 0:w], in0=Mm[:, 0:w], in1=XC[:, 0:w], op=ALU.mult),
                     reads=[R_(us, 4), R_(us, 1)], writes=[R_(us, 4)])
                if s == 0:
                    init = lst[:, l, j:j + 1]
                    rinit = ('lst',)
                else:
                    pus = (u - 1) % NSET
                    pw_ = SUBS[s - 1][1]
                    init = U[:, pus, 5, pw_ - 1:pw_]
                    rinit = R_(pus, 5)
                P.op('dve', lambda e: e.tensor_tensor_scan(out=HS[:, 0:w], data0=Rr[:, 0:w], data1=Mm[:, 0:w], initial=init,
                                                           op0=ALU.mult, op1=ALU.add),
                     reads=[R_(us, 2), R_(us, 4), rinit], writes=[R_(us, 5)])
                if s == 2:
                    P.op('act', lambda e: e.activation(out=lst[:, l, j:j + 1], in_=HS[:, w - 1:w], func=AF.Identity),
                         reads=[R_(us, 5)], writes=[('lst',)])
                gi = u % NTB
                P.op('dve', lambda e: e.tensor_tensor(out=act[:, j, o:o + w], in0=HS[:, 0:w], in1=Tb[gi][:, 0:w], op=ALU.mult),
                     reads=[R_(us, 5), ('T', gi)], writes=[('act', j, s)])

            for t in range(nu + 2):
                if t < nu:
                    A_pe(t)
                if 0 <= t - 1 < nu:
                    B1_dve(t - 1)
                    B1_pe(t - 1)
                if t < nu:
                    A_evac(t)
                if 0 <= t - 1 < nu:
                    B1_act(t - 1)
                if 0 <= t - 2 < nu:
                    B2_dve(t - 2)

            for d in range(KC):
                slot = acquire('wo', d)
                banks = [next_bank() for _ in SUBS]

                def mm(e, slot=slot, banks=banks):
                    last = None
                    for k in range(KC):
                        for s, (o, w) in enumerate(SUBS):
                            last = e.matmul(ps[banks[s]][:, 0:w], wr[slot][:, k * 128:(k + 1) * 128],
                                            act[:, k, o:o + w], start=(k == 0), stop=(k == KC - 1))
                    return last

                P.op('pe', mm, reads=[('ws', slot)] + [('act', k, s) for k in range(KC) for s in range(3)],
                     writes=[('ps', b) for b in banks])
                release(slot)
                for s, (o, w) in enumerate(SUBS):
                    b = banks[s]
                    P.op('dve', lambda e, b=b, d=d, o=o, w=w: e.tensor_tensor(
                        out=hT[:, d, o:o + w], in0=ps[b][:, 0:w], in1=hT[:, d, o:o + w], op=ALU.add),
                        reads=[('ps', b), ('h', d, s)], writes=[('h', d, s)])
                    free_bank(b)

        allh = [('h', k, s) for k in range(KC) for s in range(3)]
        for st in range(nst):
            def ld(e, st=st):
                last = None
                for k in range(KC):
                    if last is not None:
                        last.then_inc(sems['ldh'], 16)
                    last = e.dma_start(out=hT[:, k, :], in_=xin[:, k, st * TT:(st + 1) * TT])
                return last

            P.op('sp', ld, writes=allh, sem='ldh', pre_inc=16 * (KC - 1))
            for (l, kind) in phases:
                if kind == 'mix':
                    mixer(l, st)
                else:
                    ffn(l, 1 if kind == 'ffn1' else 2)
            norm(NLAYER * CV_L, 'final')
            c_lo = NMETA if st == 0 else 0
            t_lo = st * TT + c_lo - NMETA
            ncol = TT - c_lo

            def stf(e, c_lo=c_lo, t_lo=t_lo, ncol=ncol):
                last = None
                for k in range(KC):
                    if last is not None:
                        last.then_inc(sems['sth'], 16)
                    last = e.dma_start(out=y[:, k, t_lo:t_lo + ncol], in_=hT[:, k, c_lo:c_lo + ncol])
                return last

            P.op('sp', stf, reads=allh, sem='sth', pre_inc=16 * (KC - 1))
        P.final_waits['sp'].append(('sth', P.cnt['sth']))
        assert ring['acquired'] == total_blocks and ring['emitted'] == total_blocks

        sems = {}
        for key in P.cnt:
            sems[key] = es.enter_context(nc.semaphore(f"s_{key}"))
        block = es.enter_context(nc.Block())
        P.emit(block, sems)
    return nc


def _kmaj(w):
    K = w.shape[0] // 128
    return np.ascontiguousarray(w.reshape(K, 128, w.shape[1]).transpose(1, 0, 2)).reshape(128, K * w.shape[1])


def build_wstream(inp, phases):
    layout, WTOT = stream_layout(phases)
    wst = np.empty((128, WTOT), dtype=np.float32)
    for (l, kind, tag, a, b, off, n) in layout:
        if kind in ('ffn1', 'ffn2'):
            w_in = inp['ffn1_w_in' if kind == 'ffn1' else 'ffn2_w_in'][l]
            w_out = inp['ffn1_w_out' if kind == 'ffn1' else 'ffn2_w_out'][l]
            if tag == 'g':
                blkv = _kmaj(w_in[:, a * 128:(a + 1) * 128])
            elif tag == 'u':
                blkv = _kmaj(w_in[:, DFF + a * 128:DFF + (a + 1) * 128])
            else:
                q, d = a, b
                blkv = _kmaj(w_out[q * FQ * 128:(q + 1) * FQ * 128, d * 128:(d + 1) * 128])
        else:
            w_in = inp['w_in'][l]
            if tag == 'zx':
                blkv = _kmaj(w_in[:, a * 128:(a + 1) * 128])
            elif tag == 'zg':
                blkv = _kmaj(w_in[:, 1024 + a * 128:1024 + (a + 1) * 128])
            elif tag == 'zp':
                blkv = _kmaj(w_in[:, 2048 + a * 128:2048 + (a + 1) * 128])
            elif tag == 'gt':
                blkv = np.concatenate([inp['lru_wa'][l, a], inp['lru_wx'][l, a]], axis=1)
            elif tag == 'pw':
                blkv = _kmaj(inp['pool_w'][l, a])
            else:
                blkv = _kmaj(inp['w_out'][l][:, a * 128:(a + 1) * 128])
        assert blkv.shape == (128, n), (blkv.shape, n, tag)
        wst[:, off:off + n] = blkv
    return wst


def build_cvec(inp):
    cvv = np.zeros((128, NCV), dtype=np.float32)

    def col(v):
        return np.asarray(v, dtype=np.float32).reshape(-1, 128).T

    for l in range(NLAYER):
        c0 = l * CV_L
        cvv[:, c0 + 0:c0 + 16] = col(inp['ffn1_norm'][l])
        cvv[:, c0 + 16:c0 + 32] = col(inp['mix_norm'][l])
        cvv[:, c0 + 32:c0 + 48] = col(inp['ffn2_norm'][l])
        for kk in range(4):
            cvv[:, c0 + 48 + kk * 8:c0 + 56 + kk * 8] = col(inp['conv_w'][l, kk])
        cvv[:, c0 + 80:c0 + 88] = col(inp['conv_b'][l])
        cvv[:, c0 + 88:c0 + 96] = col(inp['lru_ba'][l])
        cvv[:, c0 + 96:c0 + 104] = col(inp['lru_bx'][l])
        cvv[:, c0 + 104:c0 + 112] = col(inp['lru_a_param'][l])
        cvv[:, c0 + 112:c0 + 120] = col(inp['pool_b'][l])
        cvv[:, c0 + 120:c0 + 128] = col(inp['pool_scale'][l])
    cvv[:, NLAYER * CV_L:NLAYER * CV_L + KC] = col(inp['final_norm'])
    return cvv


def build_xin(x_b, meta):
    hfull = np.concatenate([meta, x_b], axis=0)
    return np.ascontiguousarray(hfull.T.reshape(KC, 128, TTOT).transpose(1, 0, 2))


_CACHE = {}


def kernel(**inputs):
    inp = {k: np.asarray(v) for k, v in inputs.items()}
    x = inp['x'].astype(np.float32, copy=False)
    B = x.shape[0]
    phases = phases_all()
    if 'nc' not in _CACHE:
        _CACHE['nc'] = build_program(NST, phases, NSLOT)
    nc = _CACHE['nc']
    wst = build_wstream(inp, phases)
    cvv = build_cvec(inp)
    meta = inp['meta_tokens'].astype(np.float32, copy=False)
    in_maps = [{"xin": build_xin(x[b], meta), "wst": wst, "cv": cvv} for b in range(B)]
    res = run_bass_kernel_spmd(nc, in_maps, core_ids=list(range(B)))
    out = np.empty((B, SEQ, D), dtype=np.float32)
    for b in range(B):
        yb = np.asarray(res.results[b]["y"])
        out[b] = yb.transpose(2, 1, 0).reshape(SEQ, D)
    return out
```

```python
import numpy as np
from contextlib import ExitStack

import concourse.bass as bass
import concourse.mybir as mybir
from concourse.bass_utils import run_bass_kernel_spmd

F32 = mybir.dt.float32
BF16 = mybir.dt.bfloat16
AF = mybir.ActivationFunctionType
ALU = mybir.AluOpType

D = 2048
KC = 16
SEQ = 4096
NMETA = 16
TTOT = SEQ + NMETA
DFF = 5632
FC = DFF // 128
Q = 4
FQ = FC // Q
NST = 4
TT = TTOT // NST
SUBS = [(0, 343), (343, 343), (686, 342)]
HIST = 16
NSLOT = 6
EPS = 1e-6
NHEAD = 8
POOL_WIN = (2, 4, 8, 16)
NLAYER = 2
UW = 360

CV_L = 128
NCV = CV_L * NLAYER + KC
DV_L = 40


def phases_all():
    ph = []
    for l in range(NLAYER):
        ph += [(l, 'ffn1'), (l, 'mix'), (l, 'ffn2')]
    return ph


def phase_blocks(kind):
    bl = []
    if kind in ('ffn1', 'ffn2'):
        for q in range(Q):
            for fi in range(FQ):
                f = q * FQ + fi
                bl.append(('g', f, 0, KC * 128))
                bl.append(('u', f, 0, KC * 128))
            for d in range(KC):
                bl.append(('o', q, d, FQ * 128))
    else:
        for j in range(NHEAD):
            bl.append(('zx', j, 0, KC * 128))
            if j == 0:
                bl.append(('zg', 0, 0, KC * 128))
            if j + 1 < NHEAD:
                bl.append(('zg', j + 1, 0, KC * 128))
            bl.append(('gt', j, 0, 256))
        for g in range(4):
            bl.append(('zp', 2 * g, 0, KC * 128))
            bl.append(('zp', 2 * g + 1, 0, KC * 128))
            bl.append(('pw', g, 0, 512))
        for d in range(KC):
            bl.append(('wo', d, 0, KC * 128))
    return bl


def stream_layout(phases):
    out = []
    off = 0
    for (l, kind) in phases:
        for (tag, a, b, n) in phase_blocks(kind):
            out.append((l, kind, tag, a, b, off, n))
            off += n
    return out, off


class Prog:
    ENGS = ('pe', 'act', 'dve', 'pool', 'sp')

    def __init__(self):
        self.q = {e: [] for e in self.ENGS}
        self.cnt = {}
        self.waited = {e: {} for e in self.ENGS}
        self.lastw = {}
        self.readers = {}
        self.final_waits = {e: [] for e in self.ENGS}

    def op(self, eng, fn, reads=(), writes=(), sem=None, pre_inc=0):
        key = sem if sem is not None else eng
        inc = 16 if sem is not None else 1
        deps = {}

        def add(tok):
            if tok is None:
                return
            k, v = tok
            if deps.get(k, 0) < v:
                deps[k] = v

        for r in reads:
            add(self.lastw.get(r))
        for w in writes:
            add(self.lastw.get(w))
            rd = self.readers.get(w)
            if rd:
                for tok in rd.values():
                    add(tok)
        waits = []
        for k, v in deps.items():
            if k == 'pe' and eng == 'pe':
                continue
            if self.waited[eng].get(k, 0) >= v:
                continue
            self.waited[eng][k] = v
            waits.append((k, v))
        self.cnt[key] = self.cnt.get(key, 0) + pre_inc + inc
        tok = (key, self.cnt[key])
        self.q[eng].append((waits, fn, key, inc))
        for r in reads:
            self.readers.setdefault(r, {})[key] = tok
        for w in writes:
            self.lastw[w] = tok
            self.readers[w] = {}
        return tok

    def emit(self, block, sems):
        names = {'pe': 'tensor', 'act': 'scalar', 'dve': 'vector', 'pool': 'gpsimd', 'sp': 'sync'}
        for eng in self.ENGS:
            ops = self.q[eng]
            fw = self.final_waits[eng]

            def body(e, ops=ops, fw=fw):
                for waits, fn, key, inc in ops:
                    for k, v in waits:
                        e.wait_ge(sems[k], v)
                    inst = fn(e)
                    inst.then_inc(sems[key], inc)
                for k, v in fw:
                    e.wait_ge(sems[k], v)

            getattr(block, names[eng])(body)


def build_program(nst=NST, phases=None, nslot=NSLOT):
    if phases is None:
        phases = phases_all()
    layout, WTOT = stream_layout(phases)
    NBLK = len(layout)

    nc = bass.Bass("TRN2", target_bir_lowering=False)
    xin = nc.dram_tensor("xin", [128, KC, TTOT], F32, kind="ExternalInput").ap()
    wst = nc.dram_tensor("wst", [128, WTOT], F32, kind="ExternalInput").ap()
    cv = nc.dram_tensor("cv", [128, NCV], F32, kind="ExternalInput").ap()
    y = nc.dram_tensor("y", [128, KC, SEQ], F32, kind="ExternalOutput").ap()

    P = Prog()
    es = ExitStack()
    with es:
        def sb(name, shape, dt):
            return es.enter_context(nc.sbuf_tensor(name, shape, dt))

        hT = sb("hT", [128, KC, TT], F32)
        xn = sb("xn", [128, KC, HIST + TT], BF16)
        act = sb("act", [128, KC, TT], BF16)
        wr = [sb(f"wr{i}", [128, KC * 128], BF16) for i in range(nslot)]
        sq = [sb(f"sq{i}", [128, TT], BF16) for i in range(2)]
        rstd = sb("rstd", [128, TT], F32)
        NTB = 6
        Tb = [sb(f"Tb{i}", [128, 344], F32) for i in range(NTB)]
        NSET = 3
        U = sb("U", [128, NSET, 6, UW], F32)
        UB = sb("UB", [128, NSET, 2, UW], BF16)
        cvt = sb("cvt", [128, NCV], F32)
        dvt = sb("dvt", [128, NLAYER * DV_L], F32)
        ones = sb("ones", [128, 128], BF16)
        invc = sb("invc", [128, 4, 16], F32)
        xnh = sb("xnh", [128, NLAYER, KC, HIST], BF16)
        lst = sb("lst", [128, NLAYER, NHEAD], F32)
        tmp = sb("tmp", [128, 6, 8], F32)
        ps = [es.enter_context(nc.psum_tensor(f"ps{i}", [128, 512], F32)) for i in range(8)]

        bank_free = list(range(8))

        def next_bank():
            assert bank_free, "out of PSUM banks"
            return bank_free.pop(0)

        def free_bank(b):
            assert b not in bank_free
            bank_free.append(b)

        ring = {'emitted': 0, 'acquired': 0, 'free': [True] * nslot}
        total_blocks = NBLK * nst

        def try_emit_dma():
            while ring['emitted'] < total_blocks:
                n = ring['emitted']
                slot = n % nslot
                if not ring['free'][slot]:
                    return
                ring['free'][slot] = False
                (_, _, _, _, _, off, ncols) = layout[n % NBLK]

                def fn(e, slot=slot, off=off, ncols=ncols):
                    return e.dma_start(out=wr[slot][:, 0:ncols], in_=wst[:, off:off + ncols])

                P.op('pool', fn, reads=[('pro',)], writes=[('ws', slot)], sem=f'ws{slot}')
                ring['emitted'] += 1

        def acquire(tag, a=None):
            n = ring['acquired']
            assert n < ring['emitted'], "weight ring too small for emission order"
            ent = layout[n % NBLK]
            assert ent[2] == tag and (a is None or ent[3] == a), (ent, tag, a)
            ring['acquired'] += 1
            return n % nslot

        def release(slot):
            ring['free'][slot] = True
            try_emit_dma()

        P.op('sp', lambda e: e.dma_start(out=cvt[:, :], in_=cv[:, :]), writes=[('cv',)], sem='ldc')
        P.op('dve', lambda e: e.memset(ones[:, :], 1.0), writes=[('ones',)])
        P.op('dve', lambda e: e.memset(lst[:, :, :], 0.0), writes=[('lst',)])
        for g, win in enumerate(POOL_WIN):
            P.op('dve', lambda e, g=g, win=win: e.memset(invc[:, g, :], 1.0 / win), writes=[('invc',)])
            for t in range(win - 1):
                P.op('dve', lambda e, g=g, t=t: e.memset(invc[:, g, t:t + 1], 1.0 / (t + 1)), writes=[('invc',)])
        for l in range(NLAYER):
            c0 = l * CV_L
            d0 = l * DV_L
            apc = cvt[:, c0 + 104:c0 + 112]
            P.op('dve', lambda e, apc=apc: e.tensor_scalar(out=tmp[:, 0, :], in0=apc, scalar1=-1.0, scalar2=None, op0=ALU.mult),
                 reads=[('cv',)], writes=[('tmp', 0)])
            P.op('dve', lambda e, apc=apc: e.tensor_tensor(out=tmp[:, 1, :], in0=tmp[:, 0, :], in1=apc, op=ALU.max),
                 reads=[('tmp', 0), ('cv',)], writes=[('tmp', 1)])
            P.op('act', lambda e: e.activation(out=tmp[:, 2, :], in_=tmp[:, 1, :], func=AF.Exp, scale=-1.0),
                 reads=[('tmp', 1)], writes=[('tmp', 2)])
            P.op('act', lambda e: e.activation(out=tmp[:, 3, :], in_=tmp[:, 2, :], func=AF.Ln, bias=1.0),
                 reads=[('tmp', 2)], writes=[('tmp', 3)])
            P.op('dve', lambda e: e.tensor_scalar(out=tmp[:, 4, :], in0=tmp[:, 0, :], scalar1=0.0, scalar2=None, op0=ALU.max),
                 reads=[('tmp', 0)], writes=[('tmp', 4)])
            P.op('dve', lambda e: e.tensor_tensor(out=tmp[:, 5, :], in0=tmp[:, 4, :], in1=tmp[:, 3, :], op=ALU.add),
                 reads=[('tmp', 4), ('tmp', 3)], writes=[('tmp', 5)])
            P.op('dve', lambda e, d0=d0: e.tensor_scalar(out=dvt[:, d0:d0 + 8], in0=tmp[:, 5, :], scalar1=-8.0, scalar2=None, op0=ALU.mult),
                 reads=[('tmp', 5)], writes=[('dv',)])
            P.op('dve', lambda e, d0=d0: e.tensor_scalar(out=dvt[:, d0 + 8:d0 + 16], in0=tmp[:, 5, :], scalar1=-4.0, scalar2=None, op0=ALU.mult),
                 reads=[('tmp', 5)], writes=[('dv',)])
            P.op('dve', lambda e, d0=d0, c0=c0: e.tensor_tensor(out=dvt[:, d0 + 16:d0 + 24], in0=cvt[:, c0 + 112:c0 + 120],
                                                                 in1=cvt[:, c0 + 120:c0 + 128], op=ALU.mult),
                 reads=[('cv',)], writes=[('dv',)])
            P.op('dve', lambda e, d0=d0, c0=c0: e.tensor_scalar(out=dvt[:, d0 + 24:d0 + 40], in0=cvt[:, c0 + 88:c0 + 104], scalar1=0.5,
                                                                 scalar2=None, op0=ALU.mult),
                 reads=[('cv',)], writes=[('dv',)])
        P.op('dve', lambda e: e.memset(tmp[:, 0, :], 0.0), reads=[('dv',), ('ones',), ('invc',), ('lst',)], writes=[('pro',), ('tmp', 0)])

        try_emit_dma()

        sq_ctr = [0]
        tb_ctr = [0]

        def hreg(k):
            return [('h', k, s) for s in range(3)]

        def norm(gcol0, mode):
            banks = [next_bank() for _ in SUBS]
            for k in range(KC):
                i = k % 2
                if i == 0:
                    P.op('act', lambda e, k=k, i=i: e.activation(out=sq[i][:, :], in_=hT[:, k, :], func=AF.Square),
                         reads=hreg(k), writes=[('sq', i)])
                else:
                    P.op('dve', lambda e, k=k, i=i: e.tensor_tensor(out=sq[i][:, :], in0=hT[:, k, :], in1=hT[:, k, :], op=ALU.mult),
                         reads=hreg(k), writes=[('sq', i)])

                def mm(e, k=k, i=i):
                    last = None
                    for s, (o, w) in enumerate(SUBS):
                        last = e.matmul(ps[banks[s]][:, 0:w], ones[:, :], sq[i][:, o:o + w],
                                        start=(k == 0), stop=(k == KC - 1))
                    return last

                P.op('pe', mm, reads=[('sq', i), ('ones',)], writes=[('ps', b) for b in banks])
            for s, (o, w) in enumerate(SUBS):
                bk = banks[s]
                P.op('act', lambda e, bk=bk, w=w: e.activation(out=ps[bk][:, 0:w], in_=ps[bk][:, 0:w],
                                                               func=AF.Ln, scale=1.0 / D, bias=EPS),
                     reads=[('ps', bk)], writes=[('ps', bk)])
                P.op('act', lambda e, bk=bk, o=o, w=w: e.activation(out=rstd[:, o:o + w], in_=ps[bk][:, 0:w],
                                                                    func=AF.Exp, scale=-0.5),
                     reads=[('ps', bk)], writes=[('rstd', s)])
                free_bank(bk)
            for k in range(KC):
                if mode == 'xn':
                    P.op('dve', lambda e, k=k: e.scalar_tensor_tensor(
                        out=xn[:, k, HIST:HIST + TT], in0=hT[:, k, :], scalar=cvt[:, gcol0 + k:gcol0 + k + 1],
                        in1=rstd[:, :], op0=ALU.mult, op1=ALU.mult),
                        reads=hreg(k) + [('rstd', s) for s in range(3)] + [('cv',)],
                        writes=[('xn', k, s) for s in range(3)])
                else:
                    P.op('dve', lambda e, k=k: e.scalar_tensor_tensor(
                        out=hT[:, k, :], in0=hT[:, k, :], scalar=cvt[:, gcol0 + k:gcol0 + k + 1],
                        in1=rstd[:, :], op0=ALU.mult, op1=ALU.mult),
                        reads=hreg(k) + [('rstd', s) for s in range(3)] + [('cv',)],
                        writes=hreg(k))

        def xn_regs():
            return [('xn', k, s) for k in range(KC) for s in range(3)]

        def ffn(l, which):
            gcol0 = l * CV_L + (0 if which == 1 else 32)
            norm(gcol0, 'xn')
            for q in range(Q):
                for fi in range(FQ):
                    f = q * FQ + fi
                    bks = []
                    if f == 0:
                        sg, su = acquire('g', f), acquire('u', f)
                        bkg = [next_bank() for _ in SUBS]
                        bku = [next_bank() for _ in SUBS]
                        for k in range(KC):
                            def mmk(e, sg=sg, su=su, bkg=bkg, bku=bku, k=k):
                                last = None
                                for slot, banks in ((sg, bkg), (su, bku)):
                                    for s, (o, w) in enumerate(SUBS):
                                        last = e.matmul(ps[banks[s]][:, 0:w], wr[slot][:, k * 128:(k + 1) * 128],
                                                        xn[:, k, HIST + o:HIST + o + w], start=(k == 0), stop=(k == KC - 1))
                                return last
                            P.op('pe', mmk, reads=[('ws', sg), ('ws', su)] + [('xn', k, s) for s in range(3)],
                                 writes=[('ps', b) for b in bkg + bku])
                        release(sg)
                        release(su)
                        bks = [bkg, bku]
                    else:
                        for tag in ('g', 'u'):
                            slot = acquire(tag, f)
                            banks = [next_bank() for _ in SUBS]

                            def mm(e, slot=slot, banks=banks):
                                last = None
                                for k in range(KC):
                                    for s, (o, w) in enumerate(SUBS):
                                        last = e.matmul(ps[banks[s]][:, 0:w], wr[slot][:, k * 128:(k + 1) * 128],
                                                        xn[:, k, HIST + o:HIST + o + w], start=(k == 0), stop=(k == KC - 1))
                                return last

                            P.op('pe', mm, reads=[('ws', slot)] + xn_regs(), writes=[('ps', b) for b in banks])
                            release(slot)
                            bks.append(banks)
                    for s, (o, w) in enumerate(SUBS):
                        ti = tb_ctr[0] % NTB
                        tb_ctr[0] += 1
                        bg, bu = bks[0][s], bks[1][s]
                        P.op('act', lambda e, ti=ti, bg=bg, w=w: e.activation(out=Tb[ti][:, 0:w], in_=ps[bg][:, 0:w], func=AF.Silu),
                             reads=[('ps', bg)], writes=[('T', ti)])
                        P.op('dve', lambda e, ti=ti, bu=bu, fi=fi, o=o, w=w: e.tensor_tensor(
                            out=act[:, fi, o:o + w], in0=Tb[ti][:, 0:w], in1=ps[bu][:, 0:w], op=ALU.mult),
                            reads=[('T', ti), ('ps', bu)], writes=[('act', fi, s)])
                        free_bank(bg)
                        free_bank(bu)
                for d in range(KC):
                    slot = acquire('o', q)
                    banks = [next_bank() for _ in SUBS]

                    def mm(e, slot=slot, banks=banks):
                        last = None
                        for kk in range(FQ):
                            for s, (o, w) in enumerate(SUBS):
                                last = e.matmul(ps[banks[s]][:, 0:w], wr[slot][:, kk * 128:(kk + 1) * 128],
                                                act[:, kk, o:o + w], start=(kk == 0), stop=(kk == FQ - 1))
                        return last

                    P.op('pe', mm, reads=[('ws', slot)] + [('act', kk, s) for kk in range(FQ) for s in range(3)],
                         writes=[('ps', b) for b in banks])
                    release(slot)
                    for s, (o, w) in enumerate(SUBS):
                        b = banks[s]
                        P.op('dve', lambda e, b=b, d=d, o=o, w=w: e.scalar_tensor_tensor(
                            out=hT[:, d, o:o + w], in0=ps[b][:, 0:w], scalar=0.5, in1=hT[:, d, o:o + w],
                            op0=ALU.mult, op1=ALU.add),
                            reads=[('ps', b), ('h', d, s)], writes=[('h', d, s)])
                        free_bank(b)

        def mixer(l, st):
            c0 = l * CV_L
            d0 = l * DV_L
            if st == 0:
                P.op('dve', lambda e: e.memset(xn[:, :, 0:HIST], 0.0), writes=[('xnh',)])
            else:
                P.op('act', lambda e: e.activation(out=xn[:, :, 0:HIST], in_=xnh[:, l, :, :], func=AF.Identity),
                     reads=[('xnhist', l)], writes=[('xnh',)])
            norm(c0 + 16, 'xn')
            P.op('act', lambda e: e.activation(out=xnh[:, l, :, :], in_=xn[:, :, TT:TT + HIST], func=AF.Identity),
                 reads=[('xn', k, 2) for k in range(KC)], writes=[('xnhist', l)])

            def xin_regs(s, hist):
                r = [('xn', k, s) for k in range(KC)]
                if hist:
                    if s == 0:
                        r.append(('xnh',))
                    else:
                        r += [('xn', k, s - 1) for k in range(KC)]
                return r

            def proj(slot, bank, s, nh, split_k=False):
                o, w = SUBS[s]

                def mm(e, ks=range(KC)):
                    last = None
                    for k in ks:
                        last = e.matmul(ps[bank][:, 0:nh + w], wr[slot][:, k * 128:(k + 1) * 128],
                                        xn[:, k, HIST + o - nh:HIST + o + w], start=(k == 0), stop=(k == KC - 1))
                    return last

                if split_k:
                    for k in range(KC):
                        r = [('ws', slot), ('xn', k, s)]
                        if nh > 0:
                            r.append(('xnh',) if s == 0 else ('xn', k, s - 1))
                        P.op('pe', lambda e, k=k: mm(e, [k]), reads=r, writes=[('ps', bank)])
                else:
                    P.op('pe', mm, reads=[('ws', slot)] + xin_regs(s, nh > 0), writes=[('ps', bank)])

            units = [('lru', j, s) for j in range(NHEAD) for s in range(3)] + \
                    [('pool', g, s) for g in range(4) for s in range(3)]
            nu = len(units)
            blk = {}
            ctx = [dict() for _ in units]
            zgb = {}

            def R_(us, i):
                return ('U', us, i)

            def A_pe(u):
                kind, j, s = units[u]
                if kind == 'lru':
                    if s == 0:
                        blk['zx', j] = acquire('zx', j)
                    bzx = next_bank()
                    ctx[u]['bzx'] = bzx
                    proj(blk['zx', j], bzx, s, 3, split_k=(u == 0))
                    if u == 0:
                        zs = acquire('zg', 0)
                        zgb[0] = [next_bank() for _ in range(3)]
                        for s2 in range(3):
                            proj(zs, zgb[0][s2], s2, 0)
                        release(zs)
                    if j + 1 < NHEAD:
                        if s == 0:
                            blk['zg', j + 1] = acquire('zg', j + 1)
                            zgb[j + 1] = [None] * 3
                        zgb[j + 1][s] = next_bank()
                        proj(blk['zg', j + 1], zgb[j + 1][s], s, 0)
                        if s == 2:
                            release(blk['zg', j + 1])
                    if s == 2:
                        release(blk['zx', j])
                else:
                    g = j
                    if s == 0:
                        blk['zp', 2 * g] = acquire('zp', 2 * g)
                        blk['zp', 2 * g + 1] = acquire('zp', 2 * g + 1)
                    bz = [next_bank(), next_bank()]
                    ctx[u]['bz'] = bz
                    for cc in range(2):
                        proj(blk['zp', 2 * g + cc], bz[cc], s, 15)
                    if s == 2:
                        release(blk['zp', 2 * g])
                        release(blk['zp', 2 * g + 1])

            def A_evac(u):
                kind, j, s = units[u]
                us = u % NSET
                o, w = SUBS[s]
                if kind == 'lru':
                    bzx = ctx[u]['bzx']
                    ZX, XC = U[:, us, 0, :], U[:, us, 1, :]
                    cw3 = cvt[:, c0 + 48 + 3 * 8 + j:c0 + 48 + 3 * 8 + j + 1]
                    cb = cvt[:, c0 + 80 + j:c0 + 81 + j]
                    P.op('dve', lambda e: e.tensor_copy(out=ZX[:, 0:3 + w], in_=ps[bzx][:, 0:3 + w]),
                         reads=[('ps', bzx)], writes=[R_(us, 0)])
                    free_bank(bzx)
                    P.op('act', lambda e: e.activation(out=XC[:, 0:w], in_=ZX[:, 3:3 + w], func=AF.Identity, scale=cw3, bias=cb),
                         reads=[R_(us, 0), ('cv',)], writes=[R_(us, 1)])
                    gh = None
                    if u == 0:
                        gh = 0
                    elif s == 2 and j + 1 < NHEAD:
                        gh = j + 1
                    if gh is not None:
                        bzg = zgb[gh]
                        for s2 in range(3):
                            w2 = SUBS[s2][1]
                            gi = (3 * gh + s2) % NTB
                            P.op('act', lambda e, s2=s2, w2=w2, gi=gi, bzg=bzg: e.activation(
                                out=Tb[gi][:, 0:w2], in_=ps[bzg[s2]][:, 0:w2], func=AF.Gelu_apprx_tanh),
                                reads=[('ps', bzg[s2])], writes=[('T', gi)])
                            free_bank(bzg[s2])
                else:
                    bz = ctx[u]['bz']
                    for cc in range(2):
                        P.op('act', lambda e, cc=cc: e.activation(out=U[:, us, cc, 0:15 + w], in_=ps[bz[cc]][:, 0:15 + w], func=AF.Identity),
                             reads=[('ps', bz[cc])], writes=[R_(us, cc)])
                        free_bank(bz[cc])

            def B1_dve(u):
                kind, j, s = units[u]
                us = u % NSET
                o, w = SUBS[s]
                if kind == 'lru':
                    ZX, XC = U[:, us, 0, :], U[:, us, 1, :]
                    XCB = UB[:, us, 0, :]
                    cw = [cvt[:, c0 + 48 + kk * 8 + j:c0 + 48 + kk * 8 + j + 1] for kk in range(4)]
                    for kk in (2, 1, 0):
                        P.op('dve', lambda e, kk=kk: e.scalar_tensor_tensor(out=XC[:, 0:w], in0=ZX[:, kk:kk + w], scalar=cw[kk],
                                                                            in1=XC[:, 0:w], op0=ALU.mult, op1=ALU.add),
                             reads=[R_(us, 0), R_(us, 1), ('cv',)], writes=[R_(us, 1)])
                    P.op('dve', lambda e: e.tensor_tensor(out=XCB[:, 0:w], in0=XC[:, 0:w], in1=XC[:, 0:w], op=ALU.max),
                         reads=[R_(us, 1)], writes=[('UB', us, 0)])
                else:
                    g = j
                    win = POOL_WIN[g]
                    nlev = g + 1
                    n = 15 + w
                    for cc in range(2):
                        Uc = U[:, us, cc, :]
                        A = U[:, us, 2 + 2 * cc, :]
                        B = U[:, us, 3 + 2 * cc, :]
                        rA, rB = R_(us, 2 + 2 * cc), R_(us, 3 + 2 * cc)
                        src_, rsrc = Uc, R_(us, cc)
                        for lev in range(nlev):
                            sh = 1 << lev
                            lo = (1 << (lev + 1)) - 1
                            dst, rdst = (A, rA) if lev % 2 == 0 else (B, rB)
                            P.op('dve', lambda e, src_=src_, dst=dst, sh=sh, lo=lo: e.tensor_tensor(
                                out=dst[:, lo:n], in0=src_[:, lo:n], in1=src_[:, lo - sh:n - sh], op=ALU.add),
                                reads=[rsrc], writes=[rdst])
                            src_, rsrc = dst, rdst
                        Dd = UB[:, us, cc, :]
                        P.op('dve', lambda e, src_=src_, Uc=Uc, Dd=Dd: e.scalar_tensor_tensor(
                            out=Dd[:, 0:w], in0=src_[:, 15:15 + w], scalar=1.0 / win, in1=Uc[:, 15:15 + w],
                            op0=ALU.mult, op1=ALU.subtract),
                            reads=[rsrc, R_(us, cc)], writes=[('UB', us, cc)])
                        if st == 0 and s == 0:
                            m = win - 1
                            other, rother = (B, rB) if src_ is A else (A, rA)
                            P.op('dve', lambda e, src_=src_, other=other, m=m: e.tensor_tensor(
                                out=other[:, 0:m], in0=src_[:, 15:15 + m], in1=invc[:, g, 0:m], op=ALU.mult),
                                reads=[rsrc, ('invc',)], writes=[rother])
                            P.op('dve', lambda e, other=other, Uc=Uc, Dd=Dd, m=m: e.tensor_tensor(
                                out=Dd[:, 0:m], in0=other[:, 0:m], in1=Uc[:, 15:15 + m], op=ALU.subtract),
                                reads=[rother, R_(us, cc), ('UB', us, cc)], writes=[('UB', us, cc)])

            def B1_pe(u):
                kind, j, s = units[u]
                us = u % NSET
                o, w = SUBS[s]
                if kind == 'lru':
                    XCB = UB[:, us, 0, :]
                    if s == 0:
                        blk['gt', j] = acquire('gt', j)
                    gs = blk['gt', j]
                    br, bi = next_bank(), next_bank()
                    ctx[u]['br'], ctx[u]['bi'] = br, bi

                    def mm(e):
                        e.matmul(ps[br][:, 0:w], wr[gs][:, 0:128], XCB[:, 0:w], start=True, stop=True)
                        return e.matmul(ps[bi][:, 0:w], wr[gs][:, 128:256], XCB[:, 0:w], start=True, stop=True)

                    P.op('pe', mm, reads=[('ws', gs), ('UB', us, 0)], writes=[('ps', br), ('ps', bi)])
                    if s == 2:
                        release(gs)
                else:
                    g = j
                    if s == 0:
                        blk['pw', g] = acquire('pw', g)
                    pslot = blk['pw', g]
                    by = [next_bank(), next_bank()]
                    ctx[u]['by'] = by

                    def mm(e):
                        last = None
                        for jc in range(2):
                            for kc in range(2):
                                last = e.matmul(ps[by[jc]][:, 0:w], wr[pslot][:, kc * 256 + jc * 128:kc * 256 + jc * 128 + 128],
                                                UB[:, us, kc, 0:w], start=(kc == 0), stop=(kc == 1))
                        return last

                    P.op('pe', mm, reads=[('ws', pslot), ('UB', us, 0), ('UB', us, 1)], writes=[('ps', by[0]), ('ps', by[1])])
                    if s == 2:
                        release(pslot)

            def B1_act(u):
                kind, j, s = units[u]
                us = u % NSET
                o, w = SUBS[s]
                if kind == 'lru':
                    Rr, Ii, Mm = U[:, us, 2, :], U[:, us, 3, :], U[:, us, 4, :]
                    br, bi = ctx[u]['br'], ctx[u]['bi']
                    hba = dvt[:, d0 + 24 + j:d0 + 25 + j]
                    hbx = dvt[:, d0 + 32 + j:d0 + 33 + j]
                    cc1 = dvt[:, d0 + j:d0 + j + 1]
                    cch = dvt[:, d0 + 8 + j:d0 + 9 + j]
                    P.op('act', lambda e: e.activation(out=Rr[:, 0:w], in_=ps[br][:, 0:w], func=AF.Tanh, scale=0.5, bias=hba),
                         reads=[('ps', br), ('dv',)], writes=[R_(us, 2)])
                    P.op('act', lambda e: e.activation(out=Ii[:, 0:w], in_=ps[bi][:, 0:w], func=AF.Tanh, scale=0.5, bias=hbx),
                         reads=[('ps', bi), ('dv',)], writes=[R_(us, 3)])
                    free_bank(br)
                    free_bank(bi)
                    P.op('act', lambda e: e.activation(out=Rr[:, 0:w], in_=Rr[:, 0:w], func=AF.Exp, scale=cch, bias=cch),
                         reads=[R_(us, 2), ('dv',)], writes=[R_(us, 2)])
                else:
                    g = j
                    by = ctx[u]['by']
                    for jc in range(2):
                        c = 2 * g + jc
                        sc = cvt[:, c0 + 120 + c:c0 + 121 + c]
                        bsc = dvt[:, d0 + 16 + c:d0 + 17 + c]
                        P.op('act', lambda e, jc=jc, c=c, sc=sc, bsc=bsc: e.activation(
                            out=act[:, 8 + c, o:o + w], in_=ps[by[jc]][:, 0:w], func=AF.Identity, scale=sc, bias=bsc),
                            reads=[('ps', by[jc]), ('cv',), ('dv',)], writes=[('act', 8 + c, s)])
                        free_bank(by[jc])

            def B15_dve(u):
                kind, j, s = units[u]
                if kind != 'lru':
                    return
                us = u % NSET
                o, w = SUBS[s]
                Rr, Mm = U[:, us, 2, :], U[:, us, 4, :]
                P.op('dve', lambda e: e.tensor_tensor(out=Mm[:, 0:w], in0=Rr[:, 0:w], in1=Rr[:, 0:w], op=ALU.mult),
                     reads=[R_(us, 2)], writes=[R_(us, 4)])

            def B15_act(u):
                kind, j, s = units[u]
                if kind != 'lru':
                    return
                us = u % NSET
                o, w = SUBS[s]
                Mm = U[:, us, 4, :]
                P.op('act', lambda e: e.activation(out=Mm[:, 0:w], in_=Mm[:, 0:w], func=AF.Sqrt, scale=-0.25, bias=0.25),
                     reads=[R_(us, 4)], writes=[R_(us, 4)])

            def B2_dve(u):
                kind, j, s = units[u]
                if kind != 'lru':
                    return
                us = u % NSET
                o, w = SUBS[s]
                XC, Rr, Ii, Mm, HS = [U[:, us, i, :] for i in (1, 2, 3, 4, 5)]
                P.op('dve', lambda e: e.scalar_tensor_tensor(out=Mm[:, 0:w], in0=Ii[:, 0:w], scalar=1.0, in1=Mm[:, 0:w],
                                                             op0=ALU.add, op1=ALU.mult),
                     reads=[R_(us, 3), R_(us, 4)], writes=[R_(us, 4)])
                P.op('dve', lambda e: e.tensor_tensor(out=Mm[:, 0:w], in0=Mm[:, 0:w], in1=XC[:, 0:w], op=ALU.mult),
                     reads=[R_(us, 4), R_(us, 1)], writes=[R_(us, 4)])
                if s == 0:
                    init = lst[:, l, j:j + 1]
                    rinit = ('lst',)
                else:
                    pus = (u - 1) % NSET
                    pw_ = SUBS[s - 1][1]
                    init = U[:, pus, 5, pw_ - 1:pw_]
                    rinit = R_(pus, 5)
                P.op('dve', lambda e: e.tensor_tensor_scan(out=HS[:, 0:w], data0=Rr[:, 0:w], data1=Mm[:, 0:w], initial=init,
                                                           op0=ALU.mult, op1=ALU.add),
                     reads=[R_(us, 2), R_(us, 4), rinit], writes=[R_(us, 5)])
                if s == 2:
                    P.op('act', lambda e: e.activation(out=lst[:, l, j:j + 1], in_=HS[:, w - 1:w], func=AF.Identity),
                         reads=[R_(us, 5)], writes=[('lst',)])
                gi = u % NTB
                P.op('dve', lambda e: e.tensor_tensor(out=act[:, j, o:o + w], in0=HS[:, 0:w], in1=Tb[gi][:, 0:w], op=ALU.mult),
                     reads=[R_(us, 5), ('T', gi)], writes=[('act', j, s)])

            for t in range(nu + 2):
                if t < nu:
                    A_pe(t)
                if 0 <= t - 2 < nu:
                    B15_act(t - 2)
                if 0 <= t - 1 < nu:
                    B1_dve(t - 1)
                    B1_pe(t - 1)
                if t < nu:
                    A_evac(t)
                if 0 <= t - 1 < nu:
                    B1_act(t - 1)
                if 0 <= t - 2 < nu:
                    B2_dve(t - 2)
                if 0 <= t - 1 < nu:
                    B15_dve(t - 1)

            for d in range(KC):
                slot = acquire('wo', d)
                banks = [next_bank() for _ in SUBS]

                def mm(e, slot=slot, banks=banks):
                    last = None
                    for k in range(KC):
                        for s, (o, w) in enumerate(SUBS):
                            last = e.matmul(ps[banks[s]][:, 0:w], wr[slot][:, k * 128:(k + 1) * 128],
                                            act[:, k, o:o + w], start=(k == 0), stop=(k == KC - 1))
                    return last

                P.op('pe', mm, reads=[('ws', slot)] + [('act', k, s) for k in range(KC) for s in range(3)],
                     writes=[('ps', b) for b in banks])
                release(slot)
                for s, (o, w) in enumerate(SUBS):
                    b = banks[s]
                    P.op('dve', lambda e, b=b, d=d, o=o, w=w: e.tensor_tensor(
                        out=hT[:, d, o:o + w], in0=ps[b][:, 0:w], in1=hT[:, d, o:o + w], op=ALU.add),
                        reads=[('ps', b), ('h', d, s)], writes=[('h', d, s)])
                    free_bank(b)

        for st in range(nst):
            ld_eng = 'sp' if st == 0 else 'act'
            for k in range(KC):
                P.op(ld_eng, lambda e, k=k, st=st: e.dma_start(out=hT[:, k, :], in_=xin[:, k, st * TT:(st + 1) * TT]),
                     writes=hreg(k), sem=f'ld{k}')
            for (l, kind) in phases:
                if kind == 'mix':
                    mixer(l, st)
                else:
                    ffn(l, 1 if kind == 'ffn1' else 2)
            norm(NLAYER * CV_L, 'final')
            c_lo = NMETA if st == 0 else 0
            t_lo = st * TT + c_lo - NMETA
            ncol = TT - c_lo
            for k in range(KC):
                P.op('sp', lambda e, k=k, c_lo=c_lo, t_lo=t_lo, ncol=ncol: e.dma_start(
                    out=y[:, k, t_lo:t_lo + ncol], in_=hT[:, k, c_lo:c_lo + ncol]),
                    reads=hreg(k), sem=f'st{k}')
        for k in range(KC):
            P.final_waits['sp'].append((f'st{k}', P.cnt[f'st{k}']))
        assert ring['acquired'] == total_blocks and ring['emitted'] == total_blocks

        sems = {}
        for key in P.cnt:
            sems[key] = es.enter_context(nc.semaphore(f"s_{key}"))
        block = es.enter_context(nc.Block())
        P.emit(block, sems)
    return nc


def _kmaj(w):
    K = w.shape[0] // 128
    return np.ascontiguousarray(w.reshape(K, 128, w.shape[1]).transpose(1, 0, 2)).reshape(128, K * w.shape[1])


def build_wstream(inp, phases):
    layout, WTOT = stream_layout(phases)
    wst = np.empty((128, WTOT), dtype=np.float32)
    for (l, kind, tag, a, b, off, n) in layout:
        if kind in ('ffn1', 'ffn2'):
            w_in = inp['ffn1_w_in' if kind == 'ffn1' else 'ffn2_w_in'][l]
            w_out = inp['ffn1_w_out' if kind == 'ffn1' else 'ffn2_w_out'][l]
            if tag == 'g':
                blkv = _kmaj(w_in[:, a * 128:(a + 1) * 128])
            elif tag == 'u':
                blkv = _kmaj(w_in[:, DFF + a * 128:DFF + (a + 1) * 128])
            else:
                q, d = a, b
                blkv = _kmaj(w_out[q * FQ * 128:(q + 1) * FQ * 128, d * 128:(d + 1) * 128])
        else:
            w_in = inp['w_in'][l]
            if tag == 'zx':
                blkv = _kmaj(w_in[:, a * 128:(a + 1) * 128])
            elif tag == 'zg':
                blkv = _kmaj(w_in[:, 1024 + a * 128:1024 + (a + 1) * 128])
            elif tag == 'zp':
                blkv = _kmaj(w_in[:, 2048 + a * 128:2048 + (a + 1) * 128])
            elif tag == 'gt':
                blkv = np.concatenate([inp['lru_wa'][l, a], inp['lru_wx'][l, a]], axis=1)
            elif tag == 'pw':
                blkv = _kmaj(inp['pool_w'][l, a])
            else:
                blkv = _kmaj(inp['w_out'][l][:, a * 128:(a + 1) * 128])
        assert blkv.shape == (128, n), (blkv.shape, n, tag)
        wst[:, off:off + n] = blkv
    return wst


def build_cvec(inp):
    cvv = np.zeros((128, NCV), dtype=np.float32)

    def col(v):
        return np.asarray(v, dtype=np.float32).reshape(-1, 128).T

    for l in range(NLAYER):
        c0 = l * CV_L
        cvv[:, c0 + 0:c0 + 16] = col(inp['ffn1_norm'][l])
        cvv[:, c0 + 16:c0 + 32] = col(inp['mix_norm'][l])
        cvv[:, c0 + 32:c0 + 48] = col(inp['ffn2_norm'][l])
        for kk in range(4):
            cvv[:, c0 + 48 + kk * 8:c0 + 56 + kk * 8] = col(inp['conv_w'][l, kk])
        cvv[:, c0 + 80:c0 + 88] = col(inp['conv_b'][l])
        cvv[:, c0 + 88:c0 + 96] = col(inp['lru_ba'][l])
        cvv[:, c0 + 96:c0 + 104] = col(inp['lru_bx'][l])
        cvv[:, c0 + 104:c0 + 112] = col(inp['lru_a_param'][l])
        cvv[:, c0 + 112:c0 + 120] = col(inp['pool_b'][l])
        cvv[:, c0 + 120:c0 + 128] = col(inp['pool_scale'][l])
    cvv[:, NLAYER * CV_L:NLAYER * CV_L + KC] = col(inp['final_norm'])
    return cvv


def build_xin(x_b, meta):
    hfull = np.concatenate([meta, x_b], axis=0)
    return np.ascontiguousarray(hfull.T.reshape(KC, 128, TTOT).transpose(1, 0, 2))


_CACHE = {}


def kernel(**inputs):
    inp = {k: np.asarray(v) for k, v in inputs.items()}
    x = inp['x'].astype(np.float32, copy=False)
    B = x.shape[0]
    phases = phases_all()
    if 'nc' not in _CACHE:
        _CACHE['nc'] = build_program(NST, phases, NSLOT)
    nc = _CACHE['nc']
    wst = build_wstream(inp, phases)
    cvv = build_cvec(inp)
    meta = inp['meta_tokens'].astype(np.float32, copy=False)
    in_maps = [{"xin": build_xin(x[b], meta), "wst": wst, "cv": cvv} for b in range(B)]
    res = run_bass_kernel_spmd(nc, in_maps, core_ids=list(range(B)))
    out = np.empty((B, SEQ, D), dtype=np.float32)
    for b in range(B):
        yb = np.asarray(res.results[b]["y"])
        out[b] = yb.transpose(2, 1, 0).reshape(SEQ, D)
    return out
```

```python
import numpy as np
from contextlib import ExitStack

import concourse.bass as bass
import concourse.mybir as mybir
from concourse.bass_utils import run_bass_kernel_spmd

F32 = mybir.dt.float32
BF16 = mybir.dt.bfloat16
AF = mybir.ActivationFunctionType
ALU = mybir.AluOpType

D = 2048
KC = 16
SEQ = 4096
NMETA = 16
TTOT = SEQ + NMETA
DFF = 5632
FC = DFF // 128
Q = 4
FQ = FC // Q
NST = 4
TT = TTOT // NST
SUBS = [(0, 343), (343, 343), (686, 342)]
HIST = 16
NSLOT = 6
EPS = 1e-6
NHEAD = 8
POOL_WIN = (2, 4, 8, 16)
NLAYER = 2
UW = 360

CV_L = 128
NCV = CV_L * NLAYER + KC
DV_L = 40


def phases_all():
    ph = []
    for l in range(NLAYER):
        ph += [(l, 'ffn1'), (l, 'mix'), (l, 'ffn2')]
    return ph


def mixer_units():
    def lru(j):
        return [('lru', j, s) for s in range(3)]

    def pool(g):
        return [('pool', g, s) for s in range(3)]

    order = pool(0) + lru(0) + lru(1) + pool(1) + lru(2) + lru(3) + pool(2) + lru(4) + lru(5) + lru(6) + lru(7) + pool(3)
    return order


def phase_blocks(kind):
    bl = []
    if kind in ('ffn1', 'ffn2'):
        for q in range(Q):
            for fi in range(FQ):
                f = q * FQ + fi
                bl.append(('g', f, 0, KC * 128))
                bl.append(('u', f, 0, KC * 128))
            for d in range(KC):
                bl.append(('o', q, d, FQ * 128))
    else:
        units = mixer_units()
        nu = len(units)

        def a_acq(u):
            kind, j, s = units[u]
            if s != 0:
                return
            if kind == 'lru':
                bl.append(('zx', j, 0, KC * 128))
                if j == 0:
                    bl.append(('zg', 0, 0, KC * 128))
                if j + 1 < NHEAD:
                    bl.append(('zg', j + 1, 0, KC * 128))
            else:
                bl.append(('zp', 2 * j, 0, KC * 128))
                bl.append(('zp', 2 * j + 1, 0, KC * 128))

        def b_acq(u):
            kind, j, s = units[u]
            if s != 0:
                return
            if kind == 'lru':
                bl.append(('gt', j, 0, 256))
            else:
                bl.append(('pw', j, 0, 512))

        for t in range(nu + 1):
            if t < nu:
                a_acq(t)
            if 0 <= t - 1 < nu:
                b_acq(t - 1)
        for d in range(KC):
            bl.append(('wo', d, 0, KC * 128))
    return bl


def stream_layout(phases):
    out = []
    off = 0
    for (l, kind) in phases:
        for (tag, a, b, n) in phase_blocks(kind):
            out.append((l, kind, tag, a, b, off, n))
            off += n
    return out, off


class Prog:
    ENGS = ('pe', 'act', 'dve', 'pool', 'sp')

    def __init__(self):
        self.q = {e: [] for e in self.ENGS}
        self.cnt = {}
        self.waited = {e: {} for e in self.ENGS}
        self.lastw = {}
        self.readers = {}
        self.final_waits = {e: [] for e in self.ENGS}

    def op(self, eng, fn, reads=(), writes=(), sem=None, pre_inc=0):
        key = sem if sem is not None else eng
        inc = 16 if sem is not None else 1
        deps = {}

        def add(tok):
            if tok is None:
                return
            k, v = tok
            if deps.get(k, 0) < v:
                deps[k] = v

        for r in reads:
            add(self.lastw.get(r))
        for w in writes:
            add(self.lastw.get(w))
            rd = self.readers.get(w)
            if rd:
                for tok in rd.values():
                    add(tok)
        waits = []
        for k, v in deps.items():
            if k == 'pe' and eng == 'pe':
                continue
            if self.waited[eng].get(k, 0) >= v:
                continue
            self.waited[eng][k] = v
            waits.append((k, v))
        self.cnt[key] = self.cnt.get(key, 0) + pre_inc + inc
        tok = (key, self.cnt[key])
        self.q[eng].append((waits, fn, key, inc))
        for r in reads:
            self.readers.setdefault(r, {})[key] = tok
        for w in writes:
            self.lastw[w] = tok
            self.readers[w] = {}
        return tok

    def emit(self, block, sems):
        names = {'pe': 'tensor', 'act': 'scalar', 'dve': 'vector', 'pool': 'gpsimd', 'sp': 'sync'}
        for eng in self.ENGS:
            ops = self.q[eng]
            fw = self.final_waits[eng]

            def body(e, ops=ops, fw=fw):
                for waits, fn, key, inc in ops:
                    for k, v in waits:
                        e.wait_ge(sems[k], v)
                    inst = fn(e)
                    inst.then_inc(sems[key], inc)
                for k, v in fw:
                    e.wait_ge(sems[k], v)

            getattr(block, names[eng])(body)


def build_program(nst=NST, phases=None, nslot=NSLOT):
    if phases is None:
        phases = phases_all()
    layout, WTOT = stream_layout(phases)
    NBLK = len(layout)

    nc = bass.Bass("TRN2", target_bir_lowering=False)
    xin = nc.dram_tensor("xin", [128, KC, TTOT], F32, kind="ExternalInput").ap()
    wst = nc.dram_tensor("wst", [128, WTOT], F32, kind="ExternalInput").ap()
    cv = nc.dram_tensor("cv", [128, NCV], F32, kind="ExternalInput").ap()
    y = nc.dram_tensor("y", [128, KC, SEQ], F32, kind="ExternalOutput").ap()

    P = Prog()
    es = ExitStack()
    with es:
        def sb(name, shape, dt):
            return es.enter_context(nc.sbuf_tensor(name, shape, dt))

        hT = sb("hT", [128, KC, TT], F32)
        xn = sb("xn", [128, KC, HIST + TT], BF16)
        act = sb("act", [128, KC, TT], BF16)
        wr = [sb(f"wr{i}", [128, KC * 128], BF16) for i in range(nslot)]
        sq = [sb(f"sq{i}", [128, TT], BF16) for i in range(2)]
        rstd = sb("rstd", [128, TT], F32)
        NTB = 6
        Tb = [sb(f"Tb{i}", [128, 344], F32) for i in range(NTB)]
        NSET = 3
        U = sb("U", [128, NSET, 6, UW], F32)
        UB = sb("UB", [128, NSET, 2, UW], BF16)
        cvt = sb("cvt", [128, NCV], F32)
        dvt = sb("dvt", [128, NLAYER * DV_L], F32)
        ones = sb("ones", [128, 128], BF16)
        invc = sb("invc", [128, 4, 16], F32)
        xnh = sb("xnh", [128, NLAYER, KC, HIST], BF16)
        lst = sb("lst", [128, NLAYER, NHEAD], F32)
        tmp = sb("tmp", [128, 6, 8], F32)
        ps = [es.enter_context(nc.psum_tensor(f"ps{i}", [128, 512], F32)) for i in range(8)]

        bank_free = list(range(8))

        def next_bank():
            assert bank_free, "out of PSUM banks"
            return bank_free.pop(0)

        def free_bank(b):
            assert b not in bank_free
            bank_free.append(b)

        ring = {'emitted': 0, 'acquired': 0, 'free': [True] * nslot}
        total_blocks = NBLK * nst

        def try_emit_dma():
            while ring['emitted'] < total_blocks:
                n = ring['emitted']
                slot = n % nslot
                if not ring['free'][slot]:
                    return
                ring['free'][slot] = False
                (_, _, _, _, _, off, ncols) = layout[n % NBLK]

                def fn(e, slot=slot, off=off, ncols=ncols):
                    return e.dma_start(out=wr[slot][:, 0:ncols], in_=wst[:, off:off + ncols])

                P.op('pool', fn, reads=[('pro',)], writes=[('ws', slot)], sem=f'ws{slot}')
                ring['emitted'] += 1

        def acquire(tag, a=None):
            n = ring['acquired']
            assert n < ring['emitted'], "weight ring too small for emission order"
            ent = layout[n % NBLK]
            assert ent[2] == tag and (a is None or ent[3] == a), (ent, tag, a)
            ring['acquired'] += 1
            return n % nslot

        def release(slot):
            ring['free'][slot] = True
            try_emit_dma()

        P.op('sp', lambda e: e.dma_start(out=cvt[:, :], in_=cv[:, :]), writes=[('cv',)], sem='ldc')
        P.op('dve', lambda e: e.memset(ones[:, :], 1.0), writes=[('ones',)])
        P.op('dve', lambda e: e.memset(lst[:, :, :], 0.0), writes=[('lst',)])
        for g, win in enumerate(POOL_WIN):
            P.op('dve', lambda e, g=g, win=win: e.memset(invc[:, g, :], 1.0 / win), writes=[('invc',)])
            for t in range(win - 1):
                P.op('dve', lambda e, g=g, t=t: e.memset(invc[:, g, t:t + 1], 1.0 / (t + 1)), writes=[('invc',)])
        for l in range(NLAYER):
            c0 = l * CV_L
            d0 = l * DV_L
            apc = cvt[:, c0 + 104:c0 + 112]
            P.op('dve', lambda e, apc=apc: e.tensor_scalar(out=tmp[:, 0, :], in0=apc, scalar1=-1.0, scalar2=None, op0=ALU.mult),
                 reads=[('cv',)], writes=[('tmp', 0)])
            P.op('dve', lambda e, apc=apc: e.tensor_tensor(out=tmp[:, 1, :], in0=tmp[:, 0, :], in1=apc, op=ALU.max),
                 reads=[('tmp', 0), ('cv',)], writes=[('tmp', 1)])
            P.op('act', lambda e: e.activation(out=tmp[:, 2, :], in_=tmp[:, 1, :], func=AF.Exp, scale=-1.0),
                 reads=[('tmp', 1)], writes=[('tmp', 2)])
            P.op('act', lambda e: e.activation(out=tmp[:, 3, :], in_=tmp[:, 2, :], func=AF.Ln, bias=1.0),
                 reads=[('tmp', 2)], writes=[('tmp', 3)])
            P.op('dve', lambda e: e.tensor_scalar(out=tmp[:, 4, :], in0=tmp[:, 0, :], scalar1=0.0, scalar2=None, op0=ALU.max),
                 reads=[('tmp', 0)], writes=[('tmp', 4)])
            P.op('dve', lambda e: e.tensor_tensor(out=tmp[:, 5, :], in0=tmp[:, 4, :], in1=tmp[:, 3, :], op=ALU.add),
                 reads=[('tmp', 4), ('tmp', 3)], writes=[('tmp', 5)])
            P.op('dve', lambda e, d0=d0: e.tensor_scalar(out=dvt[:, d0:d0 + 8], in0=tmp[:, 5, :], scalar1=-8.0, scalar2=None, op0=ALU.mult),
                 reads=[('tmp', 5)], writes=[('dv',)])
            P.op('dve', lambda e, d0=d0: e.tensor_scalar(out=dvt[:, d0 + 8:d0 + 16], in0=tmp[:, 5, :], scalar1=-4.0, scalar2=None, op0=ALU.mult),
                 reads=[('tmp', 5)], writes=[('dv',)])
            P.op('dve', lambda e, d0=d0, c0=c0: e.tensor_tensor(out=dvt[:, d0 + 16:d0 + 24], in0=cvt[:, c0 + 112:c0 + 120],
                                                                 in1=cvt[:, c0 + 120:c0 + 128], op=ALU.mult),
                 reads=[('cv',)], writes=[('dv',)])
            P.op('dve', lambda e, d0=d0, c0=c0: e.tensor_scalar(out=dvt[:, d0 + 24:d0 + 40], in0=cvt[:, c0 + 88:c0 + 104], scalar1=0.5,
                                                                 scalar2=None, op0=ALU.mult),
                 reads=[('cv',)], writes=[('dv',)])
        P.op('dve', lambda e: e.memset(tmp[:, 0, :], 0.0), reads=[('dv',), ('ones',), ('invc',), ('lst',)], writes=[('pro',), ('tmp', 0)])

        try_emit_dma()

        sq_ctr = [0]
        tb_ctr = [0]

        def hreg(k):
            return [('h', k, s) for s in range(3)]

        def norm(gcol0, mode):
            banks = [next_bank() for _ in SUBS]
            for k in range(KC):
                i = k % 2
                if i == 0:
                    P.op('act', lambda e, k=k, i=i: e.activation(out=sq[i][:, :], in_=hT[:, k, :], func=AF.Square),
                         reads=hreg(k), writes=[('sq', i)])
                else:
                    P.op('dve', lambda e, k=k, i=i: e.tensor_tensor(out=sq[i][:, :], in0=hT[:, k, :], in1=hT[:, k, :], op=ALU.mult),
                         reads=hreg(k), writes=[('sq', i)])

                def mm(e, k=k, i=i):
                    last = None
                    for s, (o, w) in enumerate(SUBS):
                        last = e.matmul(ps[banks[s]][:, 0:w], ones[:, :], sq[i][:, o:o + w],
                                        start=(k == 0), stop=(k == KC - 1))
                    return last

                P.op('pe', mm, reads=[('sq', i), ('ones',)], writes=[('ps', b) for b in banks])
            for s, (o, w) in enumerate(SUBS):
                bk = banks[s]
                P.op('act', lambda e, bk=bk, w=w: e.activation(out=ps[bk][:, 0:w], in_=ps[bk][:, 0:w],
                                                               func=AF.Ln, scale=1.0 / D, bias=EPS),
                     reads=[('ps', bk)], writes=[('ps', bk)])
                P.op('act', lambda e, bk=bk, o=o, w=w: e.activation(out=rstd[:, o:o + w], in_=ps[bk][:, 0:w],
                                                                    func=AF.Exp, scale=-0.5),
                     reads=[('ps', bk)], writes=[('rstd', s)])
                free_bank(bk)
            for k in range(KC):
                if mode == 'xn':
                    P.op('dve', lambda e, k=k: e.scalar_tensor_tensor(
                        out=xn[:, k, HIST:HIST + TT], in0=hT[:, k, :], scalar=cvt[:, gcol0 + k:gcol0 + k + 1],
                        in1=rstd[:, :], op0=ALU.mult, op1=ALU.mult),
                        reads=hreg(k) + [('rstd', s) for s in range(3)] + [('cv',)],
                        writes=[('xn', k, s) for s in range(3)])
                else:
                    P.op('dve', lambda e, k=k: e.scalar_tensor_tensor(
                        out=hT[:, k, :], in0=hT[:, k, :], scalar=cvt[:, gcol0 + k:gcol0 + k + 1],
                        in1=rstd[:, :], op0=ALU.mult, op1=ALU.mult),
                        reads=hreg(k) + [('rstd', s) for s in range(3)] + [('cv',)],
                        writes=hreg(k))

        def xn_regs():
            return [('xn', k, s) for k in range(KC) for s in range(3)]

        def ffn(l, which):
            gcol0 = l * CV_L + (0 if which == 1 else 32)
            norm(gcol0, 'xn')
            for q in range(Q):
                for fi in range(FQ):
                    f = q * FQ + fi
                    bks = []
                    if f == 0:
                        sg, su = acquire('g', f), acquire('u', f)
                        bkg = [next_bank() for _ in SUBS]
                        bku = [next_bank() for _ in SUBS]
                        for k in range(KC):
                            def mmk(e, sg=sg, su=su, bkg=bkg, bku=bku, k=k):
                                last = None
                                for slot, banks in ((sg, bkg), (su, bku)):
                                    for s, (o, w) in enumerate(SUBS):
                                        last = e.matmul(ps[banks[s]][:, 0:w], wr[slot][:, k * 128:(k + 1) * 128],
                                                        xn[:, k, HIST + o:HIST + o + w], start=(k == 0), stop=(k == KC - 1))
                                return last
                            P.op('pe', mmk, reads=[('ws', sg), ('ws', su)] + [('xn', k, s) for s in range(3)],
                                 writes=[('ps', b) for b in bkg + bku])
                        release(sg)
                        release(su)
                        bks = [bkg, bku]
                    else:
                        for tag in ('g', 'u'):
                            slot = acquire(tag, f)
                            banks = [next_bank() for _ in SUBS]

                            def mm(e, slot=slot, banks=banks):
                                last = None
                                for k in range(KC):
                                    for s, (o, w) in enumerate(SUBS):
                                        last = e.matmul(ps[banks[s]][:, 0:w], wr[slot][:, k * 128:(k + 1) * 128],
                                                        xn[:, k, HIST + o:HIST + o + w], start=(k == 0), stop=(k == KC - 1))
                                return last

                            P.op('pe', mm, reads=[('ws', slot)] + xn_regs(), writes=[('ps', b) for b in banks])
                            release(slot)
                            bks.append(banks)
                    for s, (o, w) in enumerate(SUBS):
                        ti = tb_ctr[0] % NTB
                        tb_ctr[0] += 1
                        bg, bu = bks[0][s], bks[1][s]
                        P.op('act', lambda e, ti=ti, bg=bg, w=w: e.activation(out=Tb[ti][:, 0:w], in_=ps[bg][:, 0:w], func=AF.Silu),
                             reads=[('ps', bg)], writes=[('T', ti)])
                        P.op('dve', lambda e, ti=ti, bu=bu, fi=fi, o=o, w=w: e.tensor_tensor(
                            out=act[:, fi, o:o + w], in0=Tb[ti][:, 0:w], in1=ps[bu][:, 0:w], op=ALU.mult),
                            reads=[('T', ti), ('ps', bu)], writes=[('act', fi, s)])
                        free_bank(bg)
                        free_bank(bu)
                for d in range(KC):
                    slot = acquire('o', q)
                    banks = [next_bank() for _ in SUBS]

                    def mm(e, slot=slot, banks=banks):
                        last = None
                        for kk in range(FQ):
                            for s, (o, w) in enumerate(SUBS):
                                last = e.matmul(ps[banks[s]][:, 0:w], wr[slot][:, kk * 128:(kk + 1) * 128],
                                                act[:, kk, o:o + w], start=(kk == 0), stop=(kk == FQ - 1))
                        return last

                    P.op('pe', mm, reads=[('ws', slot)] + [('act', kk, s) for kk in range(FQ) for s in range(3)],
                         writes=[('ps', b) for b in banks])
                    release(slot)
                    for s, (o, w) in enumerate(SUBS):
                        b = banks[s]
                        P.op('dve', lambda e, b=b, d=d, o=o, w=w: e.scalar_tensor_tensor(
                            out=hT[:, d, o:o + w], in0=ps[b][:, 0:w], scalar=0.5, in1=hT[:, d, o:o + w],
                            op0=ALU.mult, op1=ALU.add),
                            reads=[('ps', b), ('h', d, s)], writes=[('h', d, s)])
                        free_bank(b)

        def mixer(l, st):
            c0 = l * CV_L
            d0 = l * DV_L
            if st == 0:
                P.op('dve', lambda e: e.memset(xn[:, :, 0:HIST], 0.0), writes=[('xnh',)])
            else:
                P.op('act', lambda e: e.activation(out=xn[:, :, 0:HIST], in_=xnh[:, l, :, :], func=AF.Identity),
                     reads=[('xnhist', l)], writes=[('xnh',)])
            norm(c0 + 16, 'xn')
            P.op('act', lambda e: e.activation(out=xnh[:, l, :, :], in_=xn[:, :, TT:TT + HIST], func=AF.Identity),
                 reads=[('xn', k, 2) for k in range(KC)], writes=[('xnhist', l)])

            def xin_regs(s, hist):
                r = [('xn', k, s) for k in range(KC)]
                if hist:
                    if s == 0:
                        r.append(('xnh',))
                    else:
                        r += [('xn', k, s - 1) for k in range(KC)]
                return r

            def proj(slot, bank, s, nh, split_k=False):
                o, w = SUBS[s]

                def mm(e, ks=range(KC)):
                    last = None
                    for k in ks:
                        last = e.matmul(ps[bank][:, 0:nh + w], wr[slot][:, k * 128:(k + 1) * 128],
                                        xn[:, k, HIST + o - nh:HIST + o + w], start=(k == 0), stop=(k == KC - 1))
                    return last

                if split_k:
                    for k in range(KC):
                        r = [('ws', slot), ('xn', k, s)]
                        if nh > 0:
                            r.append(('xnh',) if s == 0 else ('xn', k, s - 1))
                        P.op('pe', lambda e, k=k: mm(e, [k]), reads=r, writes=[('ps', bank)])
                else:
                    P.op('pe', mm, reads=[('ws', slot)] + xin_regs(s, nh > 0), writes=[('ps', bank)])

            units = mixer_units()
            nu = len(units)
            blk = {}
            ctx = [dict() for _ in units]
            zgb = {}

            def R_(us, i):
                return ('U', us, i)

            def A_pe(u):
                kind, j, s = units[u]
                if kind == 'lru':
                    if s == 0:
                        blk['zx', j] = acquire('zx', j)
                    bzx = next_bank()
                    ctx[u]['bzx'] = bzx
                    proj(blk['zx', j], bzx, s, 3, split_k=(u == 0))
                    if j == 0 and s == 0:
                        zs = acquire('zg', 0)
                        zgb[0] = [next_bank() for _ in range(3)]
                        for s2 in range(3):
                            proj(zs, zgb[0][s2], s2, 0)
                        release(zs)
                    if j + 1 < NHEAD:
                        if s == 0:
                            blk['zg', j + 1] = acquire('zg', j + 1)
                            zgb[j + 1] = [None] * 3
                        zgb[j + 1][s] = next_bank()
                        proj(blk['zg', j + 1], zgb[j + 1][s], s, 0)
                        if s == 2:
                            release(blk['zg', j + 1])
                    if s == 2:
                        release(blk['zx', j])
                else:
                    g = j
                    if s == 0:
                        blk['zp', 2 * g] = acquire('zp', 2 * g)
                        blk['zp', 2 * g + 1] = acquire('zp', 2 * g + 1)
                    bz = [next_bank(), next_bank()]
                    ctx[u]['bz'] = bz
                    for cc in range(2):
                        proj(blk['zp', 2 * g + cc], bz[cc], s, 15, split_k=(u == 0))
                    if s == 2:
                        release(blk['zp', 2 * g])
                        release(blk['zp', 2 * g + 1])

            def A_evac(u):
                kind, j, s = units[u]
                us = u % NSET
                o, w = SUBS[s]
                if kind == 'lru':
                    bzx = ctx[u]['bzx']
                    ZX, XC = U[:, us, 0, :], U[:, us, 1, :]
                    cw3 = cvt[:, c0 + 48 + 3 * 8 + j:c0 + 48 + 3 * 8 + j + 1]
                    cb = cvt[:, c0 + 80 + j:c0 + 81 + j]
                    P.op('dve', lambda e: e.tensor_copy(out=ZX[:, 0:3 + w], in_=ps[bzx][:, 0:3 + w]),
                         reads=[('ps', bzx)], writes=[R_(us, 0)])
                    free_bank(bzx)
                    P.op('act', lambda e: e.activation(out=XC[:, 0:w], in_=ZX[:, 3:3 + w], func=AF.Identity, scale=cw3, bias=cb),
                         reads=[R_(us, 0), ('cv',)], writes=[R_(us, 1)])
                    gh = None
                    if j == 0 and s == 0:
                        gh = 0
                    elif s == 2 and j + 1 < NHEAD:
                        gh = j + 1
                    if gh is not None:
                        bzg = zgb[gh]
                        for s2 in range(3):
                            w2 = SUBS[s2][1]
                            gi = (3 * gh + s2) % NTB
                            P.op('act', lambda e, s2=s2, w2=w2, gi=gi, bzg=bzg: e.activation(
                                out=Tb[gi][:, 0:w2], in_=ps[bzg[s2]][:, 0:w2], func=AF.Gelu_apprx_tanh),
                                reads=[('ps', bzg[s2])], writes=[('T', gi)])
                            free_bank(bzg[s2])
                else:
                    bz = ctx[u]['bz']
                    for cc in range(2):
                        P.op('act', lambda e, cc=cc: e.activation(out=U[:, us, cc, 0:15 + w], in_=ps[bz[cc]][:, 0:15 + w], func=AF.Identity),
                             reads=[('ps', bz[cc])], writes=[R_(us, cc)])
                        free_bank(bz[cc])

            def B1_dve(u):
                kind, j, s = units[u]
                us = u % NSET
                o, w = SUBS[s]
                if kind == 'lru':
                    ZX, XC = U[:, us, 0, :], U[:, us, 1, :]
                    XCB = UB[:, us, 0, :]
                    cw = [cvt[:, c0 + 48 + kk * 8 + j:c0 + 48 + kk * 8 + j + 1] for kk in range(4)]
                    for kk in (2, 1, 0):
                        P.op('dve', lambda e, kk=kk: e.scalar_tensor_tensor(out=XC[:, 0:w], in0=ZX[:, kk:kk + w], scalar=cw[kk],
                                                                            in1=XC[:, 0:w], op0=ALU.mult, op1=ALU.add),
                             reads=[R_(us, 0), R_(us, 1), ('cv',)], writes=[R_(us, 1)])
                    P.op('dve', lambda e: e.tensor_tensor(out=XCB[:, 0:w], in0=XC[:, 0:w], in1=XC[:, 0:w], op=ALU.max),
                         reads=[R_(us, 1)], writes=[('UB', us, 0)])
                else:
                    g = j
                    win = POOL_WIN[g]
                    nlev = g + 1
                    n = 15 + w
                    for cc in range(2):
                        Uc = U[:, us, cc, :]
                        A = U[:, us, 2 + 2 * cc, :]
                        B = U[:, us, 3 + 2 * cc, :]
                        rA, rB = R_(us, 2 + 2 * cc), R_(us, 3 + 2 * cc)
                        src_, rsrc = Uc, R_(us, cc)
                        for lev in range(nlev):
                            sh = 1 << lev
                            lo = (1 << (lev + 1)) - 1
                            dst, rdst = (A, rA) if lev % 2 == 0 else (B, rB)
                            P.op('dve', lambda e, src_=src_, dst=dst, sh=sh, lo=lo: e.tensor_tensor(
                                out=dst[:, lo:n], in0=src_[:, lo:n], in1=src_[:, lo - sh:n - sh], op=ALU.add),
                                reads=[rsrc], writes=[rdst])
                            src_, rsrc = dst, rdst
                        Dd = UB[:, us, cc, :]
                        P.op('dve', lambda e, src_=src_, Uc=Uc, Dd=Dd: e.scalar_tensor_tensor(
                            out=Dd[:, 0:w], in0=src_[:, 15:15 + w], scalar=1.0 / win, in1=Uc[:, 15:15 + w],
                            op0=ALU.mult, op1=ALU.subtract),
                            reads=[rsrc, R_(us, cc)], writes=[('UB', us, cc)])
                        if st == 0 and s == 0:
                            m = win - 1
                            other, rother = (B, rB) if src_ is A else (A, rA)
                            P.op('dve', lambda e, src_=src_, other=other, m=m: e.tensor_tensor(
                                out=other[:, 0:m], in0=src_[:, 15:15 + m], in1=invc[:, g, 0:m], op=ALU.mult),
                                reads=[rsrc, ('invc',)], writes=[rother])
                            P.op('dve', lambda e, other=other, Uc=Uc, Dd=Dd, m=m: e.tensor_tensor(
                                out=Dd[:, 0:m], in0=other[:, 0:m], in1=Uc[:, 15:15 + m], op=ALU.subtract),
                                reads=[rother, R_(us, cc), ('UB', us, cc)], writes=[('UB', us, cc)])

            def B1_pe(u):
                kind, j, s = units[u]
                us = u % NSET
                o, w = SUBS[s]
                if kind == 'lru':
                    XCB = UB[:, us, 0, :]
                    if s == 0:
                        blk['gt', j] = acquire('gt', j)
                    gs = blk['gt', j]
                    br, bi = next_bank(), next_bank()
                    ctx[u]['br'], ctx[u]['bi'] = br, bi

                    def mm(e):
                        e.matmul(ps[br][:, 0:w], wr[gs][:, 0:128], XCB[:, 0:w], start=True, stop=True)
                        return e.matmul(ps[bi][:, 0:w], wr[gs][:, 128:256], XCB[:, 0:w], start=True, stop=True)

                    P.op('pe', mm, reads=[('ws', gs), ('UB', us, 0)], writes=[('ps', br), ('ps', bi)])
                    if s == 2:
                        release(gs)
                else:
                    g = j
                    if s == 0:
                        blk['pw', g] = acquire('pw', g)
                    pslot = blk['pw', g]
                    by = [next_bank(), next_bank()]
                    ctx[u]['by'] = by

                    def mm(e):
                        last = None
                        for jc in range(2):
                            for kc in range(2):
                                last = e.matmul(ps[by[jc]][:, 0:w], wr[pslot][:, kc * 256 + jc * 128:kc * 256 + jc * 128 + 128],
                                                UB[:, us, kc, 0:w], start=(kc == 0), stop=(kc == 1))
                        return last

                    P.op('pe', mm, reads=[('ws', pslot), ('UB', us, 0), ('UB', us, 1)], writes=[('ps', by[0]), ('ps', by[1])])
                    if s == 2:
                        release(pslot)

            def B1_act(u):
                kind, j, s = units[u]
                us = u % NSET
                o, w = SUBS[s]
                if kind == 'lru':
                    Rr, Ii, Mm = U[:, us, 2, :], U[:, us, 3, :], U[:, us, 4, :]
                    br, bi = ctx[u]['br'], ctx[u]['bi']
                    hba = dvt[:, d0 + 24 + j:d0 + 25 + j]
                    hbx = dvt[:, d0 + 32 + j:d0 + 33 + j]
                    cc1 = dvt[:, d0 + j:d0 + j + 1]
                    cch = dvt[:, d0 + 8 + j:d0 + 9 + j]
                    P.op('act', lambda e: e.activation(out=Rr[:, 0:w], in_=ps[br][:, 0:w], func=AF.Tanh, scale=0.5, bias=hba),
                         reads=[('ps', br), ('dv',)], writes=[R_(us, 2)])
                    P.op('act', lambda e: e.activation(out=Ii[:, 0:w], in_=ps[bi][:, 0:w], func=AF.Tanh, scale=0.5, bias=hbx),
                         reads=[('ps', bi), ('dv',)], writes=[R_(us, 3)])
                    free_bank(br)
                    free_bank(bi)
                    P.op('act', lambda e: e.activation(out=Rr[:, 0:w], in_=Rr[:, 0:w], func=AF.Exp, scale=cch, bias=cch),
                         reads=[R_(us, 2), ('dv',)], writes=[R_(us, 2)])
                else:
                    g = j
                    by = ctx[u]['by']
                    for jc in range(2):
                        c = 2 * g + jc
                        sc = cvt[:, c0 + 120 + c:c0 + 121 + c]
                        bsc = dvt[:, d0 + 16 + c:d0 + 17 + c]
                        P.op('act', lambda e, jc=jc, c=c, sc=sc, bsc=bsc: e.activation(
                            out=act[:, 8 + c, o:o + w], in_=ps[by[jc]][:, 0:w], func=AF.Identity, scale=sc, bias=bsc),
                            reads=[('ps', by[jc]), ('cv',), ('dv',)], writes=[('act', 8 + c, s)])
                        free_bank(by[jc])

            def B15_dve(u):
                kind, j, s = units[u]
                if kind != 'lru':
                    return
                us = u % NSET
                o, w = SUBS[s]
                Rr, Mm = U[:, us, 2, :], U[:, us, 4, :]
                P.op('dve', lambda e: e.tensor_tensor(out=Mm[:, 0:w], in0=Rr[:, 0:w], in1=Rr[:, 0:w], op=ALU.mult),
                     reads=[R_(us, 2)], writes=[R_(us, 4)])

            def B15_act(u):
                kind, j, s = units[u]
                if kind != 'lru':
                    return
                us = u % NSET
                o, w = SUBS[s]
                Mm = U[:, us, 4, :]
                P.op('act', lambda e: e.activation(out=Mm[:, 0:w], in_=Mm[:, 0:w], func=AF.Sqrt, scale=-0.25, bias=0.25),
                     reads=[R_(us, 4)], writes=[R_(us, 4)])

            def B2_dve(u):
                kind, j, s = units[u]
                if kind != 'lru':
                    return
                us = u % NSET
                o, w = SUBS[s]
                XC, Rr, Ii, Mm, HS = [U[:, us, i, :] for i in (1, 2, 3, 4, 5)]
                P.op('dve', lambda e: e.scalar_tensor_tensor(out=Mm[:, 0:w], in0=Ii[:, 0:w], scalar=1.0, in1=Mm[:, 0:w],
                                                             op0=ALU.add, op1=ALU.mult),
                     reads=[R_(us, 3), R_(us, 4)], writes=[R_(us, 4)])
                P.op('dve', lambda e: e.tensor_tensor(out=Mm[:, 0:w], in0=Mm[:, 0:w], in1=XC[:, 0:w], op=ALU.mult),
                     reads=[R_(us, 4), R_(us, 1)], writes=[R_(us, 4)])
                if s == 0:
                    init = lst[:, l, j:j + 1]
                    rinit = ('lst',)
                else:
                    pus = (u - 1) % NSET
                    pw_ = SUBS[s - 1][1]
                    init = U[:, pus, 5, pw_ - 1:pw_]
                    rinit = R_(pus, 5)
                P.op('dve', lambda e: e.tensor_tensor_scan(out=HS[:, 0:w], data0=Rr[:, 0:w], data1=Mm[:, 0:w], initial=init,
                                                           op0=ALU.mult, op1=ALU.add),
                     reads=[R_(us, 2), R_(us, 4), rinit], writes=[R_(us, 5)])
                if s == 2:
                    P.op('act', lambda e: e.activation(out=lst[:, l, j:j + 1], in_=HS[:, w - 1:w], func=AF.Identity),
                         reads=[R_(us, 5)], writes=[('lst',)])
                gi = (3 * j + s) % NTB
                P.op('dve', lambda e: e.tensor_tensor(out=act[:, j, o:o + w], in0=HS[:, 0:w], in1=Tb[gi][:, 0:w], op=ALU.mult),
                     reads=[R_(us, 5), ('T', gi)], writes=[('act', j, s)])

            for t in range(nu + 2):
                if t < nu:
                    A_pe(t)
                if 0 <= t - 2 < nu:
                    B15_act(t - 2)
                if 0 <= t - 1 < nu:
                    B1_dve(t - 1)
                    B1_pe(t - 1)
                if t < nu:
                    A_evac(t)
                if 0 <= t - 1 < nu:
                    B1_act(t - 1)
                if 0 <= t - 2 < nu:
                    B2_dve(t - 2)
                if 0 <= t - 1 < nu:
                    B15_dve(t - 1)

            for d in range(KC):
                slot = acquire('wo', d)
                banks = [next_bank() for _ in SUBS]

                def mm(e, slot=slot, banks=banks):
                    last = None
                    for k in range(KC):
                        for s, (o, w) in enumerate(SUBS):
                            last = e.matmul(ps[banks[s]][:, 0:w], wr[slot][:, k * 128:(k + 1) * 128],
                                            act[:, k, o:o + w], start=(k == 0), stop=(k == KC - 1))
                    return last

                P.op('pe', mm, reads=[('ws', slot)] + [('act', k, s) for k in range(KC) for s in range(3)],
                     writes=[('ps', b) for b in banks])
                release(slot)
                for s, (o, w) in enumerate(SUBS):
                    b = banks[s]
                    P.op('dve', lambda e, b=b, d=d, o=o, w=w: e.tensor_tensor(
                        out=hT[:, d, o:o + w], in0=ps[b][:, 0:w], in1=hT[:, d, o:o + w], op=ALU.add),
                        reads=[('ps', b), ('h', d, s)], writes=[('h', d, s)])
                    free_bank(b)

        for st in range(nst):
            ld_eng = 'sp' if st == 0 else 'act'
            for k in range(KC):
                P.op(ld_eng, lambda e, k=k, st=st: e.dma_start(out=hT[:, k, :], in_=xin[:, k, st * TT:(st + 1) * TT]),
                     writes=hreg(k), sem=f'ld{k}')
            for (l, kind) in phases:
                if kind == 'mix':
                    mixer(l, st)
                else:
                    ffn(l, 1 if kind == 'ffn1' else 2)
            norm(NLAYER * CV_L, 'final')
            c_lo = NMETA if st == 0 else 0
            t_lo = st * TT + c_lo - NMETA
            ncol = TT - c_lo
            for k in range(KC):
                P.op('sp', lambda e, k=k, c_lo=c_lo, t_lo=t_lo, ncol=ncol: e.dma_start(
                    out=y[:, k, t_lo:t_lo + ncol], in_=hT[:, k, c_lo:c_lo + ncol]),
                    reads=hreg(k), sem=f'st{k}')
        for k in range(KC):
            P.final_waits['sp'].append((f'st{k}', P.cnt[f'st{k}']))
        assert ring['acquired'] == total_blocks and ring['emitted'] == total_blocks

        sems = {}
        for key in P.cnt:
            sems[key] = es.enter_context(nc.semaphore(f"s_{key}"))
        block = es.enter_context(nc.Block())
        P.emit(block, sems)
    return nc


def _kmaj(w):
    K = w.shape[0] // 128
    return np.ascontiguousarray(w.reshape(K, 128, w.shape[1]).transpose(1, 0, 2)).reshape(128, K * w.shape[1])


def build_wstream(inp, phases):
    layout, WTOT = stream_layout(phases)
    wst = np.empty((128, WTOT), dtype=np.float32)
    for (l, kind, tag, a, b, off, n) in layout:
        if kind in ('ffn1', 'ffn2'):
            w_in = inp['ffn1_w_in' if kind == 'ffn1' else 'ffn2_w_in'][l]
            w_out = inp['ffn1_w_out' if kind == 'ffn1' else 'ffn2_w_out'][l]
            if tag == 'g':
                blkv = _kmaj(w_in[:, a * 128:(a + 1) * 128])
            elif tag == 'u':
                blkv = _kmaj(w_in[:, DFF + a * 128:DFF + (a + 1) * 128])
            else:
                q, d = a, b
                blkv = _kmaj(w_out[q * FQ * 128:(q + 1) * FQ * 128, d * 128:(d + 1) * 128])
        else:
            w_in = inp['w_in'][l]
            if tag == 'zx':
                blkv = _kmaj(w_in[:, a * 128:(a + 1) * 128])
            elif tag == 'zg':
                blkv = _kmaj(w_in[:, 1024 + a * 128:1024 + (a + 1) * 128])
            elif tag == 'zp':
                blkv = _kmaj(w_in[:, 2048 + a * 128:2048 + (a + 1) * 128])
            elif tag == 'gt':
                blkv = np.concatenate([inp['lru_wa'][l, a], inp['lru_wx'][l, a]], axis=1)
            elif tag == 'pw':
                blkv = _kmaj(inp['pool_w'][l, a])
            else:
                blkv = _kmaj(inp['w_out'][l][:, a * 128:(a + 1) * 128])
        assert blkv.shape == (128, n), (blkv.shape, n, tag)
        wst[:, off:off + n] = blkv
    return wst


def build_cvec(inp):
    cvv = np.zeros((128, NCV), dtype=np.float32)

    def col(v):
        return np.asarray(v, dtype=np.float32).reshape(-1, 128).T

    for l in range(NLAYER):
        c0 = l * CV_L
        cvv[:, c0 + 0:c0 + 16] = col(inp['ffn1_norm'][l])
        cvv[:, c0 + 16:c0 + 32] = col(inp['mix_norm'][l])
        cvv[:, c0 + 32:c0 + 48] = col(inp['ffn2_norm'][l])
        for kk in range(4):
            cvv[:, c0 + 48 + kk * 8:c0 + 56 + kk * 8] = col(inp['conv_w'][l, kk])
        cvv[:, c0 + 80:c0 + 88] = col(inp['conv_b'][l])
        cvv[:, c0 + 88:c0 + 96] = col(inp['lru_ba'][l])
        cvv[:, c0 + 96:c0 + 104] = col(inp['lru_bx'][l])
        cvv[:, c0 + 104:c0 + 112] = col(inp['lru_a_param'][l])
        cvv[:, c0 + 112:c0 + 120] = col(inp['pool_b'][l])
        cvv[:, c0 + 120:c0 + 128] = col(inp['pool_scale'][l])
    cvv[:, NLAYER * CV_L:NLAYER * CV_L + KC] = col(inp['final_norm'])
    return cvv


def build_xin(x_b, meta):
    hfull = np.concatenate([meta, x_b], axis=0)
    return np.ascontiguousarray(hfull.T.reshape(KC, 128, TTOT).transpose(1, 0, 2))


_CACHE = {}


def kernel(**inputs):
    inp = {k: np.asarray(v) for k, v in inputs.items()}
    x = inp['x'].astype(np.float32, copy=False)
    B = x.shape[0]
    phases = phases_all()
    if 'nc' not in _CACHE:
        _CACHE['nc'] = build_program(NST, phases, NSLOT)
    nc = _CACHE['nc']
    wst = build_wstream(inp, phases)
    cvv = build_cvec(inp)
    meta = inp['meta_tokens'].astype(np.float32, copy=False)
    in_maps = [{"xin": build_xin(x[b], meta), "wst": wst, "cv": cvv} for b in range(B)]
    res = run_bass_kernel_spmd(nc, in_maps, core_ids=list(range(B)))
    out = np.empty((B, SEQ, D), dtype=np.float32)
    for b in range(B):
        yb = np.asarray(res.results[b]["y"])
        out[b] = yb.transpose(2, 1, 0).reshape(SEQ, D)
    return out
```

```python
import numpy as np
from contextlib import ExitStack

import concourse.bass as bass
import concourse.mybir as mybir
from concourse.bass_utils import run_bass_kernel_spmd

F32 = mybir.dt.float32
BF16 = mybir.dt.bfloat16
AF = mybir.ActivationFunctionType
ALU = mybir.AluOpType

D = 2048
KC = 16
SEQ = 4096
NMETA = 16
TTOT = SEQ + NMETA
DFF = 5632
FC = DFF // 128
Q = 4
FQ = FC // Q
NST = 4
TT = TTOT // NST
SUBS = [(0, 343), (343, 343), (686, 342)]
HIST = 16
NSLOT = 6
EPS = 1e-6
NHEAD = 8
POOL_WIN = (2, 4, 8, 16)
NLAYER = 2
UW = 360

CV_L = 128
NCV = CV_L * NLAYER + KC
DV_L = 40


def phases_all():
    ph = []
    for l in range(NLAYER):
        ph += [(l, 'ffn1'), (l, 'mix'), (l, 'ffn2')]
    return ph


def phase_blocks(kind):
    bl = []
    if kind in ('ffn1', 'ffn2'):
        for q in range(Q):
            for fi in range(FQ):
                f = q * FQ + fi
                bl.append(('g', f, 0, KC * 128))
                bl.append(('u', f, 0, KC * 128))
            for d in range(KC):
                bl.append(('o', q, d, FQ * 128))
    else:
        for j in range(NHEAD):
            bl.append(('zx', j, 0, KC * 128))
            if j == 0:
                bl.append(('zg', 0, 0, KC * 128))
            if j + 1 < NHEAD:
                bl.append(('zg', j + 1, 0, KC * 128))
            bl.append(('gt', j, 0, 256))
        for g in range(4):
            bl.append(('zp', 2 * g, 0, KC * 128))
            bl.append(('zp', 2 * g + 1, 0, KC * 128))
            bl.append(('pw', g, 0, 512))
        for d in range(KC):
            bl.append(('wo', d, 0, KC * 128))
    return bl


def stream_layout(phases):
    out = []
    off = 0
    for (l, kind) in phases:
        for (tag, a, b, n) in phase_blocks(kind):
            out.append((l, kind, tag, a, b, off, n))
            off += n
    return out, off


class Prog:
    ENGS = ('pe', 'act', 'dve', 'pool', 'sp')

    def __init__(self):
        self.q = {e: [] for e in self.ENGS}
        self.cnt = {}
        self.waited = {e: {} for e in self.ENGS}
        self.lastw = {}
        self.readers = {}
        self.final_waits = {e: [] for e in self.ENGS}

    def op(self, eng, fn, reads=(), writes=(), sem=None, pre_inc=0):
        key = sem if sem is not None else eng
        inc = 16 if sem is not None else 1
        deps = {}

        def add(tok):
            if tok is None:
                return
            k, v = tok
            if deps.get(k, 0) < v:
                deps[k] = v

        for r in reads:
            add(self.lastw.get(r))
        for w in writes:
            add(self.lastw.get(w))
            rd = self.readers.get(w)
            if rd:
                for tok in rd.values():
                    add(tok)
        waits = []
        for k, v in deps.items():
            if k == 'pe' and eng == 'pe':
                continue
            if self.waited[eng].get(k, 0) >= v:
                continue
            self.waited[eng][k] = v
            waits.append((k, v))
        self.cnt[key] = self.cnt.get(key, 0) + pre_inc + inc
        tok = (key, self.cnt[key])
        self.q[eng].append((waits, fn, key, inc))
        for r in reads:
            self.readers.setdefault(r, {})[key] = tok
        for w in writes:
            self.lastw[w] = tok
            self.readers[w] = {}
        return tok

    def emit(self, block, sems):
        names = {'pe': 'tensor', 'act': 'scalar', 'dve': 'vector', 'pool': 'gpsimd', 'sp': 'sync'}
        for eng in self.ENGS:
            ops = self.q[eng]
            fw = self.final_waits[eng]

            def body(e, ops=ops, fw=fw):
                for waits, fn, key, inc in ops:
                    for k, v in waits:
                        e.wait_ge(sems[k], v)
                    inst = fn(e)
                    inst.then_inc(sems[key], inc)
                for k, v in fw:
                    e.wait_ge(sems[k], v)

            getattr(block, names[eng])(body)


def build_program(nst=NST, phases=None, nslot=NSLOT):
    if phases is None:
        phases = phases_all()
    layout, WTOT = stream_layout(phases)
    NBLK = len(layout)

    nc = bass.Bass("TRN2", target_bir_lowering=False)
    xin = nc.dram_tensor("xin", [128, KC, TTOT], F32, kind="ExternalInput").ap()
    wst = nc.dram_tensor("wst", [128, WTOT], F32, kind="ExternalInput").ap()
    cv = nc.dram_tensor("cv", [128, NCV], F32, kind="ExternalInput").ap()
    y = nc.dram_tensor("y", [128, KC, SEQ], F32, kind="ExternalOutput").ap()

    P = Prog()
    es = ExitStack()
    with es:
        def sb(name, shape, dt):
            return es.enter_context(nc.sbuf_tensor(name, shape, dt))

        hT = sb("hT", [128, KC, TT], F32)
        xn = sb("xn", [128, KC, HIST + TT], BF16)
        act = sb("act", [128, KC, TT], BF16)
        wr = [sb(f"wr{i}", [128, KC * 128], BF16) for i in range(nslot)]
        sq = [sb(f"sq{i}", [128, TT], BF16) for i in range(2)]
        rstd = sb("rstd", [128, TT], F32)
        NTB = 6
        Tb = [sb(f"Tb{i}", [128, 344], F32) for i in range(NTB)]
        NSET = 3
        U = sb("U", [128, NSET, 6, UW], F32)
        UB = sb("UB", [128, NSET, 2, UW], BF16)
        cvt = sb("cvt", [128, NCV], F32)
        dvt = sb("dvt", [128, NLAYER * DV_L], F32)
        ones = sb("ones", [128, 128], BF16)
        invc = sb("invc", [128, 4, 16], F32)
        xnh = sb("xnh", [128, NLAYER, KC, HIST], BF16)
        lst = sb("lst", [128, NLAYER, NHEAD], F32)
        tmp = sb("tmp", [128, 6, 8], F32)
        ps = [es.enter_context(nc.psum_tensor(f"ps{i}", [128, 512], F32)) for i in range(8)]

        bank_free = list(range(8))

        def next_bank():
            assert bank_free, "out of PSUM banks"
            return bank_free.pop(0)

        def free_bank(b):
            assert b not in bank_free
            bank_free.append(b)

        ring = {'emitted': 0, 'acquired': 0, 'free': [True] * nslot}
        total_blocks = NBLK * nst

        def try_emit_dma():
            while ring['emitted'] < total_blocks:
                n = ring['emitted']
                slot = n % nslot
                if not ring['free'][slot]:
                    return
                ring['free'][slot] = False
                (_, _, _, _, _, off, ncols) = layout[n % NBLK]

                def fn(e, slot=slot, off=off, ncols=ncols):
                    return e.dma_start(out=wr[slot][:, 0:ncols], in_=wst[:, off:off + ncols])

                P.op('pool', fn, reads=[('pro',)], writes=[('ws', slot)], sem=f'ws{slot}')
                ring['emitted'] += 1

        def acquire(tag, a=None):
            n = ring['acquired']
            assert n < ring['emitted'], "weight ring too small for emission order"
            ent = layout[n % NBLK]
            assert ent[2] == tag and (a is None or ent[3] == a), (ent, tag, a)
            ring['acquired'] += 1
            return n % nslot

        def release(slot):
            ring['free'][slot] = True
            try_emit_dma()

        P.op('sp', lambda e: e.dma_start(out=cvt[:, :], in_=cv[:, :]), writes=[('cv',)], sem='ldc')
        P.op('dve', lambda e: e.memset(ones[:, :], 1.0), writes=[('ones',)])
        P.op('dve', lambda e: e.memset(lst[:, :, :], 0.0), writes=[('lst',)])
        for g, win in enumerate(POOL_WIN):
            P.op('dve', lambda e, g=g, win=win: e.memset(invc[:, g, :], 1.0 / win), writes=[('invc',)])
            for t in range(win - 1):
                P.op('dve', lambda e, g=g, t=t: e.memset(invc[:, g, t:t + 1], 1.0 / (t + 1)), writes=[('invc',)])
        for l in range(NLAYER):
            c0 = l * CV_L
            d0 = l * DV_L
            apc = cvt[:, c0 + 104:c0 + 112]
            P.op('dve', lambda e, apc=apc: e.tensor_scalar(out=tmp[:, 0, :], in0=apc, scalar1=-1.0, scalar2=None, op0=ALU.mult),
                 reads=[('cv',)], writes=[('tmp', 0)])
            P.op('dve', lambda e, apc=apc: e.tensor_tensor(out=tmp[:, 1, :], in0=tmp[:, 0, :], in1=apc, op=ALU.max),
                 reads=[('tmp', 0), ('cv',)], writes=[('tmp', 1)])
            P.op('act', lambda e: e.activation(out=tmp[:, 2, :], in_=tmp[:, 1, :], func=AF.Exp, scale=-1.0),
                 reads=[('tmp', 1)], writes=[('tmp', 2)])
            P.op('act', lambda e: e.activation(out=tmp[:, 3, :], in_=tmp[:, 2, :], func=AF.Ln, bias=1.0),
                 reads=[('tmp', 2)], writes=[('tmp', 3)])
            P.op('dve', lambda e: e.tensor_scalar(out=tmp[:, 4, :], in0=tmp[:, 0, :], scalar1=0.0, scalar2=None, op0=ALU.max),
                 reads=[('tmp', 0)], writes=[('tmp', 4)])
            P.op('dve', lambda e: e.tensor_tensor(out=tmp[:, 5, :], in0=tmp[:, 4, :], in1=tmp[:, 3, :], op=ALU.add),
                 reads=[('tmp', 4), ('tmp', 3)], writes=[('tmp', 5)])
            P.op('dve', lambda e, d0=d0: e.tensor_scalar(out=dvt[:, d0:d0 + 8], in0=tmp[:, 5, :], scalar1=-8.0, scalar2=None, op0=ALU.mult),
                 reads=[('tmp', 5)], writes=[('dv',)])
            P.op('dve', lambda e, d0=d0: e.tensor_scalar(out=dvt[:, d0 + 8:d0 + 16], in0=tmp[:, 5, :], scalar1=-4.0, scalar2=None, op0=ALU.mult),
                 reads=[('tmp', 5)], writes=[('dv',)])
            P.op('dve', lambda e, d0=d0, c0=c0: e.tensor_tensor(out=dvt[:, d0 + 16:d0 + 24], in0=cvt[:, c0 + 112:c0 + 120],
                                                                 in1=cvt[:, c0 + 120:c0 + 128], op=ALU.mult),
                 reads=[('cv',)], writes=[('dv',)])
            P.op('dve', lambda e, d0=d0, c0=c0: e.tensor_scalar(out=dvt[:, d0 + 24:d0 + 40], in0=cvt[:, c0 + 88:c0 + 104], scalar1=0.5,
                                                                 scalar2=None, op0=ALU.mult),
                 reads=[('cv',)], writes=[('dv',)])
        P.op('dve', lambda e: e.memset(tmp[:, 0, :], 0.0), reads=[('dv',), ('ones',), ('invc',), ('lst',)], writes=[('pro',), ('tmp', 0)])

        try_emit_dma()

        sq_ctr = [0]
        tb_ctr = [0]

        def hreg(k):
            return [('h', k, s) for s in range(3)]

        def norm(gcol0, mode):
            banks = [next_bank() for _ in SUBS]
            for k in range(KC):
                i = k % 2
                if i == 0:
                    P.op('act', lambda e, k=k, i=i: e.activation(out=sq[i][:, :], in_=hT[:, k, :], func=AF.Square),
                         reads=hreg(k), writes=[('sq', i)])
                else:
                    P.op('dve', lambda e, k=k, i=i: e.tensor_tensor(out=sq[i][:, :], in0=hT[:, k, :], in1=hT[:, k, :], op=ALU.mult),
                         reads=hreg(k), writes=[('sq', i)])

                def mm(e, k=k, i=i):
                    last = None
                    for s, (o, w) in enumerate(SUBS):
                        last = e.matmul(ps[banks[s]][:, 0:w], ones[:, :], sq[i][:, o:o + w],
                                        start=(k == 0), stop=(k == KC - 1))
                    return last

                P.op('pe', mm, reads=[('sq', i), ('ones',)], writes=[('ps', b) for b in banks])
            for s, (o, w) in enumerate(SUBS):
                bk = banks[s]
                P.op('act', lambda e, bk=bk, w=w: e.activation(out=ps[bk][:, 0:w], in_=ps[bk][:, 0:w],
                                                               func=AF.Ln, scale=1.0 / D, bias=EPS),
                     reads=[('ps', bk)], writes=[('ps', bk)])
                P.op('act', lambda e, bk=bk, o=o, w=w: e.activation(out=rstd[:, o:o + w], in_=ps[bk][:, 0:w],
                                                                    func=AF.Exp, scale=-0.5),
                     reads=[('ps', bk)], writes=[('rstd', s)])
                free_bank(bk)
            for k in range(KC):
                if mode == 'xn':
                    P.op('dve', lambda e, k=k: e.scalar_tensor_tensor(
                        out=xn[:, k, HIST:HIST + TT], in0=hT[:, k, :], scalar=cvt[:, gcol0 + k:gcol0 + k + 1],
                        in1=rstd[:, :], op0=ALU.mult, op1=ALU.mult),
                        reads=hreg(k) + [('rstd', s) for s in range(3)] + [('cv',)],
                        writes=[('xn', k, s) for s in range(3)])
                else:
                    P.op('dve', lambda e, k=k: e.scalar_tensor_tensor(
                        out=hT[:, k, :], in0=hT[:, k, :], scalar=cvt[:, gcol0 + k:gcol0 + k + 1],
                        in1=rstd[:, :], op0=ALU.mult, op1=ALU.mult),
                        reads=hreg(k) + [('rstd', s) for s in range(3)] + [('cv',)],
                        writes=hreg(k))

        def xn_regs():
            return [('xn', k, s) for k in range(KC) for s in range(3)]

        def ffn(l, which):
            gcol0 = l * CV_L + (0 if which == 1 else 32)
            norm(gcol0, 'xn')
            for q in range(Q):
                for fi in range(FQ):
                    f = q * FQ + fi
                    bks = []
                    if f == 0:
                        sg, su = acquire('g', f), acquire('u', f)
                        bkg = [next_bank() for _ in SUBS]
                        bku = [next_bank() for _ in SUBS]
                        for k in range(KC):
                            def mmk(e, sg=sg, su=su, bkg=bkg, bku=bku, k=k):
                                last = None
                                for slot, banks in ((sg, bkg), (su, bku)):
                                    for s, (o, w) in enumerate(SUBS):
                                        last = e.matmul(ps[banks[s]][:, 0:w], wr[slot][:, k * 128:(k + 1) * 128],
                                                        xn[:, k, HIST + o:HIST + o + w], start=(k == 0), stop=(k == KC - 1))
                                return last
                            P.op('pe', mmk, reads=[('ws', sg), ('ws', su)] + [('xn', k, s) for s in range(3)],
                                 writes=[('ps', b) for b in bkg + bku])
                        release(sg)
                        release(su)
                        bks = [bkg, bku]
                    else:
                        for tag in ('g', 'u'):
                            slot = acquire(tag, f)
                            banks = [next_bank() for _ in SUBS]

                            def mm(e, slot=slot, banks=banks):
                                last = None
                                for k in range(KC):
                                    for s, (o, w) in enumerate(SUBS):
                                        last = e.matmul(ps[banks[s]][:, 0:w], wr[slot][:, k * 128:(k + 1) * 128],
                                                        xn[:, k, HIST + o:HIST + o + w], start=(k == 0), stop=(k == KC - 1))
                                return last

                            P.op('pe', mm, reads=[('ws', slot)] + xn_regs(), writes=[('ps', b) for b in banks])
                            release(slot)
                            bks.append(banks)
                    for s, (o, w) in enumerate(SUBS):
                        ti = tb_ctr[0] % NTB
                        tb_ctr[0] += 1
                        bg, bu = bks[0][s], bks[1][s]
                        P.op('act', lambda e, ti=ti, bg=bg, w=w: e.activation(out=Tb[ti][:, 0:w], in_=ps[bg][:, 0:w], func=AF.Silu),
                             reads=[('ps', bg)], writes=[('T', ti)])
                        P.op('dve', lambda e, ti=ti, bu=bu, fi=fi, o=o, w=w: e.tensor_tensor(
                            out=act[:, fi, o:o + w], in0=Tb[ti][:, 0:w], in1=ps[bu][:, 0:w], op=ALU.mult),
                            reads=[('T', ti), ('ps', bu)], writes=[('act', fi, s)])
                        free_bank(bg)
                        free_bank(bu)
                for d in range(KC):
                    slot = acquire('o', q)
                    banks = [next_bank() for _ in SUBS]

                    def mm(e, slot=slot, banks=banks):
                        last = None
                        for kk in range(FQ):
                            for s, (o, w) in enumerate(SUBS):
                                last = e.matmul(ps[banks[s]][:, 0:w], wr[slot][:, kk * 128:(kk + 1) * 128],
                                                act[:, kk, o:o + w], start=(kk == 0), stop=(kk == FQ - 1))
                        return last

                    P.op('pe', mm, reads=[('ws', slot)] + [('act', kk, s) for kk in range(FQ) for s in range(3)],
                         writes=[('ps', b) for b in banks])
                    release(slot)
                    for s, (o, w) in enumerate(SUBS):
                        b = banks[s]
                        P.op('dve', lambda e, b=b, d=d, o=o, w=w: e.scalar_tensor_tensor(
                            out=hT[:, d, o:o + w], in0=ps[b][:, 0:w], scalar=0.5, in1=hT[:, d, o:o + w],
                            op0=ALU.mult, op1=ALU.add),
                            reads=[('ps', b), ('h', d, s)], writes=[('h', d, s)])
                        free_bank(b)

        def mixer(l, st):
            c0 = l * CV_L
            d0 = l * DV_L
            if st == 0:
                P.op('dve', lambda e: e.memset(xn[:, :, 0:HIST], 0.0), writes=[('xnh',)])
            else:
                P.op('act', lambda e: e.activation(out=xn[:, :, 0:HIST], in_=xnh[:, l, :, :], func=AF.Identity),
                     reads=[('xnhist', l)], writes=[('xnh',)])
            norm(c0 + 16, 'xn')
            P.op('act', lambda e: e.activation(out=xnh[:, l, :, :], in_=xn[:, :, TT:TT + HIST], func=AF.Identity),
                 reads=[('xn', k, 2) for k in range(KC)], writes=[('xnhist', l)])

            def xin_regs(s, hist):
                r = [('xn', k, s) for k in range(KC)]
                if hist:
                    if s == 0:
                        r.append(('xnh',))
                    else:
                        r += [('xn', k, s - 1) for k in range(KC)]
                return r

            def proj(slot, bank, s, nh, split_k=False):
                o, w = SUBS[s]

                def mm(e, ks=range(KC)):
                    last = None
                    for k in ks:
                        last = e.matmul(ps[bank][:, 0:nh + w], wr[slot][:, k * 128:(k + 1) * 128],
                                        xn[:, k, HIST + o - nh:HIST + o + w], start=(k == 0), stop=(k == KC - 1))
                    return last

                if split_k:
                    for k in range(KC):
                        r = [('ws', slot), ('xn', k, s)]
                        if nh > 0:
                            r.append(('xnh',) if s == 0 else ('xn', k, s - 1))
                        P.op('pe', lambda e, k=k: mm(e, [k]), reads=r, writes=[('ps', bank)])
                else:
                    P.op('pe', mm, reads=[('ws', slot)] + xin_regs(s, nh > 0), writes=[('ps', bank)])

            units = [('lru', j, s) for j in range(NHEAD) for s in range(3)] + \
                    [('pool', g, s) for g in range(4) for s in range(3)]
            nu = len(units)
            blk = {}
            ctx = [dict() for _ in units]
            zgb = {}

            def R_(us, i):
                return ('U', us, i)

            def A_pe(u):
                kind, j, s = units[u]
                if kind == 'lru':
                    if s == 0:
                        blk['zx', j] = acquire('zx', j)
                    bzx = next_bank()
                    ctx[u]['bzx'] = bzx
                    proj(blk['zx', j], bzx, s, 3, split_k=(u == 0))
                    if u == 0:
                        zs = acquire('zg', 0)
                        zgb[0] = [next_bank() for _ in range(3)]
                        for s2 in range(3):
                            proj(zs, zgb[0][s2], s2, 0)
                        release(zs)
                    if j + 1 < NHEAD:
                        if s == 0:
                            blk['zg', j + 1] = acquire('zg', j + 1)
                            zgb[j + 1] = [None] * 3
                        zgb[j + 1][s] = next_bank()
                        proj(blk['zg', j + 1], zgb[j + 1][s], s, 0)
                        if s == 2:
                            release(blk['zg', j + 1])
                    if s == 2:
                        release(blk['zx', j])
                else:
                    g = j
                    if s == 0:
                        blk['zp', 2 * g] = acquire('zp', 2 * g)
                        blk['zp', 2 * g + 1] = acquire('zp', 2 * g + 1)
                    bz = [next_bank(), next_bank()]
                    ctx[u]['bz'] = bz
                    for cc in range(2):
                        proj(blk['zp', 2 * g + cc], bz[cc], s, 15)
                    if s == 2:
                        release(blk['zp', 2 * g])
                        release(blk['zp', 2 * g + 1])

            def A_evac(u):
                kind, j, s = units[u]
                us = u % NSET
                o, w = SUBS[s]
                if kind == 'lru':
                    bzx = ctx[u]['bzx']
                    ZX, XC = U[:, us, 0, :], U[:, us, 1, :]
                    cw3 = cvt[:, c0 + 48 + 3 * 8 + j:c0 + 48 + 3 * 8 + j + 1]
                    cb = cvt[:, c0 + 80 + j:c0 + 81 + j]
                    P.op('dve', lambda e: e.tensor_copy(out=ZX[:, 0:3 + w], in_=ps[bzx][:, 0:3 + w]),
                         reads=[('ps', bzx)], writes=[R_(us, 0)])
                    free_bank(bzx)
                    P.op('act', lambda e: e.activation(out=XC[:, 0:w], in_=ZX[:, 3:3 + w], func=AF.Identity, scale=cw3, bias=cb),
                         reads=[R_(us, 0), ('cv',)], writes=[R_(us, 1)])
                    gh = None
                    if u == 0:
                        gh = 0
                    elif s == 2 and j + 1 < NHEAD:
                        gh = j + 1
                    if gh is not None:
                        bzg = zgb[gh]
                        for s2 in range(3):
                            w2 = SUBS[s2][1]
                            gi = (3 * gh + s2) % NTB
                            P.op('act', lambda e, s2=s2, w2=w2, gi=gi, bzg=bzg: e.activation(
                                out=Tb[gi][:, 0:w2], in_=ps[bzg[s2]][:, 0:w2], func=AF.Gelu_apprx_tanh),
                                reads=[('ps', bzg[s2])], writes=[('T', gi)])
                            free_bank(bzg[s2])
                else:
                    bz = ctx[u]['bz']
                    for cc in range(2):
                        P.op('act', lambda e, cc=cc: e.activation(out=U[:, us, cc, 0:15 + w], in_=ps[bz[cc]][:, 0:15 + w], func=AF.Identity),
                             reads=[('ps', bz[cc])], writes=[R_(us, cc)])
                        free_bank(bz[cc])

            def B1_dve(u):
                kind, j, s = units[u]
                us = u % NSET
                o, w = SUBS[s]
                if kind == 'lru':
                    ZX, XC = U[:, us, 0, :], U[:, us, 1, :]
                    XCB = UB[:, us, 0, :]
                    cw = [cvt[:, c0 + 48 + kk * 8 + j:c0 + 48 + kk * 8 + j + 1] for kk in range(4)]
                    for kk in (2, 1, 0):
                        P.op('dve', lambda e, kk=kk: e.scalar_tensor_tensor(out=XC[:, 0:w], in0=ZX[:, kk:kk + w], scalar=cw[kk],
                                                                            in1=XC[:, 0:w], op0=ALU.mult, op1=ALU.add),
                             reads=[R_(us, 0), R_(us, 1), ('cv',)], writes=[R_(us, 1)])
                    P.op('dve', lambda e: e.tensor_tensor(out=XCB[:, 0:w], in0=XC[:, 0:w], in1=XC[:, 0:w], op=ALU.max),
                         reads=[R_(us, 1)], writes=[('UB', us, 0)])
                else:
                    g = j
                    win = POOL_WIN[g]
                    nlev = g + 1
                    n = 15 + w
                    for cc in range(2):
                        Uc = U[:, us, cc, :]
                        A = U[:, us, 2 + 2 * cc, :]
                        B = U[:, us, 3 + 2 * cc, :]
                        rA, rB = R_(us, 2 + 2 * cc), R_(us, 3 + 2 * cc)
                        src_, rsrc = Uc, R_(us, cc)
                        for lev in range(nlev):
                            sh = 1 << lev
                            lo = (1 << (lev + 1)) - 1
                            dst, rdst = (A, rA) if lev % 2 == 0 else (B, rB)
                            P.op('dve', lambda e, src_=src_, dst=dst, sh=sh, lo=lo: e.tensor_tensor(
                                out=dst[:, lo:n], in0=src_[:, lo:n], in1=src_[:, lo - sh:n - sh], op=ALU.add),
                                reads=[rsrc], writes=[rdst])
                            src_, rsrc = dst, rdst
                        Dd = UB[:, us, cc, :]
                        P.op('dve', lambda e, src_=src_, Uc=Uc, Dd=Dd: e.scalar_tensor_tensor(
                            out=Dd[:, 0:w], in0=src_[:, 15:15 + w], scalar=1.0 / win, in1=Uc[:, 15:15 + w],
                            op0=ALU.mult, op1=ALU.subtract),
                            reads=[rsrc, R_(us, cc)], writes=[('UB', us, cc)])
                        if st == 0 and s == 0:
                            m = win - 1
                            other, rother = (B, rB) if src_ is A else (A, rA)
                            P.op('dve', lambda e, src_=src_, other=other, m=m: e.tensor_tensor(
                                out=other[:, 0:m], in0=src_[:, 15:15 + m], in1=invc[:, g, 0:m], op=ALU.mult),
                                reads=[rsrc, ('invc',)], writes=[rother])
                            P.op('dve', lambda e, other=other, Uc=Uc, Dd=Dd, m=m: e.tensor_tensor(
                                out=Dd[:, 0:m], in0=other[:, 0:m], in1=Uc[:, 15:15 + m], op=ALU.subtract),
                                reads=[rother, R_(us, cc), ('UB', us, cc)], writes=[('UB', us, cc)])

            def B1_pe(u):
                kind, j, s = units[u]
                us = u % NSET
                o, w = SUBS[s]
                if kind == 'lru':
                    XCB = UB[:, us, 0, :]
                    if s == 0:
                        blk['gt', j] = acquire('gt', j)
                    gs = blk['gt', j]
                    br, bi = next_bank(), next_bank()
                    ctx[u]['br'], ctx[u]['bi'] = br, bi

                    def mm(e):
                        e.matmul(ps[br][:, 0:w], wr[gs][:, 0:128], XCB[:, 0:w], start=True, stop=True)
                        return e.matmul(ps[bi][:, 0:w], wr[gs][:, 128:256], XCB[:, 0:w], start=True, stop=True)

                    P.op('pe', mm, reads=[('ws', gs), ('UB', us, 0)], writes=[('ps', br), ('ps', bi)])
                    if s == 2:
                        release(gs)
                else:
                    g = j
                    if s == 0:
                        blk['pw', g] = acquire('pw', g)
                    pslot = blk['pw', g]
                    by = [next_bank(), next_bank()]
                    ctx[u]['by'] = by

                    def mm(e):
                        last = None
                        for jc in range(2):
                            for kc in range(2):
                                last = e.matmul(ps[by[jc]][:, 0:w], wr[pslot][:, kc * 256 + jc * 128:kc * 256 + jc * 128 + 128],
                                                UB[:, us, kc, 0:w], start=(kc == 0), stop=(kc == 1))
                        return last

                    P.op('pe', mm, reads=[('ws', pslot), ('UB', us, 0), ('UB', us, 1)], writes=[('ps', by[0]), ('ps', by[1])])
                    if s == 2:
                        release(pslot)

            def B1_act(u):
                kind, j, s = units[u]
                us = u % NSET
                o, w = SUBS[s]
                if kind == 'lru':
                    Rr, Ii, Mm = U[:, us, 2, :], U[:, us, 3, :], U[:, us, 4, :]
                    br, bi = ctx[u]['br'], ctx[u]['bi']
                    hba = dvt[:, d0 + 24 + j:d0 + 25 + j]
                    hbx = dvt[:, d0 + 32 + j:d0 + 33 + j]
                    cc1 = dvt[:, d0 + j:d0 + j + 1]
                    cch = dvt[:, d0 + 8 + j:d0 + 9 + j]
                    P.op('act', lambda e: e.activation(out=Rr[:, 0:w], in_=ps[br][:, 0:w], func=AF.Tanh, scale=0.5, bias=hba),
                         reads=[('ps', br), ('dv',)], writes=[R_(us, 2)])
                    P.op('act', lambda e: e.activation(out=Ii[:, 0:w], in_=ps[bi][:, 0:w], func=AF.Tanh, scale=0.5, bias=hbx),
                         reads=[('ps', bi), ('dv',)], writes=[R_(us, 3)])
                    free_bank(br)
                    free_bank(bi)
                    P.op('act', lambda e: e.activation(out=Mm[:, 0:w], in_=Rr[:, 0:w], func=AF.Exp, scale=cc1, bias=cc1),
                         reads=[R_(us, 2), ('dv',)], writes=[R_(us, 4)])
                    P.op('act', lambda e: e.activation(out=Rr[:, 0:w], in_=Rr[:, 0:w], func=AF.Exp, scale=cch, bias=cch),
                         reads=[R_(us, 2), ('dv',)], writes=[R_(us, 2)])
                    P.op('act', lambda e: e.activation(out=Mm[:, 0:w], in_=Mm[:, 0:w], func=AF.Sqrt, scale=-0.25, bias=0.25),
                         reads=[R_(us, 4)], writes=[R_(us, 4)])
                else:
                    g = j
                    by = ctx[u]['by']
                    for jc in range(2):
                        c = 2 * g + jc
                        sc = cvt[:, c0 + 120 + c:c0 + 121 + c]
                        bsc = dvt[:, d0 + 16 + c:d0 + 17 + c]
                        P.op('act', lambda e, jc=jc, c=c, sc=sc, bsc=bsc: e.activation(
                            out=act[:, 8 + c, o:o + w], in_=ps[by[jc]][:, 0:w], func=AF.Identity, scale=sc, bias=bsc),
                            reads=[('ps', by[jc]), ('cv',), ('dv',)], writes=[('act', 8 + c, s)])
                        free_bank(by[jc])

            def B2_dve(u):
                kind, j, s = units[u]
                if kind != 'lru':
                    return
                us = u % NSET
                o, w = SUBS[s]
                XC, Rr, Ii, Mm, HS = [U[:, us, i, :] for i in (1, 2, 3, 4, 5)]
                P.op('dve', lambda e: e.scalar_tensor_tensor(out=Mm[:, 0:w], in0=Ii[:, 0:w], scalar=1.0, in1=Mm[:, 0:w],
                                                             op0=ALU.add, op1=ALU.mult),
                     reads=[R_(us, 3), R_(us, 4)], writes=[R_(us, 4)])
                P.op('dve', lambda e: e.tensor_tensor(out=Mm[:, 0:w], in0=Mm[:, 0:w], in1=XC[:, 0:w], op=ALU.mult),
                     reads=[R_(us, 4), R_(us, 1)], writes=[R_(us, 4)])
                if s == 0:
                    init = lst[:, l, j:j + 1]
                    rinit = ('lst',)
                else:
                    pus = (u - 1) % NSET
                    pw_ = SUBS[s - 1][1]
                    init = U[:, pus, 5, pw_ - 1:pw_]
                    rinit = R_(pus, 5)
                P.op('dve', lambda e: e.tensor_tensor_scan(out=HS[:, 0:w], data0=Rr[:, 0:w], data1=Mm[:, 0:w], initial=init,
                                                           op0=ALU.mult, op1=ALU.add),
                     reads=[R_(us, 2), R_(us, 4), rinit], writes=[R_(us, 5)])
                if s == 2:
                    P.op('act', lambda e: e.activation(out=lst[:, l, j:j + 1], in_=HS[:, w - 1:w], func=AF.Identity),
                         reads=[R_(us, 5)], writes=[('lst',)])
                gi = u % NTB
                P.op('dve', lambda e: e.tensor_tensor(out=act[:, j, o:o + w], in0=HS[:, 0:w], in1=Tb[gi][:, 0:w], op=ALU.mult),
                     reads=[R_(us, 5), ('T', gi)], writes=[('act', j, s)])

            for t in range(nu + 2):
                if t < nu:
                    A_pe(t)
                if 0 <= t - 1 < nu:
                    B1_dve(t - 1)
                    B1_pe(t - 1)
                if t < nu:
                    A_evac(t)
                if 0 <= t - 1 < nu:
                    B1_act(t - 1)
                if 0 <= t - 2 < nu:
                    B2_dve(t - 2)

            for d in range(KC):
                slot = acquire('wo', d)
                banks = [next_bank() for _ in SUBS]

                def mm(e, slot=slot, banks=banks):
                    last = None
                    for k in range(KC):
                        for s, (o, w) in enumerate(SUBS):
                            last = e.matmul(ps[banks[s]][:, 0:w], wr[slot][:, k * 128:(k + 1) * 128],
                                            act[:, k, o:o + w], start=(k == 0), stop=(k == KC - 1))
                    return last

                P.op('pe', mm, reads=[('ws', slot)] + [('act', k, s) for k in range(KC) for s in range(3)],
                     writes=[('ps', b) for b in banks])
                release(slot)
                for s, (o, w) in enumerate(SUBS):
                    b = banks[s]
                    P.op('dve', lambda e, b=b, d=d, o=o, w=w: e.tensor_tensor(
                        out=hT[:, d, o:o + w], in0=ps[b][:, 0:w], in1=hT[:, d, o:o + w], op=ALU.add),
                        reads=[('ps', b), ('h', d, s)], writes=[('h', d, s)])
                    free_bank(b)

        for st in range(nst):
            ld_eng = 'sp' if st == 0 else 'act'
            for k in range(KC):
                P.op(ld_eng, lambda e, k=k, st=st: e.dma_start(out=hT[:, k, :], in_=xin[:, k, st * TT:(st + 1) * TT]),
                     writes=hreg(k), sem=f'ld{k}')
            for (l, kind) in phases:
                if kind == 'mix':
                    mixer(l, st)
                else:
                    ffn(l, 1 if kind == 'ffn1' else 2)
            norm(NLAYER * CV_L, 'final')
            c_lo = NMETA if st == 0 else 0
            t_lo = st * TT + c_lo - NMETA
            ncol = TT - c_lo
            for k in range(KC):
                P.op('sp', lambda e, k=k, c_lo=c_lo, t_lo=t_lo, ncol=ncol: e.dma_start(
                    out=y[:, k, t_lo:t_lo + ncol], in_=hT[:, k, c_lo:c_lo + ncol]),
                    reads=hreg(k), sem=f'st{k}')
        for k in range(KC):
            P.final_waits['sp'].append((f'st{k}', P.cnt[f'st{k}']))
        assert ring['acquired'] == total_blocks and ring['emitted'] == total_blocks

        sems = {}
        for key in P.cnt:
            sems[key] = es.enter_context(nc.semaphore(f"s_{key}"))
        block = es.enter_context(nc.Block())
        P.emit(block, sems)
    return nc


def _kmaj(w):
    K = w.shape[0] // 128
    return np.ascontiguousarray(w.reshape(K, 128, w.shape[1]).transpose(1, 0, 2)).reshape(128, K * w.shape[1])


def build_wstream(inp, phases):
    layout, WTOT = stream_layout(phases)
    wst = np.empty((128, WTOT), dtype=np.float32)
    for (l, kind, tag, a, b, off, n) in layout:
        if kind in ('ffn1', 'ffn2'):
            w_in = inp['ffn1_w_in' if kind == 'ffn1' else 'ffn2_w_in'][l]
            w_out = inp['ffn1_w_out' if kind == 'ffn1' else 'ffn2_w_out'][l]
            if tag == 'g':
                blkv = _kmaj(w_in[:, a * 128:(a + 1) * 128])
            elif tag == 'u':
                blkv = _kmaj(w_in[:, DFF + a * 128:DFF + (a + 1) * 128])
            else:
                q, d = a, b
                blkv = _kmaj(w_out[q * FQ * 128:(q + 1) * FQ * 128, d * 128:(d + 1) * 128])
        else:
            w_in = inp['w_in'][l]
            if tag == 'zx':
                blkv = _kmaj(w_in[:, a * 128:(a + 1) * 128])
            elif tag == 'zg':
                blkv = _kmaj(w_in[:, 1024 + a * 128:1024 + (a + 1) * 128])
            elif tag == 'zp':
                blkv = _kmaj(w_in[:, 2048 + a * 128:2048 + (a + 1) * 128])
            elif tag == 'gt':
                blkv = np.concatenate([inp['lru_wa'][l, a], inp['lru_wx'][l, a]], axis=1)
            elif tag == 'pw':
                blkv = _kmaj(inp['pool_w'][l, a])
            else:
                blkv = _kmaj(inp['w_out'][l][:, a * 128:(a + 1) * 128])
        assert blkv.shape == (128, n), (blkv.shape, n, tag)
        wst[:, off:off + n] = blkv
    return wst


def build_cvec(inp):
    cvv = np.zeros((128, NCV), dtype=np.float32)

    def col(v):
        return np.asarray(v, dtype=np.float32).reshape(-1, 128).T

    for l in range(NLAYER):
        c0 = l * CV_L
        cvv[:, c0 + 0:c0 + 16] = col(inp['ffn1_norm'][l])
        cvv[:, c0 + 16:c0 + 32] = col(inp['mix_norm'][l])
        cvv[:, c0 + 32:c0 + 48] = col(inp['ffn2_norm'][l])
        for kk in range(4):
            cvv[:, c0 + 48 + kk * 8:c0 + 56 + kk * 8] = col(inp['conv_w'][l, kk])
        cvv[:, c0 + 80:c0 + 88] = col(inp['conv_b'][l])
        cvv[:, c0 + 88:c0 + 96] = col(inp['lru_ba'][l])
        cvv[:, c0 + 96:c0 + 104] = col(inp['lru_bx'][l])
        cvv[:, c0 + 104:c0 + 112] = col(inp['lru_a_param'][l])
        cvv[:, c0 + 112:c0 + 120] = col(inp['pool_b'][l])
        cvv[:, c0 + 120:c0 + 128] = col(inp['pool_scale'][l])
    cvv[:, NLAYER * CV_L:NLAYER * CV_L + KC] = col(inp['final_norm'])
    return cvv


def build_xin(x_b, meta):
    hfull = np.concatenate([meta, x_b], axis=0)
    return np.ascontiguousarray(hfull.T.reshape(KC, 128, TTOT).transpose(1, 0, 2))


_CACHE = {}


def kernel(**inputs):
    inp = {k: np.asarray(v) for k, v in inputs.items()}
    x = inp['x'].astype(np.float32, copy=False)
    B = x.shape[0]
    phases = phases_all()
    if 'nc' not in _CACHE:
        _CACHE['nc'] = build_program(NST, phases, NSLOT)
    nc = _CACHE['nc']
    wst = build_wstream(inp, phases)
    cvv = build_cvec(inp)
    meta = inp['meta_tokens'].astype(np.float32, copy=False)
    in_maps = [{"xin": build_xin(x[b], meta), "wst": wst, "cv": cvv} for b in range(B)]
    res = run_bass_kernel_spmd(nc, in_maps, core_ids=list(range(B)))
    out = np.empty((B, SEQ, D), dtype=np.float32)
    for b in range(B):
        yb = np.asarray(res.results[b]["y"])
        out[b] = yb.transpose(2, 1, 0).reshape(SEQ, D)
    return out
```

```python
import numpy as np
from contextlib import ExitStack

import concourse.bass as bass
import concourse.mybir as mybir
from concourse.bass_utils import run_bass_kernel_spmd

F32 = mybir.dt.float32
BF16 = mybir.dt.bfloat16
AF = mybir.ActivationFunctionType
ALU = mybir.AluOpType

D = 2048
KC = 16
SEQ = 4096
NMETA = 16
TTOT = SEQ + NMETA
DFF = 5632
FC = DFF // 128
Q = 4
FQ = FC // Q
NST = 4
TT = TTOT // NST
SUBS = [(0, 343), (343, 343), (686, 342)]
HIST = 16
NSLOT = 6
EPS = 1e-6
NHEAD = 8
POOL_WIN = (2, 4, 8, 16)
NLAYER = 2
UW = 360

CV_L = 128
NCV = CV_L * NLAYER + KC
DV_L = 40


def phases_all():
    ph = []
    for l in range(NLAYER):
        ph += [(l, 'ffn1'), (l, 'mix'), (l, 'ffn2')]
    return ph


def phase_blocks(kind):
    bl = []
    if kind in ('ffn1', 'ffn2'):
        for q in range(Q):
            for fi in range(FQ):
                f = q * FQ + fi
                bl.append(('g', f, 0, KC * 128))
                bl.append(('u', f, 0, KC * 128))
            for d in range(KC):
                bl.append(('o', q, d, FQ * 128))
    else:
        for j in range(NHEAD):
            bl.append(('zx', j, 0, KC * 128))
            if j == 0:
                bl.append(('zg', 0, 0, KC * 128))
            if j + 1 < NHEAD:
                bl.append(('zg', j + 1, 0, KC * 128))
            bl.append(('gt', j, 0, 256))
        for g in range(4):
            bl.append(('zp', 2 * g, 0, KC * 128))
            bl.append(('zp', 2 * g + 1, 0, KC * 128))
            bl.append(('pw', g, 0, 512))
        for d in range(KC):
            bl.append(('wo', d, 0, KC * 128))
    return bl


def stream_layout(phases):
    out = []
    off = 0
    for (l, kind) in phases:
        for (tag, a, b, n) in phase_blocks(kind):
            out.append((l, kind, tag, a, b, off, n))
            off += n
    return out, off


class Prog:
    ENGS = ('pe', 'act', 'dve', 'pool', 'sp')

    def __init__(self):
        self.q = {e: [] for e in self.ENGS}
        self.cnt = {}
        self.waited = {e: {} for e in self.ENGS}
        self.lastw = {}
        self.readers = {}
        self.final_waits = {e: [] for e in self.ENGS}

    def op(self, eng, fn, reads=(), writes=(), sem=None, pre_inc=0):
        key = sem if sem is not None else eng
        inc = 16 if sem is not None else 1
        deps = {}

        def add(tok):
            if tok is None:
                return
            k, v = tok
            if deps.get(k, 0) < v:
                deps[k] = v

        for r in reads:
            add(self.lastw.get(r))
        for w in writes:
            add(self.lastw.get(w))
            rd = self.readers.get(w)
            if rd:
                for tok in rd.values():
                    add(tok)
        waits = []
        for k, v in deps.items():
            if k == 'pe' and eng == 'pe':
                continue
            if self.waited[eng].get(k, 0) >= v:
                continue
            self.waited[eng][k] = v
            waits.append((k, v))
        self.cnt[key] = self.cnt.get(key, 0) + pre_inc + inc
        tok = (key, self.cnt[key])
        self.q[eng].append((waits, fn, key, inc))
        for r in reads:
            self.readers.setdefault(r, {})[key] = tok
        for w in writes:
            self.lastw[w] = tok
            self.readers[w] = {}
        return tok

    def emit(self, block, sems):
        names = {'pe': 'tensor', 'act': 'scalar', 'dve': 'vector', 'pool': 'gpsimd', 'sp': 'sync'}
        for eng in self.ENGS:
            ops = self.q[eng]
            fw = self.final_waits[eng]

            def body(e, ops=ops, fw=fw):
                for waits, fn, key, inc in ops:
                    for k, v in waits:
                        e.wait_ge(sems[k], v)
                    inst = fn(e)
                    inst.then_inc(sems[key], inc)
                for k, v in fw:
                    e.wait_ge(sems[k], v)

            getattr(block, names[eng])(body)


def build_program(nst=NST, phases=None, nslot=NSLOT):
    if phases is None:
        phases = phases_all()
    layout, WTOT = stream_layout(phases)
    NBLK = len(layout)

    nc = bass.Bass("TRN2", target_bir_lowering=False)
    xin = nc.dram_tensor("xin", [128, KC, TTOT], F32, kind="ExternalInput").ap()
    wst = nc.dram_tensor("wst", [128, WTOT], F32, kind="ExternalInput").ap()
    cv = nc.dram_tensor("cv", [128, NCV], F32, kind="ExternalInput").ap()
    y = nc.dram_tensor("y", [128, KC, SEQ], F32, kind="ExternalOutput").ap()

    P = Prog()
    es = ExitStack()
    with es:
        def sb(name, shape, dt):
            return es.enter_context(nc.sbuf_tensor(name, shape, dt))

        hT = sb("hT", [128, KC, TT], F32)
        xn = sb("xn", [128, KC, HIST + TT], BF16)
        act = sb("act", [128, KC, TT], BF16)
        wr = [sb(f"wr{i}", [128, KC * 128], BF16) for i in range(nslot)]
        sq = [sb(f"sq{i}", [128, TT], BF16) for i in range(2)]
        rstd = sb("rstd", [128, TT], F32)
        NTB = 6
        Tb = [sb(f"Tb{i}", [128, 344], F32) for i in range(NTB)]
        NSET = 3
        U = sb("U", [128, NSET, 6, UW], F32)
        UB = sb("UB", [128, NSET, 2, UW], BF16)
        cvt = sb("cvt", [128, NCV], F32)
        dvt = sb("dvt", [128, NLAYER * DV_L], F32)
        ones = sb("ones", [128, 128], BF16)
        invc = sb("invc", [128, 4, 16], F32)
        xnh = sb("xnh", [128, NLAYER, KC, HIST], BF16)
        lst = sb("lst", [128, NLAYER, NHEAD], F32)
        tmp = sb("tmp", [128, 6, 8], F32)
        ps = [es.enter_context(nc.psum_tensor(f"ps{i}", [128, 512], F32)) for i in range(8)]

        bank_free = list(range(8))

        def next_bank():
            assert bank_free, "out of PSUM banks"
            return bank_free.pop(0)

        def free_bank(b):
            assert b not in bank_free
            bank_free.append(b)

        ring = {'emitted': 0, 'acquired': 0, 'free': [True] * nslot}
        total_blocks = NBLK * nst

        def try_emit_dma():
            while ring['emitted'] < total_blocks:
                n = ring['emitted']
                slot = n % nslot
                if not ring['free'][slot]:
                    return
                ring['free'][slot] = False
                (_, _, _, _, _, off, ncols) = layout[n % NBLK]

                def fn(e, slot=slot, off=off, ncols=ncols):
                    return e.dma_start(out=wr[slot][:, 0:ncols], in_=wst[:, off:off + ncols])

                P.op('pool', fn, reads=[('pro',)], writes=[('ws', slot)], sem=f'ws{slot}')
                ring['emitted'] += 1

        def acquire(tag, a=None):
            n = ring['acquired']
            assert n < ring['emitted'], "weight ring too small for emission order"
            ent = layout[n % NBLK]
            assert ent[2] == tag and (a is None or ent[3] == a), (ent, tag, a)
            ring['acquired'] += 1
            return n % nslot

        def release(slot):
            ring['free'][slot] = True
            try_emit_dma()

        P.op('sp', lambda e: e.dma_start(out=cvt[:, :], in_=cv[:, :]), writes=[('cv',)], sem='ldc')
        P.op('dve', lambda e: e.memset(ones[:, :], 1.0), writes=[('ones',)])
        P.op('dve', lambda e: e.memset(lst[:, :, :], 0.0), writes=[('lst',)])
        for g, win in enumerate(POOL_WIN):
            P.op('dve', lambda e, g=g, win=win: e.memset(invc[:, g, :], 1.0 / win), writes=[('invc',)])
            for t in range(win - 1):
                P.op('dve', lambda e, g=g, t=t: e.memset(invc[:, g, t:t + 1], 1.0 / (t + 1)), writes=[('invc',)])
        for l in range(NLAYER):
            c0 = l * CV_L
            d0 = l * DV_L
            apc = cvt[:, c0 + 104:c0 + 112]
            P.op('dve', lambda e, apc=apc: e.tensor_scalar(out=tmp[:, 0, :], in0=apc, scalar1=-1.0, scalar2=None, op0=ALU.mult),
                 reads=[('cv',)], writes=[('tmp', 0)])
            P.op('dve', lambda e, apc=apc: e.tensor_tensor(out=tmp[:, 1, :], in0=tmp[:, 0, :], in1=apc, op=ALU.max),
                 reads=[('tmp', 0), ('cv',)], writes=[('tmp', 1)])
            P.op('act', lambda e: e.activation(out=tmp[:, 2, :], in_=tmp[:, 1, :], func=AF.Exp, scale=-1.0),
                 reads=[('tmp', 1)], writes=[('tmp', 2)])
            P.op('act', lambda e: e.activation(out=tmp[:, 3, :], in_=tmp[:, 2, :], func=AF.Ln, bias=1.0),
                 reads=[('tmp', 2)], writes=[('tmp', 3)])
            P.op('dve', lambda e: e.tensor_scalar(out=tmp[:, 4, :], in0=tmp[:, 0, :], scalar1=0.0, scalar2=None, op0=ALU.max),
                 reads=[('tmp', 0)], writes=[('tmp', 4)])
            P.op('dve', lambda e: e.tensor_tensor(out=tmp[:, 5, :], in0=tmp[:, 4, :], in1=tmp[:, 3, :], op=ALU.add),
                 reads=[('tmp', 4), ('tmp', 3)], writes=[('tmp', 5)])
            P.op('dve', lambda e, d0=d0: e.tensor_scalar(out=dvt[:, d0:d0 + 8], in0=tmp[:, 5, :], scalar1=-8.0, scalar2=None, op0=ALU.mult),
                 reads=[('tmp', 5)], writes=[('dv',)])
            P.op('dve', lambda e, d0=d0: e.tensor_scalar(out=dvt[:, d0 + 8:d0 + 16], in0=tmp[:, 5, :], scalar1=-4.0, scalar2=None, op0=ALU.mult),
                 reads=[('tmp', 5)], writes=[('dv',)])
            P.op('dve', lambda e, d0=d0, c0=c0: e.tensor_tensor(out=dvt[:, d0 + 16:d0 + 24], in0=cvt[:, c0 + 112:c0 + 120],
                                                                 in1=cvt[:, c0 + 120:c0 + 128], op=ALU.mult),
                 reads=[('cv',)], writes=[('dv',)])
            P.op('dve', lambda e, d0=d0, c0=c0: e.tensor_scalar(out=dvt[:, d0 + 24:d0 + 40], in0=cvt[:, c0 + 88:c0 + 104], scalar1=0.5,
                                                                 scalar2=None, op0=ALU.mult),
                 reads=[('cv',)], writes=[('dv',)])
        P.op('dve', lambda e: e.memset(tmp[:, 0, :], 0.0), reads=[('dv',), ('ones',), ('invc',), ('lst',)], writes=[('pro',), ('tmp', 0)])

        try_emit_dma()

        sq_ctr = [0]
        tb_ctr = [0]

        def hreg(k):
            return [('h', k, s) for s in range(3)]

        def norm(gcol0, mode):
            banks = [next_bank() for _ in SUBS]
            for k in range(KC):
                i = k % 2
                if i == 0:
                    P.op('act', lambda e, k=k, i=i: e.activation(out=sq[i][:, :], in_=hT[:, k, :], func=AF.Square),
                         reads=hreg(k), writes=[('sq', i)])
                else:
                    P.op('dve', lambda e, k=k, i=i: e.tensor_tensor(out=sq[i][:, :], in0=hT[:, k, :], in1=hT[:, k, :], op=ALU.mult),
                         reads=hreg(k), writes=[('sq', i)])

                def mm(e, k=k, i=i):
                    last = None
                    for s, (o, w) in enumerate(SUBS):
                        last = e.matmul(ps[banks[s]][:, 0:w], ones[:, :], sq[i][:, o:o + w],
                                        start=(k == 0), stop=(k == KC - 1))
                    return last

                P.op('pe', mm, reads=[('sq', i), ('ones',)], writes=[('ps', b) for b in banks])
            for s, (o, w) in enumerate(SUBS):
                bk = banks[s]
                P.op('act', lambda e, bk=bk, w=w: e.activation(out=ps[bk][:, 0:w], in_=ps[bk][:, 0:w],
                                                               func=AF.Ln, scale=1.0 / D, bias=EPS),
                     reads=[('ps', bk)], writes=[('ps', bk)])
                P.op('act', lambda e, bk=bk, o=o, w=w: e.activation(out=rstd[:, o:o + w], in_=ps[bk][:, 0:w],
                                                                    func=AF.Exp, scale=-0.5),
                     reads=[('ps', bk)], writes=[('rstd', s)])
                free_bank(bk)
            for k in range(KC):
                if mode == 'xn':
                    P.op('dve', lambda e, k=k: e.scalar_tensor_tensor(
                        out=xn[:, k, HIST:HIST + TT], in0=hT[:, k, :], scalar=cvt[:, gcol0 + k:gcol0 + k + 1],
                        in1=rstd[:, :], op0=ALU.mult, op1=ALU.mult),
                        reads=hreg(k) + [('rstd', s) for s in range(3)] + [('cv',)],
                        writes=[('xn', k, s) for s in range(3)])
                else:
                    P.op('dve', lambda e, k=k: e.scalar_tensor_tensor(
                        out=hT[:, k, :], in0=hT[:, k, :], scalar=cvt[:, gcol0 + k:gcol0 + k + 1],
                        in1=rstd[:, :], op0=ALU.mult, op1=ALU.mult),
                        reads=hreg(k) + [('rstd', s) for s in range(3)] + [('cv',)],
                        writes=hreg(k))

        def xn_regs():
            return [('xn', k, s) for k in range(KC) for s in range(3)]

        def ffn(l, which):
            gcol0 = l * CV_L + (0 if which == 1 else 32)
            norm(gcol0, 'xn')
            for q in range(Q):
                for fi in range(FQ):
                    f = q * FQ + fi
                    bks = []
                    if f == 0:
                        sg, su = acquire('g', f), acquire('u', f)
                        bkg = [next_bank() for _ in SUBS]
                        bku = [next_bank() for _ in SUBS]
                        for k in range(KC):
                            def mmk(e, sg=sg, su=su, bkg=bkg, bku=bku, k=k):
                                last = None
                                for slot, banks in ((sg, bkg), (su, bku)):
                                    for s, (o, w) in enumerate(SUBS):
                                        last = e.matmul(ps[banks[s]][:, 0:w], wr[slot][:, k * 128:(k + 1) * 128],
                                                        xn[:, k, HIST + o:HIST + o + w], start=(k == 0), stop=(k == KC - 1))
                                return last
                            P.op('pe', mmk, reads=[('ws', sg), ('ws', su)] + [('xn', k, s) for s in range(3)],
                                 writes=[('ps', b) for b in bkg + bku])
                        release(sg)
                        release(su)
                        bks = [bkg, bku]
                    else:
                        for tag in ('g', 'u'):
                            slot = acquire(tag, f)
                            banks = [next_bank() for _ in SUBS]

                            def mm(e, slot=slot, banks=banks):
                                last = None
                                for k in range(KC):
                                    for s, (o, w) in enumerate(SUBS):
                                        last = e.matmul(ps[banks[s]][:, 0:w], wr[slot][:, k * 128:(k + 1) * 128],
                                                        xn[:, k, HIST + o:HIST + o + w], start=(k == 0), stop=(k == KC - 1))
                                return last

                            P.op('pe', mm, reads=[('ws', slot)] + xn_regs(), writes=[('ps', b) for b in banks])
                            release(slot)
                            bks.append(banks)
                    for s, (o, w) in enumerate(SUBS):
                        ti = tb_ctr[0] % NTB
                        tb_ctr[0] += 1
                        bg, bu = bks[0][s], bks[1][s]
                        P.op('act', lambda e, ti=ti, bg=bg, w=w: e.activation(out=Tb[ti][:, 0:w], in_=ps[bg][:, 0:w], func=AF.Silu),
                             reads=[('ps', bg)], writes=[('T', ti)])
                        P.op('dve', lambda e, ti=ti, bu=bu, fi=fi, o=o, w=w: e.tensor_tensor(
                            out=act[:, fi, o:o + w], in0=Tb[ti][:, 0:w], in1=ps[bu][:, 0:w], op=ALU.mult),
                            reads=[('T', ti), ('ps', bu)], writes=[('act', fi, s)])
                        free_bank(bg)
                        free_bank(bu)
                for d in range(KC):
                    slot = acquire('o', q)
                    banks = [next_bank() for _ in SUBS]

                    def mm(e, slot=slot, banks=banks):
                        last = None
                        for kk in range(FQ):
                            for s, (o, w) in enumerate(SUBS):
                                last = e.matmul(ps[banks[s]][:, 0:w], wr[slot][:, kk * 128:(kk + 1) * 128],
                                                act[:, kk, o:o + w], start=(kk == 0), stop=(kk == FQ - 1))
                        return last

                    P.op('pe', mm, reads=[('ws', slot)] + [('act', kk, s) for kk in range(FQ) for s in range(3)],
                         writes=[('ps', b) for b in banks])
                    release(slot)
                    for s, (o, w) in enumerate(SUBS):
                        b = banks[s]
                        P.op('dve', lambda e, b=b, d=d, o=o, w=w: e.scalar_tensor_tensor(
                            out=hT[:, d, o:o + w], in0=ps[b][:, 0:w], scalar=0.5, in1=hT[:, d, o:o + w],
                            op0=ALU.mult, op1=ALU.add),
                            reads=[('ps', b), ('h', d, s)], writes=[('h', d, s)])
                        free_bank(b)

        def mixer(l, st):
            c0 = l * CV_L
            d0 = l * DV_L
            if st == 0:
                P.op('dve', lambda e: e.memset(xn[:, :, 0:HIST], 0.0), writes=[('xnh',)])
            else:
                P.op('act', lambda e: e.activation(out=xn[:, :, 0:HIST], in_=xnh[:, l, :, :], func=AF.Identity),
                     reads=[('xnhist', l)], writes=[('xnh',)])
            norm(c0 + 16, 'xn')
            P.op('act', lambda e: e.activation(out=xnh[:, l, :, :], in_=xn[:, :, TT:TT + HIST], func=AF.Identity),
                 reads=[('xn', k, 2) for k in range(KC)], writes=[('xnhist', l)])

            def xin_regs(s, hist):
                r = [('xn', k, s) for k in range(KC)]
                if hist:
                    if s == 0:
                        r.append(('xnh',))
                    else:
                        r += [('xn', k, s - 1) for k in range(KC)]
                return r

            def proj(slot, bank, s, nh, split_k=False):
                o, w = SUBS[s]

                def mm(e, ks=range(KC)):
                    last = None
                    for k in ks:
                        last = e.matmul(ps[bank][:, 0:nh + w], wr[slot][:, k * 128:(k + 1) * 128],
                                        xn[:, k, HIST + o - nh:HIST + o + w], start=(k == 0), stop=(k == KC - 1))
                    return last

                if split_k:
                    for k in range(KC):
                        r = [('ws', slot), ('xn', k, s)]
                        if nh > 0:
                            r.append(('xnh',) if s == 0 else ('xn', k, s - 1))
                        P.op('pe', lambda e, k=k: mm(e, [k]), reads=r, writes=[('ps', bank)])
                else:
                    P.op('pe', mm, reads=[('ws', slot)] + xin_regs(s, nh > 0), writes=[('ps', bank)])

            units = [('lru', j, s) for j in range(NHEAD) for s in range(3)] + \
                    [('pool', g, s) for g in range(4) for s in range(3)]
            nu = len(units)
            blk = {}
            ctx = [dict() for _ in units]
            zgb = {}

            def R_(us, i):
                return ('U', us, i)

            def A_pe(u):
                kind, j, s = units[u]
                if kind == 'lru':
                    if s == 0:
                        blk['zx', j] = acquire('zx', j)
                    bzx = next_bank()
                    ctx[u]['bzx'] = bzx
                    if u == 0:
                        zxs = blk['zx', j]
                        zs = acquire('zg', 0)
                        zgb[0] = [next_bank() for _ in range(3)]
                        zb = zgb[0]
                        for k in range(KC):
                            def mmk(e, k=k, zxs=zxs, zs=zs, zb=zb, bzx=bzx):
                                o0, w0 = SUBS[0]
                                last = e.matmul(ps[bzx][:, 0:3 + w0], wr[zxs][:, k * 128:(k + 1) * 128],
                                                xn[:, k, HIST + o0 - 3:HIST + o0 + w0], start=(k == 0), stop=(k == KC - 1))
                                for s2, (o2, w2) in enumerate(SUBS):
                                    last = e.matmul(ps[zb[s2]][:, 0:w2], wr[zs][:, k * 128:(k + 1) * 128],
                                                    xn[:, k, HIST + o2:HIST + o2 + w2], start=(k == 0), stop=(k == KC - 1))
                                return last
                            P.op('pe', mmk, reads=[('ws', zxs), ('ws', zs), ('xnh',)] + [('xn', k, s2) for s2 in range(3)],
                                 writes=[('ps', bzx)] + [('ps', b) for b in zb])
                        release(zs)
                    else:
                        proj(blk['zx', j], bzx, s, 3)
                    if j + 1 < NHEAD:
                        if s == 0:
                            blk['zg', j + 1] = acquire('zg', j + 1)
                            zgb[j + 1] = [None] * 3
                        zgb[j + 1][s] = next_bank()
                        proj(blk['zg', j + 1], zgb[j + 1][s], s, 0)
                        if s == 2:
                            release(blk['zg', j + 1])
                    if s == 2:
                        release(blk['zx', j])
                else:
                    g = j
                    if s == 0:
                        blk['zp', 2 * g] = acquire('zp', 2 * g)
                        blk['zp', 2 * g + 1] = acquire('zp', 2 * g + 1)
                    bz = [next_bank(), next_bank()]
                    ctx[u]['bz'] = bz
                    for cc in range(2):
                        proj(blk['zp', 2 * g + cc], bz[cc], s, 15)
                    if s == 2:
                        release(blk['zp', 2 * g])
                        release(blk['zp', 2 * g + 1])

            def A_evac(u):
                kind, j, s = units[u]
                us = u % NSET
                o, w = SUBS[s]
                if kind == 'lru':
                    bzx = ctx[u]['bzx']
                    ZX, XC = U[:, us, 0, :], U[:, us, 1, :]
                    cw3 = cvt[:, c0 + 48 + 3 * 8 + j:c0 + 48 + 3 * 8 + j + 1]
                    cb = cvt[:, c0 + 80 + j:c0 + 81 + j]
                    P.op('dve', lambda e: e.tensor_copy(out=ZX[:, 0:3 + w], in_=ps[bzx][:, 0:3 + w]),
                         reads=[('ps', bzx)], writes=[R_(us, 0)])
                    free_bank(bzx)
                    P.op('act', lambda e: e.activation(out=XC[:, 0:w], in_=ZX[:, 3:3 + w], func=AF.Identity, scale=cw3, bias=cb),
                         reads=[R_(us, 0), ('cv',)], writes=[R_(us, 1)])
                    gh = None
                    if u == 0:
                        gh = 0
                    elif s == 2 and j + 1 < NHEAD:
                        gh = j + 1
                    if gh is not None:
                        bzg = zgb[gh]
                        for s2 in range(3):
                            w2 = SUBS[s2][1]
                            gi = (3 * gh + s2) % NTB
                            P.op('act', lambda e, s2=s2, w2=w2, gi=gi, bzg=bzg: e.activation(
                                out=Tb[gi][:, 0:w2], in_=ps[bzg[s2]][:, 0:w2], func=AF.Gelu_apprx_tanh),
                                reads=[('ps', bzg[s2])], writes=[('T', gi)])
                            free_bank(bzg[s2])
                else:
                    bz = ctx[u]['bz']
                    for cc in range(2):
                        P.op('act', lambda e, cc=cc: e.activation(out=U[:, us, cc, 0:15 + w], in_=ps[bz[cc]][:, 0:15 + w], func=AF.Identity),
                             reads=[('ps', bz[cc])], writes=[R_(us, cc)])
                        free_bank(bz[cc])

            def B1_dve(u):
                kind, j, s = units[u]
                us = u % NSET
                o, w = SUBS[s]
                if kind == 'lru':
                    ZX, XC = U[:, us, 0, :], U[:, us, 1, :]
                    XCB = UB[:, us, 0, :]
                    cw = [cvt[:, c0 + 48 + kk * 8 + j:c0 + 48 + kk * 8 + j + 1] for kk in range(4)]
                    for kk in (2, 1, 0):
                        P.op('dve', lambda e, kk=kk: e.scalar_tensor_tensor(out=XC[:, 0:w], in0=ZX[:, kk:kk + w], scalar=cw[kk],
                                                                            in1=XC[:, 0:w], op0=ALU.mult, op1=ALU.add),
                             reads=[R_(us, 0), R_(us, 1), ('cv',)], writes=[R_(us, 1)])
                    P.op('dve', lambda e: e.tensor_tensor(out=XCB[:, 0:w], in0=XC[:, 0:w], in1=XC[:, 0:w], op=ALU.max),
                         reads=[R_(us, 1)], writes=[('UB', us, 0)])
                else:
                    g = j
                    win = POOL_WIN[g]
                    nlev = g + 1
                    n = 15 + w
                    for cc in range(2):
                        Uc = U[:, us, cc, :]
                        A = U[:, us, 2 + 2 * cc, :]
                        B = U[:, us, 3 + 2 * cc, :]
                        rA, rB = R_(us, 2 + 2 * cc), R_(us, 3 + 2 * cc)
                        src_, rsrc = Uc, R_(us, cc)
                        for lev in range(nlev):
                            sh = 1 << lev
                            lo = (1 << (lev + 1)) - 1
                            dst, rdst = (A, rA) if lev % 2 == 0 else (B, rB)
                            P.op('dve', lambda e, src_=src_, dst=dst, sh=sh, lo=lo: e.tensor_tensor(
                                out=dst[:, lo:n], in0=src_[:, lo:n], in1=src_[:, lo - sh:n - sh], op=ALU.add),
                                reads=[rsrc], writes=[rdst])
                            src_, rsrc = dst, rdst
                        Dd = UB[:, us, cc, :]
                        P.op('dve', lambda e, src_=src_, Uc=Uc, Dd=Dd: e.scalar_tensor_tensor(
                            out=Dd[:, 0:w], in0=src_[:, 15:15 + w], scalar=1.0 / win, in1=Uc[:, 15:15 + w],
                            op0=ALU.mult, op1=ALU.subtract),
                            reads=[rsrc, R_(us, cc)], writes=[('UB', us, cc)])
                        if st == 0 and s == 0:
                            m = win - 1
                            other, rother = (B, rB) if src_ is A else (A, rA)
                            P.op('dve', lambda e, src_=src_, other=other, m=m: e.tensor_tensor(
                                out=other[:, 0:m], in0=src_[:, 15:15 + m], in1=invc[:, g, 0:m], op=ALU.mult),
                                reads=[rsrc, ('invc',)], writes=[rother])
                            P.op('dve', lambda e, other=other, Uc=Uc, Dd=Dd, m=m: e.tensor_tensor(
                                out=Dd[:, 0:m], in0=other[:, 0:m], in1=Uc[:, 15:15 + m], op=ALU.subtract),
                                reads=[rother, R_(us, cc), ('UB', us, cc)], writes=[('UB', us, cc)])

            def B1_pe(u):
                kind, j, s = units[u]
                us = u % NSET
                o, w = SUBS[s]
                if kind == 'lru':
                    XCB = UB[:, us, 0, :]
                    if s == 0:
                        blk['gt', j] = acquire('gt', j)
                    gs = blk['gt', j]
                    br, bi = next_bank(), next_bank()
                    ctx[u]['br'], ctx[u]['bi'] = br, bi

                    def mm(e):
                        e.matmul(ps[br][:, 0:w], wr[gs][:, 0:128], XCB[:, 0:w], start=True, stop=True)
                        return e.matmul(ps[bi][:, 0:w], wr[gs][:, 128:256], XCB[:, 0:w], start=True, stop=True)

                    P.op('pe', mm, reads=[('ws', gs), ('UB', us, 0)], writes=[('ps', br), ('ps', bi)])
                    if s == 2:
                        release(gs)
                else:
                    g = j
                    if s == 0:
                        blk['pw', g] = acquire('pw', g)
                    pslot = blk['pw', g]
                    by = [next_bank(), next_bank()]
                    ctx[u]['by'] = by

                    def mm(e):
                        last = None
                        for jc in range(2):
                            for kc in range(2):
                                last = e.matmul(ps[by[jc]][:, 0:w], wr[pslot][:, kc * 256 + jc * 128:kc * 256 + jc * 128 + 128],
                                                UB[:, us, kc, 0:w], start=(kc == 0), stop=(kc == 1))
                        return last

                    P.op('pe', mm, reads=[('ws', pslot), ('UB', us, 0), ('UB', us, 1)], writes=[('ps', by[0]), ('ps', by[1])])
                    if s == 2:
                        release(pslot)

            def B1_act(u):
                kind, j, s = units[u]
                us = u % NSET
                o, w = SUBS[s]
                if kind == 'lru':
                    Rr, Ii, Mm = U[:, us, 2, :], U[:, us, 3, :], U[:, us, 4, :]
                    br, bi = ctx[u]['br'], ctx[u]['bi']
                    hba = dvt[:, d0 + 24 + j:d0 + 25 + j]
                    hbx = dvt[:, d0 + 32 + j:d0 + 33 + j]
                    cc1 = dvt[:, d0 + j:d0 + j + 1]
                    cch = dvt[:, d0 + 8 + j:d0 + 9 + j]
                    P.op('act', lambda e: e.activation(out=Rr[:, 0:w], in_=ps[br][:, 0:w], func=AF.Tanh, scale=0.5, bias=hba),
                         reads=[('ps', br), ('dv',)], writes=[R_(us, 2)])
                    P.op('act', lambda e: e.activation(out=Ii[:, 0:w], in_=ps[bi][:, 0:w], func=AF.Tanh, scale=0.5, bias=hbx),
                         reads=[('ps', bi), ('dv',)], writes=[R_(us, 3)])
                    free_bank(br)
                    free_bank(bi)
                    P.op('act', lambda e: e.activation(out=Mm[:, 0:w], in_=Rr[:, 0:w], func=AF.Exp, scale=cc1, bias=cc1),
                         reads=[R_(us, 2), ('dv',)], writes=[R_(us, 4)])
                    P.op('act', lambda e: e.activation(out=Rr[:, 0:w], in_=Rr[:, 0:w], func=AF.Exp, scale=cch, bias=cch),
                         reads=[R_(us, 2), ('dv',)], writes=[R_(us, 2)])
                    P.op('act', lambda e: e.activation(out=Mm[:, 0:w], in_=Mm[:, 0:w], func=AF.Sqrt, scale=-0.25, bias=0.25),
                         reads=[R_(us, 4)], writes=[R_(us, 4)])
                else:
                    g = j
                    by = ctx[u]['by']
                    for jc in range(2):
                        c = 2 * g + jc
                        sc = cvt[:, c0 + 120 + c:c0 + 121 + c]
                        bsc = dvt[:, d0 + 16 + c:d0 + 17 + c]
                        P.op('act', lambda e, jc=jc, c=c, sc=sc, bsc=bsc: e.activation(
                            out=act[:, 8 + c, o:o + w], in_=ps[by[jc]][:, 0:w], func=AF.Identity, scale=sc, bias=bsc),
                            reads=[('ps', by[jc]), ('cv',), ('dv',)], writes=[('act', 8 + c, s)])
                        free_bank(by[jc])

            def B2_dve(u):
                kind, j, s = units[u]
                if kind != 'lru':
                    return
                us = u % NSET
                o, w = SUBS[s]
                XC, Rr, Ii, Mm, HS = [U[:, us, i, :] for i in (1, 2, 3, 4, 5)]
                P.op('dve', lambda e: e.scalar_tensor_tensor(out=Mm[:, 0:w], in0=Ii[:, 0:w], scalar=1.0, in1=Mm[:, 0:w],
                                                             op0=ALU.add, op1=ALU.mult),
                     reads=[R_(us, 3), R_(us, 4)], writes=[R_(us, 4)])
                P.op('dve', lambda e: e.tensor_tensor(out=Mm[:, 0:w], in0=Mm[:, 0:w], in1=XC[:, 0:w], op=ALU.mult),
                     reads=[R_(us, 4), R_(us, 1)], writes=[R_(us, 4)])
                if s == 0:
                    init = lst[:, l, j:j + 1]
                    rinit = ('lst',)
                else:
                    pus = (u - 1) % NSET
                    pw_ = SUBS[s - 1][1]
                    init = U[:, pus, 5, pw_ - 1:pw_]
                    rinit = R_(pus, 5)
                P.op('dve', lambda e: e.tensor_tensor_scan(out=HS[:, 0:w], data0=Rr[:, 0:w], data1=Mm[:, 0:w], initial=init,
                                                           op0=ALU.mult, op1=ALU.add),
                     reads=[R_(us, 2), R_(us, 4), rinit], writes=[R_(us, 5)])
                if s == 2:
                    P.op('act', lambda e: e.activation(out=lst[:, l, j:j + 1], in_=HS[:, w - 1:w], func=AF.Identity),
                         reads=[R_(us, 5)], writes=[('lst',)])
                gi = u % NTB
                P.op('dve', lambda e: e.tensor_tensor(out=act[:, j, o:o + w], in0=HS[:, 0:w], in1=Tb[gi][:, 0:w], op=ALU.mult),
                     reads=[R_(us, 5), ('T', gi)], writes=[('act', j, s)])

            for t in range(nu + 2):
                if t < nu:
                    A_pe(t)
                if 0 <= t - 1 < nu:
                    B1_dve(t - 1)
                    B1_pe(t - 1)
                if t < nu:
                    A_evac(t)
                if 0 <= t - 1 < nu:
                    B1_act(t - 1)
                if 0 <= t - 2 < nu:
                    B2_dve(t - 2)

            for d in range(KC):
                slot = acquire('wo', d)
                banks = [next_bank() for _ in SUBS]

                def mm(e, slot=slot, banks=banks):
                    last = None
                    for k in range(KC):
                        for s, (o, w) in enumerate(SUBS):
                            last = e.matmul(ps[banks[s]][:, 0:w], wr[slot][:, k * 128:(k + 1) * 128],
                                            act[:, k, o:o + w], start=(k == 0), stop=(k == KC - 1))
                    return last

                P.op('pe', mm, reads=[('ws', slot)] + [('act', k, s) for k in range(KC) for s in range(3)],
                     writes=[('ps', b) for b in banks])
                release(slot)
                for s, (o, w) in enumerate(SUBS):
                    b = banks[s]
                    P.op('dve', lambda e, b=b, d=d, o=o, w=w: e.tensor_tensor(
                        out=hT[:, d, o:o + w], in0=ps[b][:, 0:w], in1=hT[:, d, o:o + w], op=ALU.add),
                        reads=[('ps', b), ('h', d, s)], writes=[('h', d, s)])
                    free_bank(b)

        for st in range(nst):
            ld_eng = 'sp' if st == 0 else 'act'
            for k in range(KC):
                P.op(ld_eng, lambda e, k=k, st=st: e.dma_start(out=hT[:, k, :], in_=xin[:, k, st * TT:(st + 1) * TT]),
                     writes=hreg(k), sem=f'ld{k}')
            for (l, kind) in phases:
                if kind == 'mix':
                    mixer(l, st)
                else:
                    ffn(l, 1 if kind == 'ffn1' else 2)
            norm(NLAYER * CV_L, 'final')
            c_lo = NMETA if st == 0 else 0
            t_lo = st * TT + c_lo - NMETA
            ncol = TT - c_lo
            for k in range(KC):
                P.op('sp', lambda e, k=k, c_lo=c_lo, t_lo=t_lo, ncol=ncol: e.dma_start(
                    out=y[:, k, t_lo:t_lo + ncol], in_=hT[:, k, c_lo:c_lo + ncol]),
                    reads=hreg(k), sem=f'st{k}')
        for k in range(KC):
            P.final_waits['sp'].append((f'st{k}', P.cnt[f'st{k}']))
        assert ring['acquired'] == total_blocks and ring['emitted'] == total_blocks

        sems = {}
        for key in P.cnt:
            sems[key] = es.enter_context(nc.semaphore(f"s_{key}"))
        block = es.enter_context(nc.Block())
        P.emit(block, sems)
    return nc


def _kmaj(w):
    K = w.shape[0] // 128
    return np.ascontiguousarray(w.reshape(K, 128, w.shape[1]).transpose(1, 0, 2)).reshape(128, K * w.shape[1])


def build_wstream(inp, phases):
    layout, WTOT = stream_layout(phases)
    wst = np.empty((128, WTOT), dtype=np.float32)
    for (l, kind, tag, a, b, off, n) in layout:
        if kind in ('ffn1', 'ffn2'):
            w_in = inp['ffn1_w_in' if kind == 'ffn1' else 'ffn2_w_in'][l]
            w_out = inp['ffn1_w_out' if kind == 'ffn1' else 'ffn2_w_out'][l]
            if tag == 'g':
                blkv = _kmaj(w_in[:, a * 128:(a + 1) * 128])
            elif tag == 'u':
                blkv = _kmaj(w_in[:, DFF + a * 128:DFF + (a + 1) * 128])
            else:
                q, d = a, b
                blkv = _kmaj(w_out[q * FQ * 128:(q + 1) * FQ * 128, d * 128:(d + 1) * 128])
        else:
            w_in = inp['w_in'][l]
            if tag == 'zx':
                blkv = _kmaj(w_in[:, a * 128:(a + 1) * 128])
            elif tag == 'zg':
                blkv = _kmaj(w_in[:, 1024 + a * 128:1024 + (a + 1) * 128])
            elif tag == 'zp':
                blkv = _kmaj(w_in[:, 2048 + a * 128:2048 + (a + 1) * 128])
            elif tag == 'gt':
                blkv = np.concatenate([inp['lru_wa'][l, a], inp['lru_wx'][l, a]], axis=1)
            elif tag == 'pw':
                blkv = _kmaj(inp['pool_w'][l, a])
            else:
                blkv = _kmaj(inp['w_out'][l][:, a * 128:(a + 1) * 128])
        assert blkv.shape == (128, n), (blkv.shape, n, tag)
        wst[:, off:off + n] = blkv
    return wst


def build_cvec(inp):
    cvv = np.zeros((128, NCV), dtype=np.float32)

    def col(v):
        return np.asarray(v, dtype=np.float32).reshape(-1, 128).T

    for l in range(NLAYER):
        c0 = l * CV_L
        cvv[:, c0 + 0:c0 + 16] = col(inp['ffn1_norm'][l])
        cvv[:, c0 + 16:c0 + 32] = col(inp['mix_norm'][l])
        cvv[:, c0 + 32:c0 + 48] = col(inp['ffn2_norm'][l])
        for kk in range(4):
            cvv[:, c0 + 48 + kk * 8:c0 + 56 + kk * 8] = col(inp['conv_w'][l, kk])
        cvv[:, c0 + 80:c0 + 88] = col(inp['conv_b'][l])
        cvv[:, c0 + 88:c0 + 96] = col(inp['lru_ba'][l])
        cvv[:, c0 + 96:c0 + 104] = col(inp['lru_bx'][l])
        cvv[:, c0 + 104:c0 + 112] = col(inp['lru_a_param'][l])
        cvv[:, c0 + 112:c0 + 120] = col(inp['pool_b'][l])
        cvv[:, c0 + 120:c0 + 128] = col(inp['pool_scale'][l])
    cvv[:, NLAYER * CV_L:NLAYER * CV_L + KC] = col(inp['final_norm'])
    return cvv


def build_xin(x_b, meta):
    hfull = np.concatenate([meta, x_b], axis=0)
    return np.ascontiguousarray(hfull.T.reshape(KC, 128, TTOT).transpose(1, 0, 2))


_CACHE = {}


def kernel(**inputs):
    inp = {k: np.asarray(v) for k, v in inputs.items()}
    x = inp['x'].astype(np.float32, copy=False)
    B = x.shape[0]
    phases = phases_all()
    if 'nc' not in _CACHE:
        _CACHE['nc'] = build_program(NST, phases, NSLOT)
    nc = _CACHE['nc']
    wst = build_wstream(inp, phases)
    cvv = build_cvec(inp)
    meta = inp['meta_tokens'].astype(np.float32, copy=False)
    in_maps = [{"xin": build_xin(x[b], meta), "wst": wst, "cv": cvv} for b in range(B)]
    res = run_bass_kernel_spmd(nc, in_maps, core_ids=list(range(B)))
    out = np.empty((B, SEQ, D), dtype=np.float32)
    for b in range(B):
        yb = np.asarray(res.results[b]["y"])
        out[b] = yb.transpose(2, 1, 0).reshape(SEQ, D)
    return out
```
